# Optimizing a Trainium2 kernel written in Bass

```python
import math
import jax, jax.numpy as jnp
from jax import lax
import numpy as np

D_MODEL = 1024
BATCH = 2
SEQ = 8192
DEPTH = 1

RWKV_HEAD_DIM = 64
RWKV_HEADS = D_MODEL // (2 * RWKV_HEAD_DIM)
RWKV_WIDTH = RWKV_HEADS * RWKV_HEAD_DIM
DECAY_LORA = 64
ICLR_LORA = 64
GATE_LORA = 128
GN_EPS = 64e-5
N_DIR = 2

DA_HEAD_DIM = 64
DA_HEADS = D_MODEL // (4 * DA_HEAD_DIM)
DA_V_DIM = 2 * DA_HEAD_DIM
DA_QK_WIDTH = DA_HEADS * 2 * DA_HEAD_DIM
DA_V_WIDTH = DA_HEADS * DA_V_DIM
SUBLN_EPS = 1e-5
Q_BLOCK = 128

ROPE_THETA = 500000.0
ROPE_DIM = DA_HEAD_DIM // 4

D_FF = 4 * D_MODEL
PLE_DIM = 256
N_BRANCH = 2
RMS_EPS = 1e-6

RWKV_COLS = 3 * RWKV_WIDTH + N_DIR * DECAY_LORA + N_DIR * ICLR_LORA + GATE_LORA
DA_COLS = 2 * DA_QK_WIDTH + DA_V_WIDTH
GATE_COLS = N_BRANCH * D_MODEL
IN_COLS = RWKV_COLS + DA_COLS + GATE_COLS

kernel_name = "hybrid_rwkv7_diffattn_gated_encoder"


def rmsnorm(x, g, eps=RMS_EPS):
    xf = x.astype(jnp.float32)
    y = xf * lax.rsqrt(jnp.mean(xf * xf, axis=-1, keepdims=True) + eps)
    return (y * g.astype(jnp.float32)).astype(x.dtype)


def centred_shift(u, mu_prev, mu_next):
    up = jnp.pad(u, ((0, 0), (1, 1), (0, 0)))
    return u + mu_prev * (up[:, :-2] - u) + mu_next * (up[:, 2:] - u)


def rwkv7_scan(r, w, k, v, kk, a, reverse):
    B, S, H, N = r.shape

    def step(state, inp):
        r_t, w_t, k_t, v_t, kk_t, a_t = inp
        sa = jnp.einsum('bhij,bhj->bhi', state, -kk_t)
        state = (state * w_t[:, :, None, :]
                 + sa[..., None] * (kk_t * a_t)[:, :, None, :]
                 + v_t[..., None] * k_t[:, :, None, :])
        y_t = jnp.einsum('bhij,bhj->bhi', state, r_t)
        return state, y_t

    xs = tuple(jnp.moveaxis(t, 1, 0) for t in (r, w, k, v, kk, a))
    s0 = jnp.zeros((B, H, N, N), jnp.float32)
    _, ys = lax.scan(step, s0, xs, reverse=reverse)
    return jnp.moveaxis(ys, 0, 1)


def rwkv7_branch(u, mu_prev, mu_next, w0, w2, a0, a2, g2, k_k, k_a, r_k, ln_w, ln_b, w_o):
    f32 = jnp.float32
    B, S, _ = u.shape
    H, N, W = RWKV_HEADS, RWKV_HEAD_DIM, RWKV_WIDTH
    u = centred_shift(u, mu_prev, mu_next).astype(f32)
    o1, o2, o3 = W, 2 * W, 3 * W
    o4 = o3 + N_DIR * DECAY_LORA
    o5 = o4 + N_DIR * ICLR_LORA
    r = u[..., :o1]
    k = u[..., o1:o2]
    v = u[..., o2:o3]
    wd = u[..., o3:o4].reshape(B, S, N_DIR, DECAY_LORA)
    ad = u[..., o4:o5].reshape(B, S, N_DIR, ICLR_LORA)
    gd = u[..., o5:]

    w_log = -jax.nn.softplus(-(w0.astype(f32) + jnp.einsum('bsdr,drc->bsdc', jnp.tanh(wd), w2.astype(f32)))) - 0.5
    decay = jnp.exp(-jnp.exp(w_log)).reshape(B, S, N_DIR, H, N)
    a = jax.nn.sigmoid(a0.astype(f32) + jnp.einsum('bsdr,drc->bsdc', ad, a2.astype(f32))).reshape(B, S, N_DIR, H, N)
    g = jax.nn.sigmoid(gd) @ g2.astype(f32)

    kk = (k * k_k.astype(f32)).reshape(B, S, H, N)
    kk = kk / jnp.maximum(jnp.sqrt(jnp.sum(kk * kk, axis=-1, keepdims=True)), 1e-12)
    r_h = r.reshape(B, S, H, N)
    v_h = v.reshape(B, S, H, N)
    k_dir = k.reshape(B, S, 1, H, N) * (1.0 + (a - 1.0) * k_a.astype(f32).reshape(H, N))

    y = (rwkv7_scan(r_h, decay[:, :, 0], k_dir[:, :, 0], v_h, kk, a[:, :, 0], False)
         + rwkv7_scan(r_h, decay[:, :, 1], k_dir[:, :, 1], v_h, kk, a[:, :, 1], True))

    mean = jnp.mean(y, axis=-1, keepdims=True)
    var = jnp.mean(jnp.square(y - mean), axis=-1, keepdims=True)
    y = ((y - mean) * lax.rsqrt(var + GN_EPS)).reshape(B, S, W)
    y = y * ln_w.astype(f32) + ln_b.astype(f32)
    bonus = jnp.sum(jnp.sum(r_h[:, :, None] * k_dir * r_k.astype(f32), axis=-1, keepdims=True), axis=2) * v_h
    y = (y + bonus.reshape(B, S, W)) * g
    return (y @ w_o.astype(f32)).astype(w_o.dtype)


def rope_tables(S):
    pos = jnp.arange(S, dtype=jnp.float32)
    inv_freq = ROPE_THETA ** (-jnp.arange(0, ROPE_DIM, 2, dtype=jnp.float32) / ROPE_DIM)
    ang = pos[:, None] * inv_freq[None, :]
    return jnp.cos(ang), jnp.sin(ang)


def partial_rope(x, cos, sin):
    half = ROPE_DIM // 2
    c = cos[:, None, None, :]
    s = sin[:, None, None, :]
    x1 = x[..., :half].astype(jnp.float32)
    x2 = x[..., half:ROPE_DIM].astype(jnp.float32)
    rot = jnp.concatenate([x1 * c - x2 * s, x2 * c + x1 * s], axis=-1)
    return jnp.concatenate([rot.astype(x.dtype), x[..., ROPE_DIM:]], axis=-1)


def diff_attn_branch(u, lq1, lk1, lq2, lk2, subln_w, w_o, lambda_init):
    f32 = jnp.float32
    B, S, _ = u.shape
    H, DH, DV = DA_HEADS, DA_HEAD_DIM, DA_V_DIM
    q = u[..., :DA_QK_WIDTH].reshape(B, S, H, 2, DH)
    k = u[..., DA_QK_WIDTH:2 * DA_QK_WIDTH].reshape(B, S, H, 2, DH)
    v = u[..., 2 * DA_QK_WIDTH:].reshape(B, S, H, DV)
    cos, sin = rope_tables(S)
    q = partial_rope(q, cos, sin) * (DH ** -0.5)
    k = partial_rope(k, cos, sin)
    lam = (jnp.exp(jnp.sum(lq1.astype(f32) * lk1.astype(f32)))
           - jnp.exp(jnp.sum(lq2.astype(f32) * lk2.astype(f32))) + lambda_init)

    kt = k.transpose(0, 2, 3, 1, 4)
    vt = v.transpose(0, 2, 1, 3)
    n_blk = S // Q_BLOCK
    qb = q.reshape(B, n_blk, Q_BLOCK, H, 2, DH).transpose(1, 0, 3, 4, 2, 5)

    def block(q_blk):
        s = jnp.einsum('bhcqd,bhckd->bhcqk', q_blk, kt).astype(f32)
        pr = jax.nn.softmax(s, axis=-1)
        pd = pr[:, :, 0] - lam * pr[:, :, 1]
        return jnp.einsum('bhqk,bhkv->bhqv', pd.astype(vt.dtype), vt)

    o = lax.map(block, qb)
    o = o.transpose(1, 0, 3, 2, 4).reshape(B, S, H, DV).astype(f32)
    o = o * lax.rsqrt(jnp.mean(o * o, axis=-1, keepdims=True) + SUBLN_EPS)
    o = o * subln_w.astype(f32) * (1.0 - lambda_init)
    return o.reshape(B, S, DA_V_WIDTH).astype(w_o.dtype) @ w_o


def setup_inputs(seed: int = 0) -> dict:
    key = jax.random.key(seed)
    ks = iter(jax.random.split(key, 40))
    L, D, W = DEPTH, D_MODEL, RWKV_WIDTH
    f32 = jnp.float32

    def nrm(shape, scale):
        return jax.random.normal(next(ks), shape, f32) * scale

    def gain(shape):
        return 1.0 + 0.05 * jax.random.normal(next(ks), shape, f32)

    def unif(shape, lo, hi):
        return jax.random.uniform(next(ks), shape, f32, lo, hi)

    return {
        "x": nrm((BATCH, SEQ, D), 1.0),
        "p": nrm((DEPTH, BATCH, SEQ, PLE_DIM), 1.0),
        "norm_mix": gain((L, D)),
        "w_in": nrm((L, D, IN_COLS), D ** -0.5),
        "shift_mu_prev": unif((L, RWKV_COLS), 0.0, 0.5),
        "shift_mu_next": unif((L, RWKV_COLS), 0.0, 0.5),
        "rwkv_w0": unif((L, N_DIR, W), -6.0, 1.0),
        "rwkv_w2": nrm((L, N_DIR, DECAY_LORA, W), 0.1 * DECAY_LORA ** -0.5),
        "rwkv_a0": nrm((L, N_DIR, W), 0.5),
        "rwkv_a2": nrm((L, N_DIR, ICLR_LORA, W), 0.1 * ICLR_LORA ** -0.5),
        "rwkv_g2": nrm((L, GATE_LORA, W), GATE_LORA ** -0.5),
        "rwkv_k_k": 0.85 + 0.05 * jax.random.normal(next(ks), (L, W), f32),
        "rwkv_k_a": gain((L, W)),
        "rwkv_r_k": nrm((L, RWKV_HEADS, RWKV_HEAD_DIM), 0.1),
        "rwkv_ln_w": gain((L, W)),
        "rwkv_ln_b": nrm((L, W), 0.01),
        "rwkv_w_o": nrm((L, W, D), W ** -0.5),
        "da_lq1": nrm((L, DA_HEAD_DIM), 0.1),
        "da_lk1": nrm((L, DA_HEAD_DIM), 0.1),
        "da_lq2": nrm((L, DA_HEAD_DIM), 0.1),
        "da_lk2": nrm((L, DA_HEAD_DIM), 0.1),
        "da_subln_w": gain((L, DA_V_DIM)),
        "da_w_o": nrm((L, DA_V_WIDTH, D), DA_V_WIDTH ** -0.5),
        "w_out": nrm((L, D, D), D ** -0.5),
        "norm_ffn": gain((L, D)),
        "w_ff1": nrm((L, D, D_FF), D ** -0.5),
        "w_ff2": nrm((L, D_FF, D), D_FF ** -0.5),
        "norm_ple": gain((L, D)),
        "w_ple_gate": nrm((L, D, D), D ** -0.5),
        "w_ple_proj": nrm((L, PLE_DIM, D), PLE_DIM ** -0.5),
        "norm_final": gain((D,)),
    }


def reference(x, p, norm_mix, w_in, shift_mu_prev, shift_mu_next, rwkv_w0, rwkv_w2, rwkv_a0,
              rwkv_a2, rwkv_g2, rwkv_k_k, rwkv_k_a, rwkv_r_k, rwkv_ln_w, rwkv_ln_b, rwkv_w_o,
              da_lq1, da_lk1, da_lq2, da_lk2, da_subln_w, da_w_o, w_out, norm_ffn, w_ff1, w_ff2,
              norm_ple, w_ple_gate, w_ple_proj, norm_final):
    B, S, D = x.shape
    for i in range(DEPTH):
        lambda_init = 0.8 - 0.6 * math.exp(-0.3 * i)
        h = rmsnorm(x, norm_mix[i])
        u = h @ w_in[i]
        u_rwkv = u[..., :RWKV_COLS]
        u_da = u[..., RWKV_COLS:RWKV_COLS + DA_COLS]
        gates = jax.nn.sigmoid(u[..., RWKV_COLS + DA_COLS:].astype(jnp.float32)).reshape(B, S, N_BRANCH, D)
        y_a = rwkv7_branch(u_rwkv, shift_mu_prev[i], shift_mu_next[i], rwkv_w0[i], rwkv_w2[i],
                           rwkv_a0[i], rwkv_a2[i], rwkv_g2[i], rwkv_k_k[i], rwkv_k_a[i], rwkv_r_k[i],
                           rwkv_ln_w[i], rwkv_ln_b[i], rwkv_w_o[i])
        y_b = diff_attn_branch(u_da, da_lq1[i], da_lk1[i], da_lq2[i], da_lk2[i], da_subln_w[i],
                               da_w_o[i], lambda_init)
        merged = (gates[:, :, 0] * y_a.astype(jnp.float32)
                  + gates[:, :, 1] * y_b.astype(jnp.float32)).astype(x.dtype)
        x = x + merged @ w_out[i]
        h = rmsnorm(x, norm_ffn[i])
        x = x + jnp.square(jax.nn.relu(h @ w_ff1[i])) @ w_ff2[i]
        h = rmsnorm(x, norm_ple[i])
        x = x + jax.nn.sigmoid(h @ w_ple_gate[i]) * (p[i] @ w_ple_proj[i])
    return rmsnorm(x, norm_final)
```

```python
import contextlib
import math
import os
import numpy as np
import concourse.bass as bass
import concourse.mybir as mybir
from concourse.bass_utils import run_bass_kernel_spmd

F32 = mybir.dt.float32
BF16 = mybir.dt.bfloat16
ALU = mybir.AluOpType
AF = mybir.ActivationFunctionType
AX = mybir.AxisListType

S_LEN = 8192
NT = 64
NCTX = 48
NOWN = 16
CDEC = 0.6065306597126334
LAMBDA_INIT = 0.8 - 0.6 * math.exp(0.0)


class _Eng:
    def __init__(self, name, eng, sem):
        self.name, self.eng, self.sem = name, eng, sem
        self.count = 0
        self.seen = {}


class T:
    def __init__(self, t, name=""):
        self.t, self.name = t, name
        self.w = None
        self.r = {}

    def __getitem__(self, idx):
        return self.t[idx]


class Sched:
    N_DMA_SEMS = 48

    def __init__(self, nc, stack):
        self.nc, self.stack = nc, stack
        self.E = {}
        for name, e in (("pe", nc.tensor), ("act", nc.scalar), ("dve", nc.vector),
                        ("pool", nc.gpsimd), ("sp", nc.sync)):
            self.E[name] = _Eng(name, e, stack.enter_context(nc.semaphore("sem_" + name)))
        self.dsems = [[stack.enter_context(nc.semaphore("dsem%d" % i)), 0] for i in range(self.N_DMA_SEMS)]
        self.dnext = 0
        self.scopes = []
        self.rr = 0

    def sbuf(self, name, shape, dt):
        st = self.scopes[-1] if self.scopes else self.stack
        return T(st.enter_context(self.nc.sbuf_tensor("s_" + name, list(shape), dt)), name)

    def psum(self, name, shape, dt=F32):
        st = self.scopes[-1] if self.scopes else self.stack
        return T(st.enter_context(self.nc.psum_tensor("ps_" + name, list(shape), dt)), name)

    def dram(self, name, shape, dt):
        return T(self.nc.dram_tensor("dr_" + name, list(shape), dt, kind="Internal").ap(), name)

    def _need(self, E, ticket, raw):
        if ticket is None:
            return
        sem, val, src = ticket
        if src is E and (E.name == "pe" or not raw):
            return
        key = id(sem)
        if E.seen.get(key, 0) >= val:
            return
        E.eng.wait_ge(sem, val)
        E.seen[key] = val

    def _deps(self, E, reads, writes):
        for t in reads:
            self._need(E, t.w, True)
        for t in writes:
            self._need(E, t.w, False)
            for tk in t.r.values():
                self._need(E, tk, False)

    def _record(self, ticket, key, reads, writes):
        for t in reads:
            t.r[key] = ticket
        for t in writes:
            t.w = ticket
            t.r = {}

    def op(self, eng, fn, reads=(), writes=()):
        E = self.E[eng]
        self._deps(E, reads, writes)
        ins = fn(E.eng)
        E.count += 1
        ins.then_inc(E.sem, 1)
        self._record((E.sem, E.count, E), eng, reads, writes)

    def dma(self, q, out, in_, reads=(), writes=(), **kw):
        E = self.E[q]
        self._deps(E, reads, writes)
        slot = self.dsems[self.dnext]
        self.dnext = (self.dnext + 1) % len(self.dsems)
        sem, tot = slot
        if tot > 0:
            self._need(E, (sem, tot, None), False)
        E.eng.dma_start(out=out, in_=in_, **kw).then_inc(sem, 16)
        slot[1] = tot + 16
        self._record((sem, tot + 16, None), ("dma", id(sem)), reads, writes)

    def barrier(self):
        for E in self.E.values():
            for sem, tot in self.dsems:
                if tot > 0:
                    self._need(E, (sem, tot, None), False)
            for o in self.E.values():
                if o is not E and o.count > 0:
                    self._need(E, (o.sem, o.count, o), False)

    def push(self):
        st = contextlib.ExitStack()
        st.__enter__()
        self.scopes.append(st)

    def pop(self):
        self.barrier()
        self.scopes.pop().__exit__(None, None, None)

    def copy(self, eng, out_t, out_ap, in_t, in_ap, scale=None):
        if eng == "rr":
            eng = ("act", "dve")[self.rr % 2]
            self.rr += 1
        if eng == "act":
            if scale is None:
                self.op("act", lambda e: e.activation(out=out_ap, in_=in_ap, func=AF.Copy),
                        reads=[in_t], writes=[out_t])
            else:
                self.op("act", lambda e: e.activation(out=out_ap, in_=in_ap, func=AF.Copy, scale=scale),
                        reads=[in_t], writes=[out_t])
        else:
            if scale is None:
                self.op(eng, lambda e: e.tensor_copy(out=out_ap, in_=in_ap), reads=[in_t], writes=[out_t])
            else:
                self.op(eng, lambda e: e.tensor_scalar(out=out_ap, in0=in_ap, scalar1=scale, scalar2=None,
                                                       op0=ALU.mult), reads=[in_t], writes=[out_t])


def rsqrt(S, out_t, out_ap, in_t, in_ap, scale, bias_t, bias_ap):
    S.op("act", lambda e: e.activation(out=out_ap, in_=in_ap, func=AF.Sqrt, bias=bias_ap, scale=scale),
         reads=[in_t, bias_t], writes=[out_t])
    S.op("dve", lambda e: e.reciprocal(out=out_ap, in_=out_ap), reads=[out_t], writes=[out_t])


class Ring:
    def __init__(self, S, name, n, shape, dt, psum=False):
        self.tiles = [(S.psum if psum else S.sbuf)("%s%d" % (name, i), shape, dt) for i in range(n)]
        self.i = 0

    def next(self):
        t = self.tiles[self.i % len(self.tiles)]
        self.i += 1
        return t


def _emit_p4(S, nc, L):
    yT, oT, idf, idb, gfin, epst = L["yT"], L["oT"], L["idf"], L["idb"], L["gfin"], L["epst"]
    xs_d, p_d, out_d = L["xs_d"], L["p_d"], L["out_d"]
    Wg_s, Wor_s, Wod_s, Wout_s, W1_s, W2_s, Wpg_s, Wpp_s = (L[k] for k in ("Wg_s", "Wor_s", "Wod_s", "Wout_s", "W1_s", "W2_s", "Wpg_s", "Wpp_s"))
    S.push()
    onesb = S.sbuf("onesb", [128, 128], BF16)
    S.op("pool", lambda e: e.memset(onesb[:], 1.0), writes=[onesb])
    wp_ring = Ring(S, "wp", 4, [128, 32, 128], BF16)
    xT = S.sbuf("xT", [128, 8, 512], F32)
    hT4 = S.sbuf("hT4", [128, 8, 512], BF16)
    sqT = S.sbuf("sqT", [128, 8, 512], BF16)
    rst = S.sbuf("rst", [128, 512], F32)
    gat = S.sbuf("gat", [128, 16, 512], BF16)
    mer = S.sbuf("mer", [128, 8, 512], BF16)
    tmpA = Ring(S, "tmpA", 2, [128, 512], F32)
    hid = S.sbuf("hid", [128, 32, 512], BF16)
    sgp = S.sbuf("sgp", [128, 8, 512], BF16)
    pTt = S.sbuf("pTt", [128, 2, 512], BF16)
    xin_ring = Ring(S, "xin", 2, [128, 1024], F32)
    pin_ring = Ring(S, "pin", 2, [128, 256], F32)
    pinb_ring = Ring(S, "pinb", 2, [128, 256], BF16)
    outst_ring = Ring(S, "outst", 2, [128, 1024], F32)
    pM = Ring(S, "pM", 4, [128, 512], F32, psum=True)
    pTr = Ring(S, "pTr", 2, [128, 4, 128], F32, psum=True)
    pTb = S.psum("pTb", [128, 8, 128], BF16)
    pSS = S.psum("pSS", [128, 512])

    def load_w(scr, oc, KC):
        wp = wp_ring.next()
        S.dma("sp", wp[:, 0:KC, :], scr.t[oc], reads=[scr], writes=[wp])
        return wp

    def rms_h(gain_unused=None):
        for kc in range(8):
            S.op("act", lambda e: e.activation(out=sqT[:, kc, :], in_=xT[:, kc, :], func=AF.Square), reads=[xT], writes=[sqT])
        for kc in range(8):
            S.op("pe", lambda e: e.matmul(pSS[:], lhsT=onesb[:], rhs=sqT[:, kc, :], start=(kc == 0), stop=(kc == 7)),
                 reads=[onesb, sqT], writes=[pSS])
        rsqrt(S, rst, rst[:], pSS, pSS[:], 1.0 / 1024, epst, epst[:, 0:1])

    def apply_h():
        for kc in range(8):
            eng = ("dve", "pool")[kc % 2]
            S.op(eng, lambda e: e.tensor_tensor(out=hT4[:, kc, :], in0=xT[:, kc, :], in1=rst[:], op=ALU.mult),
                 reads=[xT, rst], writes=[hT4])

    def mm(scr, oc, KC, rhs_t, rhs_fn):
        wp = load_w(scr, oc, KC)
        pm_ = pM.next()
        for kc in range(KC):
            S.op("pe", lambda e: e.matmul(pm_[:], lhsT=wp[:, kc, :], rhs=rhs_fn(kc), start=(kc == 0), stop=(kc == KC - 1)),
                 reads=[wp, rhs_t], writes=[pm_])
        return pm_

    for blk in range(4):
        t0 = blk * 512
        for j in range(4):
            xin = xin_ring.next()
            S.dma("sp", xin[:], xs_d[(NCTX + blk * 4 + j) * 128:(NCTX + blk * 4 + j + 1) * 128, :], writes=[xin])
            for half in range(2):
                ptr = pTr.next()
                for k4 in range(4):
                    kc = half * 4 + k4
                    S.op("pe", lambda e: e.matmul(ptr[:, k4, :], lhsT=xin[:, kc * 128:(kc + 1) * 128], rhs=idf[:], start=True, stop=True),
                         reads=[xin, idf], writes=[ptr])
                S.copy("rr", xT, xT[:, half * 4:half * 4 + 4, j * 128:(j + 1) * 128], ptr, ptr[:])
            pin = pin_ring.next()
            pinb = pinb_ring.next()
            S.dma("sp", pin[:], p_d[t0 + j * 128:t0 + (j + 1) * 128, :], writes=[pin])
            S.copy("pool", pinb, pinb[:], pin, pin[:])
            for k2 in range(2):
                S.op("pe", lambda e: e.transpose(out=pTb[:, k2, :], in_=pinb[:, k2 * 128:(k2 + 1) * 128], identity=idb[:]),
                     reads=[pinb, idb], writes=[pTb])
            S.copy("dve", pTt, pTt[:, :, j * 128:(j + 1) * 128], pTb, pTb[:, 0:2, :])
        rms_h()
        apply_h()
        for oc in range(16):
            pm_ = mm(Wg_s, oc, 8, hT4, lambda kc: hT4[:, kc, :])
            S.op("act", lambda e: e.activation(out=gat[:, oc, :], in_=pm_[:], func=AF.Sigmoid), reads=[pm_], writes=[gat])
        for oc in range(8):
            pa_ = mm(Wor_s, oc, 4, yT, lambda kc: yT[:, kc, t0:t0 + 512])
            ta = tmpA.next()
            S.op("dve", lambda e: e.tensor_tensor(out=ta[:], in0=pa_[:], in1=gat[:, oc, :], op=ALU.mult), reads=[pa_, gat], writes=[ta])
            pb_ = mm(Wod_s, oc, 4, oT, lambda kc: oT[:, kc, t0:t0 + 512])
            tb = tmpA.next()
            S.op("dve", lambda e: e.tensor_tensor(out=tb[:], in0=pb_[:], in1=gat[:, 8 + oc, :], op=ALU.mult), reads=[pb_, gat], writes=[tb])
            S.op("pool", lambda e: e.tensor_tensor(out=mer[:, oc, :], in0=ta[:], in1=tb[:], op=ALU.add), reads=[ta, tb], writes=[mer])
        for oc in range(8):
            pm_ = mm(Wout_s, oc, 8, mer, lambda kc: mer[:, kc, :])
            S.op("dve", lambda e: e.tensor_tensor(out=xT[:, oc, :], in0=xT[:, oc, :], in1=pm_[:], op=ALU.add), reads=[xT, pm_], writes=[xT])
        rms_h()
        apply_h()
        for fc in range(32):
            pm_ = mm(W1_s, fc, 8, hT4, lambda kc: hT4[:, kc, :])
            tr = tmpA.next()
            S.op("act", lambda e: e.activation(out=tr[:], in_=pm_[:], func=AF.Relu), reads=[pm_], writes=[tr])
            eng = ("pool", "dve")[fc % 2]
            S.op(eng, lambda e: e.tensor_tensor(out=hid[:, fc, :], in0=tr[:], in1=tr[:], op=ALU.mult), reads=[tr], writes=[hid])
        for oc in range(8):
            pm_ = mm(W2_s, oc, 32, hid, lambda kc: hid[:, kc, :])
            S.op("dve", lambda e: e.tensor_tensor(out=xT[:, oc, :], in0=xT[:, oc, :], in1=pm_[:], op=ALU.add), reads=[xT, pm_], writes=[xT])
        rms_h()
        apply_h()
        for oc in range(8):
            pm_ = mm(Wpg_s, oc, 8, hT4, lambda kc: hT4[:, kc, :])
            S.op("act", lambda e: e.activation(out=sgp[:, oc, :], in_=pm_[:], func=AF.Sigmoid), reads=[pm_], writes=[sgp])
        for oc in range(8):
            pm_ = mm(Wpp_s, oc, 2, pTt, lambda kc: pTt[:, kc, :])
            ta = tmpA.next()
            S.op("dve", lambda e: e.tensor_tensor(out=ta[:], in0=pm_[:], in1=sgp[:, oc, :], op=ALU.mult), reads=[pm_, sgp], writes=[ta])
            S.op("pool", lambda e: e.tensor_tensor(out=xT[:, oc, :], in0=xT[:, oc, :], in1=ta[:], op=ALU.add), reads=[xT, ta], writes=[xT])
        rms_h()
        for kc in range(8):
            S.op("dve", lambda e: e.scalar_tensor_tensor(out=xT[:, kc, :], in0=xT[:, kc, :], scalar=gfin[:, kc:kc + 1], in1=rst[:],
                                                         op0=ALU.mult, op1=ALU.mult), reads=[xT, gfin, rst], writes=[xT])
        for j in range(4):
            ost = outst_ring.next()
            for half in range(2):
                ptr = pTr.next()
                for k4 in range(4):
                    kc = half * 4 + k4
                    S.op("pe", lambda e: e.matmul(ptr[:, k4, :], lhsT=xT[:, kc, j * 128:(j + 1) * 128], rhs=idf[:], start=True, stop=True),
                         reads=[xT, idf], writes=[ptr])
                S.copy("rr", ost, ost[:, half * 512:(half + 1) * 512], ptr, ptr[:].rearrange("p a b -> p (a b)"))
            S.dma("sp", out_d[t0 + j * 128:t0 + (j + 1) * 128, :], ost[:], reads=[ost])
    S.pop()


def build_program(dbg=None):
    nc = bass.Bass("TRN2", target_bir_lowering=False)

    def inp(name, shape, dt=F32):
        return nc.dram_tensor(name, list(shape), dt, kind="ExternalInput").ap()

    xs_d = inp("xs", [S_LEN, 1024])
    p_d = inp("p_own", [2048, 256])
    w_in_d = inp("w_in", [1024, 5504])
    wk_sw_d = inp("w_unused", [1, 1])
    gmix_d = inp("gmix", [128, 8])
    gffn_d = inp("gffn", [128, 8])
    gple_d = inp("gple", [128, 8])
    gfin_d = inp("gfin", [128, 8])
    mup_d = inp("mup", [128, 15])
    mun_d = inp("mun", [128, 15])
    w0_d = inp("w0t", [128, 8])
    a0_d = inp("a0t", [128, 8])
    w2_d = inp("w2t", [128, 512])
    a2_d = inp("a2t", [128, 512])
    g2_d = inp("g2", [128, 512])
    kk_d = inp("kkt", [128, 4])
    ka_d = inp("kat", [128, 4])
    rk_d = inp("rkt", [128, 4])
    lnw_d = inp("lnw_b", [128, 512])
    lnb_d = inp("lnb_b", [128, 512])
    wo_r_d = inp("rwkv_w_o", [512, 1024]) if dbg in (None, "p0") else None
    lam_d = inp("lam_b", [128, 4, 64])
    subln_d = inp("subln_b", [128, 128])
    wo_d_d = inp("da_w_o", [512, 1024]) if dbg in (None, "p0") else None
    wout_d = inp("w_out", [1024, 1024]) if dbg in (None, "p0") else None
    wff1_d = inp("w_ff1", [1024, 4096]) if dbg in (None, "p0") else None
    wff2_d = inp("w_ff2", [4096, 1024]) if dbg in (None, "p0") else None
    wpg_d = inp("w_ple_gate", [1024, 1024]) if dbg in (None, "p0") else None
    wpp_d = inp("w_ple_proj", [256, 1024]) if dbg in (None, "p0") else None
    cos_d = inp("cos_t", [S_LEN, 64])
    sin_d = inp("sin_t", [S_LEN, 64])
    eprev_d = inp("eprev", [128, NT])
    enext_d = inp("enext", [128, NT])
    keepf_d = inp("keepf", [128, NCTX])
    keepb_d = inp("keepb", [128, NCTX])
    ident_d = inp("ident", [128, 128])
    msl_d = inp("mask_sl", [128, 512])
    msu_d = inp("mask_su", [128, 512])
    mil_d = inp("mask_il", [128, 512])
    miu_d = inp("mask_iu", [128, 512])
    bones_d = inp("bones", [128, 128])
    hind_d = inp("hind", [128, 2])
    out_d = nc.dram_tensor("out", [2048, 1024], F32, kind="ExternalOutput").ap()
    dbg_outs = {}

    with contextlib.ExitStack() as stack:
        S = Sched(nc, stack)
        UT = S.dram("UT", [15, 128, S_LEN + 2], F32)
        KT = S.dram("KT", [4, 128, S_LEN], BF16)
        VD = S.dram("VD", [NT, 128, 512], BF16)

        def wscr(name, K, N):
            return S.dram(name, [N // 128, 128, K // 128, 128], BF16)
        Wg_s = wscr("Wg_s", 1024, 2048)
        Wor_s = wscr("Wor_s", 512, 1024)
        Wod_s = wscr("Wod_s", 512, 1024)
        Wout_s = wscr("Wout_s", 1024, 1024)
        W1_s = wscr("W1_s", 1024, 4096)
        W2_s = wscr("W2_s", 4096, 1024)
        Wpg_s = wscr("Wpg_s", 1024, 1024)
        Wpp_s = wscr("Wpp_s", 256, 1024)

        def const(name, src, shape, dt=F32, q="sp"):
            t = S.sbuf(name, shape, dt)
            S.dma(q, t[:], src, writes=[t])
            return t
        idf = const("idf", ident_d[:, :], [128, 128])
        idb = S.sbuf("idb", [128, 128], BF16)
        S.copy("dve", idb, idb[:], idf, idf[:])
        gmix = const("gmix", gmix_d[:, :], [128, 8])
        gffn = const("gffn", gffn_d[:, :], [128, 8])
        gple = const("gple", gple_d[:, :], [128, 8])
        gfin = const("gfin", gfin_d[:, :], [128, 8])
        QT = S.sbuf("QT", [128, 4, 2048], BF16)
        yT = S.sbuf("yT", [128, 4, 2048], BF16)
        epst = S.sbuf("epst", [128, 4], F32)
        for i_, v_ in enumerate((1e-6, 1e-5, 64e-5, 0.0)):
            S.op("pool", lambda e: e.memset(epst[:, i_:i_ + 1], v_), writes=[epst])
        qk2max = S.sbuf("qk2max", [128, 2], F32)
        S.op("pool", lambda e: e.memset(qk2max[:], 0.0), writes=[qk2max])

        S.push()
        wst_ring = Ring(S, "p0st", 2, [128, 4096], F32)
        wbf_ring = Ring(S, "p0bf", 2, [128, 4096], BF16)

        def convert(src_ap_fn, K, N, dst, gain):
            for kc in range(K // 128):
                st = wst_ring.next()
                bf = wbf_ring.next()
                S.dma("sp", st[:, 0:N], src_ap_fn(kc), writes=[st])
                if gain is None:
                    S.op("pool", lambda e: e.tensor_copy(out=bf[:, 0:N], in_=st[:, 0:N]), reads=[st], writes=[bf])
                else:
                    S.op("pool", lambda e: e.tensor_scalar(out=bf[:, 0:N], in0=st[:, 0:N],
                                                           scalar1=gain[:, kc:kc + 1], scalar2=None, op0=ALU.mult),
                         reads=[st, gain], writes=[bf])
                for o8 in range(0, N // 128, 8):
                    S.dma("sp", dst.t[o8:o8 + 8, :, kc, :].rearrange("o p c -> p o c"),
                          bf[:, o8 * 128:(o8 + 8) * 128].rearrange("p (o c) -> p o c", c=128), reads=[bf], writes=[dst])

        if dbg in (None, "p0"):
          convert(lambda kc: w_in_d[kc * 128:(kc + 1) * 128, 3456:5504], 1024, 2048, Wg_s, gmix)
          convert(lambda kc: wo_r_d[kc * 128:(kc + 1) * 128, :], 512, 1024, Wor_s, None)
          convert(lambda kc: wo_d_d[kc * 128:(kc + 1) * 128, :], 512, 1024, Wod_s, None)
          convert(lambda kc: wout_d[kc * 128:(kc + 1) * 128, :], 1024, 1024, Wout_s, None)
          convert(lambda kc: wff1_d[kc * 128:(kc + 1) * 128, :], 1024, 4096, W1_s, gffn)
          convert(lambda kc: wff2_d[kc * 128:(kc + 1) * 128, :], 4096, 1024, W2_s, None)
          convert(lambda kc: wpg_d[kc * 128:(kc + 1) * 128, :], 1024, 1024, Wpg_s, gple)
          convert(lambda kc: wpp_d[kc * 128:(kc + 1) * 128, :], 256, 1024, Wpp_s, None)
        S.pop()

        if dbg == "p0":
            S.barrier()
            return nc
        if dbg == "p4":
            oT = S.sbuf("oT", [128, 4, 2048], BF16)
            S.op("pool", lambda e: e.memset(yT[:], 0.0), writes=[yT])
            S.op("pool", lambda e: e.memset(oT[:], 0.0), writes=[oT])
            _emit_p4(S, nc, locals())
            S.barrier()
            return nc
        S.push()
        NW = 3456
        Wb = S.sbuf("Wb", [128, 8, NW], BF16)
        wst1 = Ring(S, "p1wst", 2, [128, NW], F32)
        for kc in range(8):
            st = wst1.next()
            S.dma("sp", st[:], w_in_d[kc * 128:(kc + 1) * 128, 0:NW], writes=[st])
            S.op("pool", lambda e: e.tensor_scalar(out=Wb[:, kc, :], in0=st[:], scalar1=gmix[:, kc:kc + 1],
                                                   scalar2=None, op0=ALU.mult), reads=[st, gmix], writes=[Wb])
        x_ring = Ring(S, "xt", 3, [128, 1024], F32)
        junk_ring = Ring(S, "junk", 2, [128, 1024], BF16)
        xn_ring = Ring(S, "xn", 2, [128, 1024], BF16)
        ss_ring = Ring(S, "ss", 4, [128, 2], F32)
        hT_ring = Ring(S, "hT", 2, [128, 8, 512], BF16)
        tp_ring = Ring(S, "tp", 2, [128, 8, 128], BF16, psum=True)
        pu_ring = Ring(S, "pu", 3, [128, 512], F32, psum=True)
        pd_ring = Ring(S, "pd", 2, [128, 512], F32, psum=True)
        ust_ring = Ring(S, "ust", 3, [128, 512], F32)
        kf_ring = Ring(S, "kf", 2, [128, 8, 64], F32)
        kb_ring = Ring(S, "kb", 2, [128, 8, 64], BF16)
        cs_ring = Ring(S, "cs", 2, [128, 2, 8, 8], F32)
        rt_ring = Ring(S, "rt", 2, [128, 4, 8, 8], F32)
        sq_ring = Ring(S, "sq", 2, [128, 512], F32)
        red_ring = Ring(S, "red", 2, [128, 10], F32)
        kts_ring = Ring(S, "kts", 2, [128, 4, 128], BF16)
        vb_ring = Ring(S, "vb", 2, [128, 512], BF16)

        def rope_and_T(pd, cs, scale, is_q, s, stat_col):
            kf = kf_ring.next()
            kb = kb_ring.next()
            S.copy("act", kf, kf[:].rearrange("p a d -> p (a d)"), pd, pd[:], scale=scale)
            sq = sq_ring.next()
            red = red_ring.next()
            S.op("pool", lambda e: e.tensor_tensor(out=sq[:], in0=kf[:].rearrange("p a d -> p (a d)"),
                                                   in1=kf[:].rearrange("p a d -> p (a d)"), op=ALU.mult),
                 reads=[kf], writes=[sq])
            S.op("dve", lambda e: e.tensor_reduce(out=red[:, 0:8], in_=sq[:].rearrange("p (a d) -> p a d", d=64),
                                                  axis=AX.X, op=ALU.add), reads=[sq], writes=[red])
            S.op("dve", lambda e: e.tensor_reduce(out=red[:, 8:9], in_=red[:, 0:8], axis=AX.X, op=ALU.max),
                 reads=[red], writes=[red])
            S.op("dve", lambda e: e.tensor_tensor(out=qk2max[:, stat_col:stat_col + 1],
                                                  in0=qk2max[:, stat_col:stat_col + 1], in1=red[:, 8:9], op=ALU.max),
                 reads=[red, qk2max], writes=[qk2max])
            S.copy("act", kb, kb[:], kf, kf[:])
            rt = rt_ring.next()
            x1 = kf[:, :, 0:8]
            x2 = kf[:, :, 8:16]
            c_ = cs[:, 0, :, :]
            s_ = cs[:, 1, :, :]
            S.op("dve", lambda e: e.tensor_tensor(out=rt[:, 0], in0=x1, in1=c_, op=ALU.mult), reads=[kf, cs], writes=[rt])
            S.op("dve", lambda e: e.tensor_tensor(out=rt[:, 1], in0=x2, in1=s_, op=ALU.mult), reads=[kf, cs], writes=[rt])
            S.op("dve", lambda e: e.tensor_tensor(out=rt[:, 2], in0=x2, in1=c_, op=ALU.mult), reads=[kf, cs], writes=[rt])
            S.op("dve", lambda e: e.tensor_tensor(out=rt[:, 3], in0=x1, in1=s_, op=ALU.mult), reads=[kf, cs], writes=[rt])
            S.op("dve", lambda e: e.tensor_tensor(out=kb[:, :, 0:8], in0=rt[:, 0], in1=rt[:, 1], op=ALU.subtract),
                 reads=[rt], writes=[kb])
            S.op("dve", lambda e: e.tensor_tensor(out=kb[:, :, 8:16], in0=rt[:, 2], in1=rt[:, 3], op=ALU.add),
                 reads=[rt], writes=[kb])
            tp = tp_ring.next()
            kb2 = kb[:].rearrange("p a d -> p (a d)")
            for hd in range(4):
                S.op("pe", lambda e: e.transpose(out=tp[:, hd, :], in_=kb2[:, hd * 128:(hd + 1) * 128], identity=idb[:]),
                     reads=[kb, idb], writes=[tp])
            if is_q:
                o = s - NCTX
                S.copy("dve", QT, QT[:, :, o * 128:(o + 1) * 128], tp, tp[:, 0:4, :])
            else:
                kts = kts_ring.next()
                S.copy("dve", kts, kts[:], tp, tp[:, 0:4, :])
                S.dma("sp", KT.t[:, :, s * 128:(s + 1) * 128].rearrange("h p t -> p h t"), kts[:],
                      reads=[kts], writes=[KT])

        for blk in range(16):
            own = blk >= 12
            hT = hT_ring.next()
            for j in range(4):
                s = blk * 4 + j
                xt = x_ring.next()
                S.dma("sp", xt[:], xs_d[s * 128:(s + 1) * 128, :], writes=[xt])
                junk = junk_ring.next()
                ss = ss_ring.next()
                S.op("pool", lambda e: e.memset(ss[:], 0.0), writes=[ss])
                S.op("act", lambda e: e.activation(out=junk[:], in_=xt[:], func=AF.Square, accum_out=ss[:, 0:1]),
                     reads=[xt], writes=[junk, ss])
                rsqrt(S, ss, ss[:, 1:2], ss, ss[:, 0:1], 1.0 / 1024, epst, epst[:, 0:1])
                xn = xn_ring.next()
                S.op("dve", lambda e: e.tensor_scalar(out=xn[:], in0=xt[:], scalar1=ss[:, 1:2], scalar2=None,
                                                      op0=ALU.mult), reads=[xt, ss], writes=[xn])
                tp = tp_ring.next()
                for kc in range(8):
                    S.op("pe", lambda e: e.transpose(out=tp[:, kc, :], in_=xn[:, kc * 128:(kc + 1) * 128],
                                                     identity=idb[:]), reads=[xn, idb], writes=[tp])
                S.copy("act", hT, hT[:, :, j * 128:(j + 1) * 128], tp, tp[:])
                cs = cs_ring.next()
                S.dma("sp", cs[:, 0].rearrange("p a d -> p (a d)"), cos_d[s * 128:(s + 1) * 128, :], writes=[cs])
                S.dma("sp", cs[:, 1].rearrange("p a d -> p (a d)"), sin_d[s * 128:(s + 1) * 128, :], writes=[cs])
                parts = [("k", 1920 + 512), ("v", 1920 + 1024)]
                if own:
                    parts.append(("q", 1920))
                for nm, c0 in parts:
                    pd = pd_ring.next()
                    for kc in range(8):
                        S.op("pe", lambda e: e.matmul(pd[:], lhsT=hT[:, kc, j * 128:(j + 1) * 128],
                                                      rhs=Wb[:, kc, c0:c0 + 512], start=(kc == 0), stop=(kc == 7)),
                             reads=[hT, Wb], writes=[pd])
                    if nm == "v":
                        vb = vb_ring.next()
                        S.copy("act", vb, vb[:], pd, pd[:])
                        S.dma("sp", VD.t[s], vb[:], reads=[vb], writes=[VD])
                    elif nm == "k":
                        rope_and_T(pd, cs, None, False, s, 1)
                    else:
                        rope_and_T(pd, cs, 0.125, True, s, 0)
            ccs = range(15) if (own or blk == 0 or blk == 11) else range(4, 14)
            for cc in ccs:
                pu = pu_ring.next()
                for kc in range(8):
                    S.op("pe", lambda e: e.matmul(pu[:], lhsT=Wb[:, kc, cc * 128:(cc + 1) * 128], rhs=hT[:, kc, :],
                                                  start=(kc == 0), stop=(kc == 7)), reads=[hT, Wb], writes=[pu])
                ust = ust_ring.next()
                S.copy("rr", ust, ust[:], pu, pu[:])
                S.dma("sp", UT.t[cc, :, 1 + blk * 512:1 + (blk + 1) * 512], ust[:], reads=[ust], writes=[UT])
        wr = S.sbuf("wrapc", [128, 15, 2], F32)
        S.dma("sp", wr[:, :, 0:1], UT.t[:, :, S_LEN:S_LEN + 1].rearrange("c p o -> p c o"), reads=[UT], writes=[wr],
              allow_slow_non_contiguous=True)
        S.dma("sp", wr[:, :, 1:2], UT.t[:, :, 1:2].rearrange("c p o -> p c o"), reads=[UT], writes=[wr],
              allow_slow_non_contiguous=True)
        S.dma("sp", UT.t[:, :, 0:1].rearrange("c p o -> p c o"), wr[:, :, 0:1], reads=[wr], writes=[UT],
              allow_slow_non_contiguous=True)
        S.dma("sp", UT.t[:, :, S_LEN + 1:S_LEN + 2].rearrange("c p o -> p c o"), wr[:, :, 1:2], reads=[wr], writes=[UT],
              allow_slow_non_contiguous=True)
        S.pop()

        if dbg == "p1":
            for nm, t, shape, dt in (("d_UT", UT, [15, 128, S_LEN + 2], F32), ("d_KT", KT, [4, 128, S_LEN], BF16),
                                     ("d_VD", VD, [NT, 128, 512], BF16)):
                o = nc.dram_tensor(nm, shape, dt, kind="ExternalOutput").ap()
                S.dma("sp", o, t.t, reads=[t])
            o = nc.dram_tensor("d_QT", [128, 4, 2048], BF16, kind="ExternalOutput").ap()
            S.dma("sp", o, QT[:], reads=[QT])
            o = nc.dram_tensor("d_qk", [128, 2], F32, kind="ExternalOutput").ap()
            S.dma("sp", o, qk2max[:], reads=[qk2max])
            S.barrier()
            return nc

        S.push()
        mup = const("mup", mup_d[:, :], [128, 15])
        mun = const("mun", mun_d[:, :], [128, 15])
        c0t = S.sbuf("c0t", [128, 15], F32)
        S.op("dve", lambda e: e.tensor_tensor(out=c0t[:], in0=mup[:], in1=mun[:], op=ALU.add), reads=[mup, mun], writes=[c0t])
        S.op("dve", lambda e: e.tensor_scalar(out=c0t[:], in0=c0t[:], scalar1=-1.0, scalar2=1.0, op0=ALU.mult, op1=ALU.add),
             reads=[c0t], writes=[c0t])
        w0t = const("w0t", w0_d[:, :], [128, 8])
        a0t = const("a0t", a0_d[:, :], [128, 8])
        kkt = const("kkt", kk_d[:, :], [128, 4])
        kat = const("kat", ka_d[:, :], [128, 4])
        rkt = const("rkt", rk_d[:, :], [128, 4])
        omka = S.sbuf("omka", [128, 4], F32)
        S.op("dve", lambda e: e.tensor_scalar(out=omka[:], in0=kat[:], scalar1=-1.0, scalar2=1.0, op0=ALU.mult, op1=ALU.add),
             reads=[kat], writes=[omka])
        eprev = const("eprev", eprev_d[:, :], [128, NT])
        enext = const("enext", enext_d[:, :], [128, NT])
        keepf = const("keepf", keepf_d[:, :], [128, NCTX])
        keepb = const("keepb", keepb_d[:, :], [128, NCTX])
        bones = const("bones", bones_d[:, :], [128, 128])
        hind = const("hind", hind_d[:, :], [128, 2])
        lnw = const("lnw", lnw_d[:, :], [128, 512])
        lnb = const("lnb", lnb_d[:, :], [128, 512])
        ones_t = S.sbuf("ones_t", [128, 128], F32)
        S.op("pool", lambda e: e.memset(ones_t[:], 1.0), writes=[ones_t])

        cbf_st = S.sbuf("cbf_st", [128, 512], F32)

        def cbf(name, src, shape):
            S.dma("sp", cbf_st[:], src, writes=[cbf_st])
            b = S.sbuf(name, shape, BF16)
            S.copy("pool", b, b[:], cbf_st, cbf_st[:])
            return b
        w2b = cbf("w2b", w2_d[:, :], [128, 512])
        a2b = cbf("a2b", a2_d[:, :], [128, 512])
        g2b = cbf("g2b", g2_d[:, :], [128, 512])
        msl = const("msl", msl_d[:, :], [128, 512])
        msu = const("msu", msu_d[:, :], [128, 512])
        mil = const("mil", mil_d[:, :], [128, 512])
        miu = const("miu", miu_d[:, :], [128, 512])
        idb4 = S.sbuf("idb4", [128, 4, 128], BF16)
        for g in range(4):
            S.copy("pool", idb4, idb4[:, g, :], idb, idb[:])

        Hs = [S.sbuf("Hs%d" % d, [128, 4, 64], F32) for d in range(2)]
        for d in range(2):
            S.op("pool", lambda e: e.memset(Hs[d][:], 0.0), writes=[Hs[d]])
        yacc = S.sbuf("yacc", [128, NOWN, 512], F32)
        bon = S.sbuf("bon", [128, NOWN, 8], F32)
        S.op("pool", lambda e: e.memset(bon[:], 0.0), writes=[bon])
        vtok_own = S.sbuf("vtok_own", [128, NOWN, 512], BF16)
        gate_own = S.sbuf("gate_own", [128, NOWN, 512], BF16)

        win_ring = Ring(S, "win", 1, [128, 15, 130], F32)
        sh_ring = Ring(S, "sh", 1, [128, 15, 128], F32)
        shtmp_ring = Ring(S, "shtmp", 2, [128, 128], F32)
        f4 = lambda nm, n=1: Ring(S, nm, n, [128, 4, 128], F32)
        b4 = lambda nm, n=2: Ring(S, nm, n, [128, 4, 128], BF16)
        th_ring = Ring(S, "th", 2, [128, 2, 128], BF16)
        sg_ring, a_ring, kx_ring, sq4_ring, kkr_ring, t_ring = f4("sg"), f4("a"), f4("kx"), f4("sq4"), f4("kkr"), f4("tt")
        kd_ring, b_ring, gc_ring, gcx_ring, e_ring = f4("kd"), f4("b"), f4("gc"), f4("gcx"), f4("e", 3)
        sc_ring = Ring(S, "sc", 2, [128, 8, 4], F32)
        bT_ring, kT_ring, vT_ring = b4("bT"), b4("kT"), b4("vT")
        aTz_ring = [b4("aTz0"), b4("aTz1")]
        rTz_ring = [b4("rTz0"), b4("rTz1")]
        for rg in aTz_ring + rTz_ring:
            for t_ in rg.tiles:
                S.op("pool", lambda e: e.memset(t_[:], 0.0), writes=[t_])
        btok_ring, ktok_ring, vtok_ring = b4("btok", 1), b4("ktok", 1), b4("vtok", 1)
        P_ring, PT_ring, TT_ring, PI_ring = b4("Pm", 2), b4("PTm", 2), b4("TTm", 2), b4("PIm", 1)
        Aak_ring, Arb_ring, Ark_ring = b4("Aak", 1), b4("Arb", 1), b4("Ark", 1)
        H0_ring = Ring(S, "H0", 2, [128, 4, 64], BF16)
        X1_ring = Ring(S, "X1", 2, [128, 4, 64], BF16)
        U_ring = Ring(S, "U", 2, [128, 4, 64], BF16)
        prod_ring = f4("prod")
        psX = S.psum("psX", [128, 4, 128])
        psY = S.psum("psY", [128, 4, 128])
        psZ = S.psum("psZ", [128, 4, 128])
        psA = Ring(S, "psA", 2, [128, 4, 128], F32, psum=True)
        psC = Ring(S, "psC", 2, [128, 4, 128], F32, psum=True)
        psT = S.psum("psT", [128, 8, 128], BF16)

        def step(s, d, own):
            fwd = (d == 0)
            ccs = list(range(15)) if own else list(range(4, 14))
            c_lo, ncc = ccs[0], len(ccs)
            win = win_ring.next()
            S.dma("sp", win[:, c_lo:c_lo + ncc, :], UT.t[c_lo:c_lo + ncc, :, s * 128:s * 128 + 130].rearrange("c p t -> p c t"),
                  reads=[UT], writes=[win])
            S.op("pool", lambda e: e.tensor_scalar(out=win[:, c_lo:c_lo + ncc, 0:1], in0=win[:, c_lo:c_lo + ncc, 0:1],
                                                   scalar1=eprev[:, s:s + 1], scalar2=None, op0=ALU.mult),
                 reads=[win, eprev], writes=[win])
            S.op("pool", lambda e: e.tensor_scalar(out=win[:, c_lo:c_lo + ncc, 129:130], in0=win[:, c_lo:c_lo + ncc, 129:130],
                                                   scalar1=enext[:, s:s + 1], scalar2=None, op0=ALU.mult),
                 reads=[win, enext], writes=[win])
            sh = sh_ring.next()
            for cc in ccs:
                S.op("pool", lambda e: e.tensor_scalar(out=sh[:, cc, :], in0=win[:, cc, 1:129], scalar1=c0t[:, cc:cc + 1],
                                                       scalar2=None, op0=ALU.mult), reads=[win, c0t], writes=[sh])
                for (lo, mu_) in ((0, mup), (2, mun)):
                    tmp_ = shtmp_ring.next()
                    S.op("pool", lambda e: e.tensor_scalar(out=tmp_[:], in0=win[:, cc, lo:lo + 128], scalar1=mu_[:, cc:cc + 1],
                                                           scalar2=None, op0=ALU.mult), reads=[win, mu_], writes=[tmp_])
                    S.op("pool", lambda e: e.tensor_tensor(out=sh[:, cc, :], in0=sh[:, cc, :], in1=tmp_[:], op=ALU.add),
                         reads=[sh, tmp_], writes=[sh])
            STG = int(os.environ.get("MK_STAGE", "9"))
            if STG <= 1:
                return
            R_, K_, V_ = sh[:, 0:4, :], sh[:, 4:8, :], sh[:, 8:12, :]
            dp = slice(d * 64, d * 64 + 64)
            th = th_ring.next()
            S.op("act", lambda e: e.activation(out=th[dp, 0, :], in_=sh[dp, 12, :], func=AF.Tanh), reads=[sh], writes=[th])
            S.copy("dve", th, th[dp, 1, :], sh, sh[dp, 13, :])
            sg = sg_ring.next()
            a_ = a_ring.next()
            pz = psA.next()
            for q4 in range(4):
                S.op("pe", lambda e: e.matmul(pz[:, q4, :], lhsT=w2b[dp, q4 * 128:(q4 + 1) * 128], rhs=th[dp, 0, :],
                                              start=True, stop=True), reads=[w2b, th], writes=[pz])
            for q4 in range(4):
                S.op("act", lambda e: e.activation(out=sg[:, q4, :], in_=pz[:, q4, :], func=AF.Sigmoid,
                                                   bias=w0t[:, d * 4 + q4:d * 4 + q4 + 1]), reads=[pz, w0t], writes=[sg])
            pa = psA.next()
            for q4 in range(4):
                S.op("pe", lambda e: e.matmul(pa[:, q4, :], lhsT=a2b[dp, q4 * 128:(q4 + 1) * 128], rhs=th[dp, 1, :],
                                              start=True, stop=True), reads=[a2b, th], writes=[pa])
            for q4 in range(4):
                S.op("act", lambda e: e.activation(out=a_[:, q4, :], in_=pa[:, q4, :], func=AF.Sigmoid,
                                                   bias=a0t[:, d * 4 + q4:d * 4 + q4 + 1]), reads=[pa, a0t], writes=[a_])
            if STG <= 2:
                return
            kx = kx_ring.next()
            for q4 in range(4):
                S.op("pool", lambda e: e.tensor_scalar(out=kx[:, q4, :], in0=sh[:, 4 + q4, :], scalar1=kkt[:, q4:q4 + 1],
                                                       scalar2=None, op0=ALU.mult), reads=[sh, kkt], writes=[kx])
            sq4 = sq4_ring.next()
            S.op("pool", lambda e: e.tensor_tensor(out=sq4[:], in0=kx[:], in1=kx[:], op=ALU.mult), reads=[kx], writes=[sq4])
            pn = psA.next()
            for q4 in range(4):
                S.op("pe", lambda e: e.matmul(pn[:, q4, :], lhsT=bones[:], rhs=sq4[:, q4, :], start=True, stop=True),
                     reads=[bones, sq4], writes=[pn])
            kkr = kkr_ring.next()
            S.op("act", lambda e: e.activation(out=kkr[:], in_=pn[:], func=AF.Sqrt), reads=[pn], writes=[kkr])
            S.op("dve", lambda e: e.tensor_scalar(out=kkr[:], in0=kkr[:], scalar1=1e-12, scalar2=None, op0=ALU.max),
                 reads=[kkr], writes=[kkr])
            S.op("dve", lambda e: e.reciprocal(out=kkr[:], in_=kkr[:]), reads=[kkr], writes=[kkr])
            S.op("dve", lambda e: e.tensor_tensor(out=kkr[:], in0=kkr[:], in1=kx[:], op=ALU.mult), reads=[kkr, kx], writes=[kkr])
            tt = t_ring.next()
            for q4 in range(4):
                S.op("dve", lambda e: e.tensor_scalar(out=tt[:, q4, :], in0=a_[:, q4, :], scalar1=kat[:, q4:q4 + 1],
                                                      scalar2=omka[:, q4:q4 + 1], op0=ALU.mult, op1=ALU.add),
                     reads=[a_, kat, omka], writes=[tt])
            kd = kd_ring.next()
            S.op("dve", lambda e: e.tensor_tensor(out=kd[:], in0=tt[:], in1=K_, op=ALU.mult), reads=[tt, sh], writes=[kd])
            bb = b_ring.next()
            S.op("pool", lambda e: e.tensor_tensor(out=bb[:], in0=kkr[:], in1=a_[:], op=ALU.mult), reads=[kkr, a_], writes=[bb])
            if STG <= 3:
                return
            gc = gc_ring.next()
            gcx = gcx_ring.next()
            sc = sc_ring.next()
            for q4 in range(4):
                S.op("dve", lambda e: e.tensor_tensor_scan(out=gc[:, q4, :], data0=ones_t[:], data1=sg[:, q4, :], initial=0.0,
                                                           op0=ALU.mult, op1=ALU.add), reads=[ones_t, sg], writes=[gc])
            if not fwd:
                S.copy("dve", sc, sc[:, 7, :], gc, gc[:, :, 127])
                for q4 in range(4):
                    S.op("dve", lambda e: e.scalar_tensor_tensor(out=gc[:, q4, :], in0=gc[:, q4, :], scalar=-1.0,
                                                                 in1=sg[:, q4, :], op0=ALU.mult, op1=ALU.add),
                         reads=[gc, sg], writes=[gc])
                    S.op("dve", lambda e: e.tensor_scalar(out=gc[:, q4, :], in0=gc[:, q4, :], scalar1=sc[:, 7, q4:q4 + 1],
                                                          scalar2=None, op0=ALU.add), reads=[gc, sc], writes=[gc])
            S.op("pool", lambda e: e.tensor_tensor(out=gcx[:], in0=gc[:], in1=sg[:], op=ALU.subtract), reads=[gc, sg], writes=[gcx])
            mid = 63 if fwd else 64
            last = 127 if fwd else 0
            S.op("dve", lambda e: e.tensor_scalar(out=sc[:, 0, :], in0=gc[:, :, mid], scalar1=CDEC, scalar2=None, op0=ALU.mult),
                 reads=[gc], writes=[sc])
            S.op("dve", lambda e: e.tensor_scalar(out=sc[:, 1, :], in0=gc[:, :, mid], scalar1=-CDEC, scalar2=None, op0=ALU.mult),
                 reads=[gc], writes=[sc])
            S.op("dve", lambda e: e.tensor_tensor(out=sc[:, 5, :], in0=gc[:, :, last], in1=gc[:, :, mid], op=ALU.subtract),
                 reads=[gc], writes=[sc])
            S.op("act", lambda e: e.activation(out=sc[:, 2, :], in_=gc[:, :, mid], func=AF.Exp, scale=-CDEC), reads=[gc], writes=[sc])
            S.op("act", lambda e: e.activation(out=sc[:, 3, :], in_=gc[:, :, last], func=AF.Exp, scale=-CDEC), reads=[gc], writes=[sc])
            S.op("act", lambda e: e.activation(out=sc[:, 4, :], in_=sc[:, 5, :], func=AF.Exp, scale=-CDEC), reads=[sc], writes=[sc])
            if not own:
                keep = (keepf if fwd else keepb)
                for r in (3, 4):
                    S.op("dve", lambda e: e.tensor_scalar(out=sc[:, r, :], in0=sc[:, r, :], scalar1=keep[:, s:s + 1],
                                                          scalar2=None, op0=ALU.mult), reads=[sc, keep], writes=[sc])
            eP, eN, ePm = e_ring.next(), e_ring.next(), e_ring.next()
            for q4 in range(4):
                S.op("act", lambda e: e.activation(out=eN[:, q4, :], in_=gc[:, q4, :], func=AF.Exp, scale=CDEC,
                                                   bias=sc[:, 1, q4:q4 + 1]), reads=[gc, sc], writes=[eN])
                S.op("act", lambda e: e.activation(out=ePm[:, q4, :], in_=gcx[:, q4, :], func=AF.Exp, scale=-CDEC,
                                                   bias=sc[:, 0, q4:q4 + 1]), reads=[gcx, sc], writes=[ePm])
                if own:
                    S.op("act", lambda e: e.activation(out=eP[:, q4, :], in_=gc[:, q4, :], func=AF.Exp, scale=-CDEC,
                                                       bias=sc[:, 0, q4:q4 + 1]), reads=[gc, sc], writes=[eP])
            bT, kT, vT = bT_ring.next(), kT_ring.next(), vT_ring.next()
            aTz = [aTz_ring[0].next(), aTz_ring[1].next()]
            for par in range(2):
                pp_ = slice(par * 64, par * 64 + 64)
                S.op("dve", lambda e: e.scalar_tensor_tensor(out=aTz[par][pp_], in0=kkr[pp_], scalar=-1.0, in1=ePm[pp_],
                                                             op0=ALU.mult, op1=ALU.mult), reads=[kkr, ePm], writes=[aTz[par]])
            S.op("pool", lambda e: e.tensor_tensor(out=bT[:], in0=bb[:], in1=eN[:], op=ALU.mult), reads=[bb, eN], writes=[bT])
            S.op("dve", lambda e: e.tensor_tensor(out=kT[:], in0=kd[:], in1=eN[:], op=ALU.mult), reads=[kd, eN], writes=[kT])
            S.copy("pool", vT, vT[:], sh, V_)
            rTz = None
            if own:
                rTz = [rTz_ring[0].next(), rTz_ring[1].next()]
                for par in range(2):
                    pp_ = slice(par * 64, par * 64 + 64)
                    S.op("pool", lambda e: e.tensor_tensor(out=rTz[par][pp_], in0=sh[pp_, 0:4, :], in1=eP[pp_], op=ALU.mult),
                         reads=[sh, eP], writes=[rTz[par]])
                prod = prod_ring.next()
                S.op("pool", lambda e: e.tensor_tensor(out=prod[:], in0=R_, in1=kd[:], op=ALU.mult), reads=[sh, kd], writes=[prod])
                for q4 in range(4):
                    S.op("pool", lambda e: e.tensor_scalar(out=prod[:, q4, :], in0=prod[:, q4, :], scalar1=rkt[:, q4:q4 + 1],
                                                           scalar2=None, op0=ALU.mult), reads=[prod, rkt], writes=[prod])
                pb = psC.next()
                for q4 in range(4):
                    S.op("pe", lambda e: e.matmul(pb[:, 0, q4 * 2:q4 * 2 + 2], lhsT=prod[:, q4, :], rhs=hind[:], start=True, stop=True),
                         reads=[prod, hind], writes=[pb])
                o = s - NCTX
                S.op("dve", lambda e: e.tensor_tensor(out=bon[:, o, :], in0=bon[:, o, :], in1=pb[:, 0, 0:8], op=ALU.add),
                     reads=[bon, pb], writes=[bon])
            if STG <= 4:
                return
            btok, ktok, vtok = btok_ring.next(), ktok_ring.next(), vtok_ring.next()
            for src, dst in ((bT, btok), (kT, ktok), (vT, vtok)):
                for q4 in range(4):
                    S.op("pe", lambda e: e.transpose(out=psT[:, q4, :], in_=src[:, q4, :], identity=idb[:]),
                         reads=[src, idb], writes=[psT])
                S.copy("rr", dst, dst[:], psT, psT[:, 0:4, :])
            if own and fwd:
                o = s - NCTX
                S.copy("pool", vtok_own, vtok_own[:, o, :], vtok, vtok[:].rearrange("p a b -> p (a b)"))
                gsb = th_ring.next()
                S.op("act", lambda e: e.activation(out=gsb[:, 0, :], in_=sh[:, 14, :], func=AF.Sigmoid), reads=[sh], writes=[gsb])
                pg = psC.next()
                S.op("pe", lambda e: e.matmul(pg[:].rearrange("p a b -> p (a b)"), lhsT=gsb[:, 0, :], rhs=g2b[:], start=True, stop=True),
                     reads=[gsb, g2b], writes=[pg])
                S.copy("act", gate_own, gate_own[:, o, :], pg, pg[:].rearrange("p a b -> p (a b)"))
            if STG <= 5:
                return
            mL, mLT, mI = (msl, msu, miu) if fwd else (msu, msl, mil)
            H = Hs[d]
            for hg in range(2):
                def hsl(hh):
                    h = hg * 4 + hh
                    return h // 2, h % 2
                pAak = psA.next()
                for hh in range(4):
                    q4, par = hsl(hh)
                    ps_ = slice(par * 64, par * 64 + 64)
                    S.op("pe", lambda e: e.matmul(psX[:, hh, :], lhsT=aTz[par][:, q4, :], rhs=bT[:, q4, :], start=True, stop=True),
                         reads=[aTz[par], bT], writes=[psX])
                    S.op("pe", lambda e: e.matmul(psY[:, hh, :], lhsT=bT[:, q4, :], rhs=aTz[par][:, q4, :], start=True, stop=True),
                         reads=[aTz[par], bT], writes=[psY])
                    S.op("pe", lambda e: e.matmul(pAak[:, hh, :], lhsT=kT[:, q4, :], rhs=aTz[par][:, q4, :], start=True, stop=True),
                         reads=[aTz[par], kT], writes=[pAak])
                Pm, PTm, TTm, Aak = P_ring.next(), PT_ring.next(), TT_ring.next(), Aak_ring.next()
                m2 = lambda m: m[:].rearrange("p (a b) -> p a b", b=128)
                S.op("dve", lambda e: e.tensor_tensor(out=Pm[:], in0=psX[:], in1=m2(mL), op=ALU.mult), reads=[psX, mL], writes=[Pm])
                S.op("dve", lambda e: e.tensor_tensor(out=PTm[:], in0=psY[:], in1=m2(mLT), op=ALU.mult), reads=[psY, mLT], writes=[PTm])
                S.op("dve", lambda e: e.tensor_tensor(out=Aak[:], in0=pAak[:], in1=m2(mLT), op=ALU.mult), reads=[pAak, mLT], writes=[Aak])
                Arb = Ark = None
                if own:
                    pArb, pArk = psA.next(), psC.next()
                    for hh in range(4):
                        q4, par = hsl(hh)
                        ps_ = slice(par * 64, par * 64 + 64)
                        S.op("pe", lambda e: e.matmul(pArb[:, hh, :], lhsT=bT[:, q4, :], rhs=rTz[par][:, q4, :], start=True, stop=True),
                             reads=[rTz[par], bT], writes=[pArb])
                        S.op("pe", lambda e: e.matmul(pArk[:, hh, :], lhsT=kT[:, q4, :], rhs=rTz[par][:, q4, :], start=True, stop=True),
                             reads=[rTz[par], kT], writes=[pArk])
                    Arb, Ark = Arb_ring.next(), Ark_ring.next()
                    S.op("dve", lambda e: e.tensor_tensor(out=Arb[:], in0=pArb[:], in1=m2(mI), op=ALU.mult), reads=[pArb, mI], writes=[Arb])
                    S.op("dve", lambda e: e.tensor_tensor(out=Ark[:], in0=pArk[:], in1=m2(mI), op=ALU.mult), reads=[pArk, mI], writes=[Ark])
                S.op("pool", lambda e: e.tensor_tensor(out=TTm[:], in0=PTm[:], in1=idb4[:], op=ALU.add), reads=[PTm, idb4], writes=[TTm])
                for lev in range(1, 7):
                    for hh in range(4):
                        S.op("pe", lambda e: e.matmul(psX[:, hh, :], lhsT=PTm[:, hh, :], rhs=Pm[:, hh, :], start=True, stop=True),
                             reads=[Pm, PTm], writes=[psX])
                    if lev < 6:
                        for hh in range(4):
                            S.op("pe", lambda e: e.matmul(psY[:, hh, :], lhsT=Pm[:, hh, :], rhs=PTm[:, hh, :], start=True, stop=True),
                                 reads=[Pm, PTm], writes=[psY])
                    Pn = P_ring.next()
                    S.copy("act", Pn, Pn[:], psX, psX[:])
                    if lev < 6:
                        PTn = PT_ring.next()
                        S.copy("dve", PTn, PTn[:], psY, psY[:])
                    for hh in range(4):
                        S.op("pe", lambda e: e.matmul(psZ[:, hh, :], lhsT=Pn[:, hh, :], rhs=TTm[:, hh, :], start=True, stop=True),
                             reads=[Pn, TTm], writes=[psZ])
                    dT = PI_ring.next()
                    S.copy("rr", dT, dT[:], psZ, psZ[:])
                    TTn = TT_ring.next()
                    S.op("pool", lambda e: e.tensor_tensor(out=TTn[:], in0=dT[:], in1=TTm[:], op=ALU.add), reads=[dT, TTm], writes=[TTn])
                    Pm, TTm = Pn, TTn
                    if lev < 6:
                        PTm = PTn
                if STG <= 6:
                    continue
                H0 = H0_ring.next()
                for qq in range(2):
                    q4 = hg * 2 + qq
                    S.op("dve", lambda e: e.tensor_scalar(out=H0[:, q4, :], in0=H[:, q4, :], scalar1=sc[:, 2, q4:q4 + 1],
                                                          scalar2=None, op0=ALU.mult), reads=[H, sc], writes=[H0])
                pX1 = psC.next()
                for hh in range(4):
                    q4, par = hsl(hh)
                    ps_ = slice(par * 64, par * 64 + 64)
                    S.op("pe", lambda e: e.matmul(pX1[:, hh, 0:64], lhsT=aTz[par][:, q4, :], rhs=H0[:, q4, :], start=True, stop=False),
                         reads=[aTz[par], H0], writes=[pX1])
                    S.op("pe", lambda e: e.matmul(pX1[:, hh, 0:64], lhsT=Aak[:, hh, :], rhs=vtok[:, q4, ps_], start=False, stop=True),
                         reads=[Aak, vtok], writes=[pX1])
                X1 = X1_ring.next()
                S.copy("act", X1, X1[:], pX1, pX1[:, :, 0:64])
                pU = psC.next()
                for hh in range(4):
                    S.op("pe", lambda e: e.matmul(pU[:, hh, 0:64], lhsT=TTm[:, hh, :], rhs=X1[:, hh, :], start=True, stop=True),
                         reads=[TTm, X1], writes=[pU])
                U = U_ring.next()
                S.copy("dve", U, U[:], pU, pU[:, :, 0:64])
                if own:
                    pY = psA.next()
                    for hh in range(4):
                        q4, par = hsl(hh)
                        ps_ = slice(par * 64, par * 64 + 64)
                        S.op("pe", lambda e: e.matmul(pY[:, hh, 0:64], lhsT=rTz[par][:, q4, :], rhs=H0[:, q4, :], start=True, stop=False),
                             reads=[rTz[par], H0], writes=[pY])
                        S.op("pe", lambda e: e.matmul(pY[:, hh, 0:64], lhsT=Arb[:, hh, :], rhs=U[:, hh, :], start=False, stop=False),
                             reads=[Arb, U], writes=[pY])
                        S.op("pe", lambda e: e.matmul(pY[:, hh, 0:64], lhsT=Ark[:, hh, :], rhs=vtok[:, q4, ps_], start=False, stop=True),
                             reads=[Ark, vtok], writes=[pY])
                    o = s - NCTX
                    ysl = yacc[:, o, hg * 256:(hg + 1) * 256].rearrange("p (a b) -> p a b", b=64)
                    if fwd:
                        S.copy("act", yacc, ysl, pY, pY[:, :, 0:64])
                    else:
                        S.op("dve", lambda e: e.tensor_tensor(out=ysl, in0=ysl, in1=pY[:, :, 0:64], op=ALU.add),
                             reads=[yacc, pY], writes=[yacc])
                pH = psC.next()
                for qq in range(2):
                    q4 = hg * 2 + qq
                    S.op("pe", lambda e: e.matmul(pH[:, qq, :], lhsT=btok[:, q4, :], rhs=U[:, qq * 2:qq * 2 + 2, :].rearrange("p a b -> p (a b)"),
                                                  start=True, stop=False), reads=[btok, U], writes=[pH])
                    S.op("pe", lambda e: e.matmul(pH[:, qq, :], lhsT=ktok[:, q4, :], rhs=vtok[:, q4, :], start=False, stop=True),
                         reads=[ktok, vtok], writes=[pH])
                for qq in range(2):
                    q4 = hg * 2 + qq
                    for par in range(2):
                        pp = slice(par * 64, par * 64 + 64)
                        S.op("dve", lambda e: e.tensor_scalar(out=H[pp, q4, :], in0=H[pp, q4, :], scalar1=sc[pp, 3, q4:q4 + 1],
                                                              scalar2=None, op0=ALU.mult), reads=[H, sc], writes=[H])
                        S.op("dve", lambda e: e.scalar_tensor_tensor(out=H[pp, q4, :], in0=pH[pp, qq, par * 64:par * 64 + 64],
                                                                     scalar=sc[pp, 4, q4:q4 + 1], in1=H[pp, q4, :],
                                                                     op0=ALU.mult, op1=ALU.add), reads=[pH, sc, H], writes=[H])

        lim = int(os.environ.get("MK_P2LIM", "0"))
        ctx_f = list(range(NCTX))
        ctx_b = list(range(NCTX - 1, -1, -1))
        own_f = list(range(NCTX, NT))
        own_b = list(range(NT - 1, NCTX - 1, -1))
        if lim:
            ctx_f, ctx_b = ctx_f[-lim:], ctx_b[:0]
        olim = int(os.environ.get("MK_OWNLIM", "0"))
        if olim:
            own_f, own_b = own_f[:olim], own_b[:olim]
        if dbg == "p13":
            ctx_f, ctx_b, own_f, own_b = [], [], [], []
        for s in ctx_f:
            step(s, 0, False)
        for s in own_f:
            step(s, 0, True)
        for s in ctx_b:
            step(s, 1, False)
        for s in own_b:
            step(s, 1, True)

        st_ring = Ring(S, "gnst", 2, [128, 8, 4], F32)
        yn_ring = Ring(S, "yn", 1, [128, 8, 64], F32)
        yb_ring = Ring(S, "yb", 2, [128, 512], BF16)
        for o in range(NOWN):
            y3 = yacc[:, o, :].rearrange("p (h j) -> p h j", j=64)
            st = st_ring.next()
            yn = yn_ring.next()
            S.op("dve", lambda e: e.tensor_reduce(out=st[:, :, 0], in_=y3, axis=AX.X, op=ALU.add), reads=[yacc], writes=[st])
            S.op("dve", lambda e: e.tensor_scalar(out=st[:, :, 0], in0=st[:, :, 0], scalar1=1.0 / 64, scalar2=None, op0=ALU.mult),
                 reads=[st], writes=[st])
            for h in range(8):
                S.op("dve", lambda e: e.tensor_scalar(out=yn[:, h, :], in0=y3[:, h, :], scalar1=st[:, h, 0:1], scalar2=None,
                                                      op0=ALU.subtract), reads=[yacc, st], writes=[yn])
            sqt = sq4_ring.next()
            sq3 = sqt[:].rearrange("p a b -> p (a b)").rearrange("p (h j) -> p h j", j=64)
            S.op("pool", lambda e: e.tensor_tensor(out=sq3, in0=yn[:], in1=yn[:], op=ALU.mult), reads=[yn], writes=[sqt])
            S.op("dve", lambda e: e.tensor_reduce(out=st[:, :, 1], in_=sq3, axis=AX.X, op=ALU.add), reads=[sqt], writes=[st])
            rsqrt(S, st, st[:, :, 1], st, st[:, :, 1], 1.0 / 64, epst, epst[:, 2:3])
            for h in range(8):
                S.op("dve", lambda e: e.tensor_scalar(out=yn[:, h, :], in0=yn[:, h, :], scalar1=st[:, h, 1:2], scalar2=None,
                                                      op0=ALU.mult), reads=[yn, st], writes=[yn])
            ynf = yn[:].rearrange("p h j -> p (h j)")
            S.op("pool", lambda e: e.tensor_tensor(out=ynf, in0=ynf, in1=lnw[:], op=ALU.mult), reads=[yn, lnw], writes=[yn])
            S.op("pool", lambda e: e.tensor_tensor(out=ynf, in0=ynf, in1=lnb[:], op=ALU.add), reads=[yn, lnb], writes=[yn])
            for h in range(8):
                S.op("dve", lambda e: e.scalar_tensor_tensor(out=yn[:, h, :], in0=vtok_own[:, o, h * 64:(h + 1) * 64],
                                                             scalar=bon[:, o, h:h + 1], in1=yn[:, h, :], op0=ALU.mult, op1=ALU.add),
                     reads=[vtok_own, bon, yn], writes=[yn])
            yb = yb_ring.next()
            S.op("dve", lambda e: e.tensor_tensor(out=yb[:], in0=ynf, in1=gate_own[:, o, :], op=ALU.mult),
                 reads=[yn, gate_own], writes=[yb])
            for q4 in range(4):
                S.op("pe", lambda e: e.transpose(out=psT[:, q4, :], in_=yb[:, q4 * 128:(q4 + 1) * 128], identity=idb[:]),
                     reads=[yb, idb], writes=[psT])
            S.copy("act", yT, yT[:, :, o * 128:(o + 1) * 128], psT, psT[:, 0:4, :])
        if dbg in ("p2", "p23"):
            o_ = nc.dram_tensor("d_yacc", [128, NOWN, 512], F32, kind="ExternalOutput").ap()
            S.dma("sp", o_, yacc[:], reads=[yacc])
            o_ = nc.dram_tensor("d_yT", [128, 4, 2048], BF16, kind="ExternalOutput").ap()
            S.dma("sp", o_, yT[:], reads=[yT])
            o_ = nc.dram_tensor("d_H", [2, 128, 4, 64], F32, kind="ExternalOutput").ap()
            for d in range(2):
                S.dma("sp", o_[d], Hs[d][:], reads=[Hs[d]])
        if dbg == "p2":
            S.barrier()
            S.pop()
            return nc
        S.pop()

        oT = S.sbuf("oT", [128, 4, 2048], BF16)
        S.push()
        pm = S.psum("pm", [128, 512])
        mrow = S.sbuf("mrow", [1, 4], F32)
        negM = S.sbuf("negM", [128, 1], F32)
        ones1 = S.sbuf("ones1", [1, 128], F32)
        S.op("pool", lambda e: e.memset(ones1[:], 1.0), writes=[ones1])
        for c in range(2):
            S.op("pe", lambda e: e.matmul(pm[0:1, 0:128], lhsT=qk2max[:, c:c + 1], rhs=idf[:], start=True, stop=True),
                 reads=[qk2max, idf], writes=[pm])
            S.op("dve", lambda e: e.tensor_reduce(out=mrow[:, c:c + 1], in_=pm[0:1, 0:128], axis=AX.X, op=ALU.max), reads=[pm], writes=[mrow])
        S.op("dve", lambda e: e.tensor_scalar(out=mrow[:, 2:3], in0=mrow[:, 0:1], scalar1=-4.0, scalar2=None, op0=ALU.mult),
             reads=[mrow], writes=[mrow])
        S.op("dve", lambda e: e.scalar_tensor_tensor(out=mrow[:, 3:4], in0=mrow[:, 1:2], scalar=-1.0 / 16, in1=mrow[:, 2:3],
                                                     op0=ALU.mult, op1=ALU.add), reads=[mrow], writes=[mrow])
        S.op("pe", lambda e: e.matmul(pm[:, 0:1], lhsT=ones1[:], rhs=mrow[:, 3:4], start=True, stop=True), reads=[ones1, mrow], writes=[pm])
        S.copy("dve", negM, negM[:], pm, pm[:, 0:1])
        lamt = const("lamt", lam_d[:, :, :], [128, 4, 64])
        lam = S.sbuf("lam", [128, 4], F32)
        S.op("dve", lambda e: e.tensor_tensor(out=lamt[:, 0, :], in0=lamt[:, 0, :], in1=lamt[:, 1, :], op=ALU.mult), reads=[lamt], writes=[lamt])
        S.op("dve", lambda e: e.tensor_tensor(out=lamt[:, 2, :], in0=lamt[:, 2, :], in1=lamt[:, 3, :], op=ALU.mult), reads=[lamt], writes=[lamt])
        S.op("dve", lambda e: e.tensor_reduce(out=lam[:, 0:1], in_=lamt[:, 0, :], axis=AX.X, op=ALU.add), reads=[lamt], writes=[lam])
        S.op("dve", lambda e: e.tensor_reduce(out=lam[:, 1:2], in_=lamt[:, 2, :], axis=AX.X, op=ALU.add), reads=[lamt], writes=[lam])
        S.op("act", lambda e: e.activation(out=lam[:, 0:2], in_=lam[:, 0:2], func=AF.Exp), reads=[lam], writes=[lam])
        S.op("dve", lambda e: e.tensor_tensor(out=lam[:, 2:3], in0=lam[:, 1:2], in1=lam[:, 0:1], op=ALU.subtract), reads=[lam], writes=[lam])
        S.op("dve", lambda e: e.tensor_scalar(out=lam[:, 2:3], in0=lam[:, 2:3], scalar1=-LAMBDA_INIT, scalar2=None, op0=ALU.add),
             reads=[lam], writes=[lam])
        subln = const("subln", subln_d[:, :], [128, 128])
        S.op("dve", lambda e: e.tensor_scalar(out=subln[:], in0=subln[:], scalar1=1.0 - LAMBDA_INIT, scalar2=None, op0=ALU.mult),
             reads=[subln], writes=[subln])
        KTh_ring = Ring(S, "KTh", 2, [128, S_LEN], BF16)
        Vh_ring = Ring(S, "Vh", 2, [128, NT, 130], BF16)
        PT_ring2 = Ring(S, "PTa", 3, [128, 2, 512], BF16)
        psS = Ring(S, "psS", 2, [128, 2, 512], F32, psum=True)
        accA = S.psum("accA", [128, 512])
        accB = S.psum("accB", [128, 512])
        accC = S.psum("accC", [128, 512])
        acc_slots = [(accA, 0), (accA, 1), (accA, 2), (accB, 0), (accB, 1), (accB, 2), (accC, 0), (accC, 1)]
        o1_ring = Ring(S, "o1", 2, [128, 130], F32)
        o2_ring = Ring(S, "o2", 2, [128, 130], F32)
        ob_ring = Ring(S, "ob", 2, [128, 128], BF16)
        for hd in range(4):
            KTh = KTh_ring.next()
            Vh = Vh_ring.next()
            S.dma("sp", KTh[:], KT.t[hd], reads=[KT], writes=[KTh])
            for g8 in range(8):
                S.dma("sp", Vh[:, g8 * 8:(g8 + 1) * 8, 0:128],
                      VD.t[g8 * 8:(g8 + 1) * 8, :, hd * 128:(hd + 1) * 128].rearrange("s p c -> p s c"), reads=[VD], writes=[Vh])
            S.op("pool", lambda e: e.memset(Vh[:, :, 128:130], 1.0), writes=[Vh])
            for qs in range(4):
                for kc in range(NT):
                    ps = psS.next()
                    for br in range(2):
                        bp = slice(br * 64, br * 64 + 64)
                        S.op("pe", lambda e: e.matmul(ps[:, br, :], lhsT=KTh[bp, kc * 128:(kc + 1) * 128],
                                                      rhs=QT[bp, hd, qs * 512:(qs + 1) * 512], start=True, stop=True),
                             reads=[KTh, QT], writes=[ps])
                    PT = PT_ring2.next()
                    S.op("act", lambda e: e.activation(out=PT[:], in_=ps[:], func=AF.Exp, bias=negM[:, 0:1]),
                         reads=[ps, negM], writes=[PT])
                    for br in range(2):
                        for qb in range(4):
                            at, ai = acc_slots[br * 4 + qb]
                            S.op("pe", lambda e: e.matmul(at[:, ai * 130:ai * 130 + 129], lhsT=PT[:, br, qb * 128:(qb + 1) * 128],
                                                          rhs=Vh[:, kc, 0:129], start=(kc == 0 and ai == 0), stop=(kc == NT - 1)),
                                 reads=[PT, Vh], writes=[at])
                for qb in range(4):
                    o1, o2 = o1_ring.next(), o2_ring.next()
                    a1, i1 = acc_slots[qb]
                    a2_, i2 = acc_slots[4 + qb]
                    S.copy("act", o1, o1[:, 0:129], a1, a1[:, i1 * 130:i1 * 130 + 129])
                    S.copy("dve", o2, o2[:, 0:129], a2_, a2_[:, i2 * 130:i2 * 130 + 129])
                    S.op("dve", lambda e: e.reciprocal(out=o1[:, 129:130], in_=o1[:, 128:129]), reads=[o1], writes=[o1])
                    S.op("dve", lambda e: e.reciprocal(out=o2[:, 129:130], in_=o2[:, 128:129]), reads=[o2], writes=[o2])
                    S.op("dve", lambda e: e.tensor_tensor(out=o2[:, 129:130], in0=o2[:, 129:130], in1=lam[:, 2:3], op=ALU.mult),
                         reads=[o2, lam], writes=[o2])
                    S.op("dve", lambda e: e.tensor_scalar(out=o1[:, 0:128], in0=o1[:, 0:128], scalar1=o1[:, 129:130], scalar2=None,
                                                          op0=ALU.mult), reads=[o1], writes=[o1])
                    S.op("dve", lambda e: e.scalar_tensor_tensor(out=o1[:, 0:128], in0=o2[:, 0:128], scalar=o2[:, 129:130],
                                                                 in1=o1[:, 0:128], op0=ALU.mult, op1=ALU.add),
                         reads=[o1, o2], writes=[o1])
                    S.op("pool", lambda e: e.memset(o2[:, 128:130], 0.0), writes=[o2])
                    S.op("act", lambda e: e.activation(out=o2[:, 0:128], in_=o1[:, 0:128], func=AF.Square, accum_out=o2[:, 128:129]),
                         reads=[o1], writes=[o2])
                    rsqrt(S, o2, o2[:, 128:129], o2, o2[:, 128:129], 1.0 / 128, epst, epst[:, 1:2])
                    ob = ob_ring.next()
                    S.op("dve", lambda e: e.scalar_tensor_tensor(out=ob[:], in0=o1[:, 0:128], scalar=o2[:, 128:129], in1=subln[:],
                                                                 op0=ALU.mult, op1=ALU.mult), reads=[o1, o2, subln], writes=[ob])
                    ptp = psS.next()
                    ptb = ptp[:].rearrange("p a b -> p (a b)")
                    S.op("pe", lambda e: e.matmul(ptb[:, 0:128], lhsT=ob[:], rhs=idb[:], start=True, stop=True),
                         reads=[ob, idb], writes=[ptp])
                    tok0 = qs * 512 + qb * 128
                    S.copy("act", oT, oT[:, hd, tok0:tok0 + 128], ptp, ptb[:, 0:128])
        if dbg in ("p3", "p23", "p13"):
            o_ = nc.dram_tensor("d_oT", [128, 4, 2048], BF16, kind="ExternalOutput").ap()
            S.dma("sp", o_, oT[:], reads=[oT])
            S.barrier()
            S.pop()
            return nc
        S.pop()

        _emit_p4(S, nc, locals())
        S.barrier()
    return nc


def _tab(v, ncol):
    return np.ascontiguousarray(np.asarray(v, np.float32).reshape(ncol, 128).T)


def prep_core_inputs(inputs, c):
    b, qi = c // 4, c % 4
    x = np.asarray(inputs["x"], np.float32)
    own0 = 2048 * qi
    idx = np.concatenate([np.arange(own0 + 2048, S_LEN), np.arange(0, own0), np.arange(own0, own0 + 2048)])
    d = {}
    d["xs"] = np.ascontiguousarray(x[b, idx])
    d["p_own"] = np.ascontiguousarray(np.asarray(inputs["p"], np.float32)[0, b, own0:own0 + 2048])
    d["w_in"] = np.asarray(inputs["w_in"], np.float32)[0]
    d["w_unused"] = np.zeros((1, 1), np.float32)
    d["gmix"] = _tab(inputs["norm_mix"][0], 8)
    d["gffn"] = _tab(inputs["norm_ffn"][0], 8)
    d["gple"] = _tab(inputs["norm_ple"][0], 8)
    d["gfin"] = _tab(inputs["norm_final"], 8)
    d["mup"] = _tab(inputs["shift_mu_prev"][0], 15)
    d["mun"] = _tab(inputs["shift_mu_next"][0], 15)
    d["w0t"] = _tab(np.asarray(inputs["rwkv_w0"], np.float32)[0].reshape(-1), 8)
    d["a0t"] = _tab(np.asarray(inputs["rwkv_a0"], np.float32)[0].reshape(-1), 8)
    d["w2t"] = np.ascontiguousarray(np.asarray(inputs["rwkv_w2"], np.float32)[0].reshape(128, 512))
    d["a2t"] = np.ascontiguousarray(np.asarray(inputs["rwkv_a2"], np.float32)[0].reshape(128, 512))
    d["g2"] = np.asarray(inputs["rwkv_g2"], np.float32)[0]
    d["kkt"] = _tab(inputs["rwkv_k_k"][0], 4)
    d["kat"] = _tab(inputs["rwkv_k_a"][0], 4)
    d["rkt"] = _tab(np.asarray(inputs["rwkv_r_k"], np.float32)[0].reshape(-1), 4)
    d["lnw_b"] = np.ascontiguousarray(np.broadcast_to(np.asarray(inputs["rwkv_ln_w"], np.float32)[0], (128, 512)))
    d["lnb_b"] = np.ascontiguousarray(np.broadcast_to(np.asarray(inputs["rwkv_ln_b"], np.float32)[0], (128, 512)))
    d["rwkv_w_o"] = np.asarray(inputs["rwkv_w_o"], np.float32)[0]
    lam = np.stack([np.asarray(inputs[k], np.float32)[0] for k in ("da_lq1", "da_lk1", "da_lq2", "da_lk2")])
    d["lam_b"] = np.ascontiguousarray(np.broadcast_to(lam, (128, 4, 64)))
    d["subln_b"] = np.ascontiguousarray(np.broadcast_to(np.asarray(inputs["da_subln_w"], np.float32)[0], (128, 128)))
    d["da_w_o"] = np.asarray(inputs["da_w_o"], np.float32)[0]
    d["w_out"] = np.asarray(inputs["w_out"], np.float32)[0]
    d["w_ff1"] = np.asarray(inputs["w_ff1"], np.float32)[0]
    d["w_ff2"] = np.asarray(inputs["w_ff2"], np.float32)[0]
    d["w_ple_gate"] = np.asarray(inputs["w_ple_gate"], np.float32)[0]
    d["w_ple_proj"] = np.asarray(inputs["w_ple_proj"], np.float32)[0]
    inv_freq = (np.float32(500000.0) ** (-np.arange(0, 16, 2, dtype=np.float32) / np.float32(16))).astype(np.float32)
    ang = idx.astype(np.float32)[:, None] * inv_freq[None, :]
    d["cos_t"] = np.ascontiguousarray(np.tile(np.cos(ang).astype(np.float32), (1, 8)))
    d["sin_t"] = np.ascontiguousarray(np.tile(np.sin(ang).astype(np.float32), (1, 8)))
    first = idx[0::128]
    last = idx[127::128]
    d["eprev"] = np.ascontiguousarray(np.broadcast_to((first != 0).astype(np.float32), (128, NT)))
    d["enext"] = np.ascontiguousarray(np.broadcast_to((last != S_LEN - 1).astype(np.float32), (128, NT)))
    nA = 16 * (3 - qi)
    j = np.arange(NCTX)
    d["keepf"] = np.ascontiguousarray(np.broadcast_to((j >= nA).astype(np.float32), (128, NCTX)))
    d["keepb"] = np.ascontiguousarray(np.broadcast_to((j < nA).astype(np.float32), (128, NCTX)))
    d["ident"] = np.eye(128, dtype=np.float32)
    r = np.arange(128)
    sl = (r[:, None] > r[None, :]).astype(np.float32)
    il = (r[:, None] >= r[None, :]).astype(np.float32)
    d["mask_sl"] = np.ascontiguousarray(np.tile(sl, (1, 4)))
    d["mask_su"] = np.ascontiguousarray(np.tile(sl.T, (1, 4)))
    d["mask_il"] = np.ascontiguousarray(np.tile(il, (1, 4)))
    d["mask_iu"] = np.ascontiguousarray(np.tile(il.T, (1, 4)))
    bo = np.zeros((128, 128), np.float32)
    bo[:64, :64] = 1
    bo[64:, 64:] = 1
    d["bones"] = bo
    hi = np.zeros((128, 2), np.float32)
    hi[:64, 0] = 1
    hi[64:, 1] = 1
    d["hind"] = hi
    return d


_NC_CACHE = {}


def kernel(**inputs):
    if "nc" not in _NC_CACHE:
        _NC_CACHE["nc"] = build_program()
    nc = _NC_CACHE["nc"]
    in_maps = [prep_core_inputs(inputs, c) for c in range(8)]
    res = run_bass_kernel_spmd(nc, in_maps, core_ids=list(range(8)))
    out = np.zeros((2, S_LEN, 1024), np.float32)
    for c in range(8):
        b, qi = c // 4, c % 4
        out[b, 2048 * qi:2048 * (qi + 1)] = res.results[c]["out"]
    return out
```

```python
import contextlib
import math
import os
import numpy as np
import concourse.bass as bass
import concourse.mybir as mybir
from concourse.bass_utils import run_bass_kernel_spmd

F32 = mybir.dt.float32
BF16 = mybir.dt.bfloat16
ALU = mybir.AluOpType
AF = mybir.ActivationFunctionType
AX = mybir.AxisListType

S_LEN = 8192
NT = 64
NCTX = 48
NOWN = 16
CDEC = 0.6065306597126334
LAMBDA_INIT = 0.8 - 0.6 * math.exp(0.0)


class _Eng:
    def __init__(self, name, eng, sem):
        self.name, self.eng, self.sem = name, eng, sem
        self.count = 0
        self.seen = {}


class T:
    def __init__(self, t, name=""):
        self.t, self.name = t, name
        self.w = None
        self.r = {}

    def __getitem__(self, idx):
        return self.t[idx]


class Sched:
    N_DMA_SEMS = 48

    def __init__(self, nc, stack):
        self.nc, self.stack = nc, stack
        self.E = {}
        for name, e in (("pe", nc.tensor), ("act", nc.scalar), ("dve", nc.vector),
                        ("pool", nc.gpsimd), ("sp", nc.sync)):
            self.E[name] = _Eng(name, e, stack.enter_context(nc.semaphore("sem_" + name)))
        self.dsems = [[stack.enter_context(nc.semaphore("dsem%d" % i)), 0] for i in range(self.N_DMA_SEMS)]
        self.dnext = 0
        self.scopes = []
        self.rr = 0

    def sbuf(self, name, shape, dt):
        st = self.scopes[-1] if self.scopes else self.stack
        return T(st.enter_context(self.nc.sbuf_tensor("s_" + name, list(shape), dt)), name)

    def psum(self, name, shape, dt=F32):
        st = self.scopes[-1] if self.scopes else self.stack
        return T(st.enter_context(self.nc.psum_tensor("ps_" + name, list(shape), dt)), name)

    def dram(self, name, shape, dt):
        return T(self.nc.dram_tensor("dr_" + name, list(shape), dt, kind="Internal").ap(), name)

    def _need(self, E, ticket, raw):
        if ticket is None:
            return
        sem, val, src = ticket
        if src is E and (E.name == "pe" or not raw):
            return
        key = id(sem)
        if E.seen.get(key, 0) >= val:
            return
        E.eng.wait_ge(sem, val)
        E.seen[key] = val

    def _deps(self, E, reads, writes):
        for t in reads:
            self._need(E, t.w, True)
        for t in writes:
            self._need(E, t.w, False)
            for tk in t.r.values():
                self._need(E, tk, False)

    def _record(self, ticket, key, reads, writes):
        for t in reads:
            t.r[key] = ticket
        for t in writes:
            t.w = ticket
            t.r = {}

    def op(self, eng, fn, reads=(), writes=()):
        E = self.E[eng]
        self._deps(E, reads, writes)
        ins = fn(E.eng)
        E.count += 1
        ins.then_inc(E.sem, 1)
        self._record((E.sem, E.count, E), eng, reads, writes)

    def dma(self, q, out, in_, reads=(), writes=(), **kw):
        E = self.E[q]
        self._deps(E, reads, writes)
        slot = self.dsems[self.dnext]
        self.dnext = (self.dnext + 1) % len(self.dsems)
        sem, tot = slot
        if tot > 0:
            self._need(E, (sem, tot, None), False)
        E.eng.dma_start(out=out, in_=in_, **kw).then_inc(sem, 16)
        slot[1] = tot + 16
        self._record((sem, tot + 16, None), ("dma", id(sem)), reads, writes)

    def barrier(self):
        for E in self.E.values():
            for sem, tot in self.dsems:
                if tot > 0:
                    self._need(E, (sem, tot, None), False)
            for o in self.E.values():
                if o is not E and o.count > 0:
                    self._need(E, (o.sem, o.count, o), False)

    def push(self):
        st = contextlib.ExitStack()
        st.__enter__()
        self.scopes.append(st)

    def pop(self):
        self.barrier()
        self.scopes.pop().__exit__(None, None, None)

    def copy(self, eng, out_t, out_ap, in_t, in_ap, scale=None):
        if eng == "rr":
            eng = ("act", "dve")[self.rr % 2]
            self.rr += 1
        if eng == "act":
            if scale is None:
                self.op("act", lambda e: e.activation(out=out_ap, in_=in_ap, func=AF.Copy),
                        reads=[in_t], writes=[out_t])
            else:
                self.op("act", lambda e: e.activation(out=out_ap, in_=in_ap, func=AF.Copy, scale=scale),
                        reads=[in_t], writes=[out_t])
        else:
            if scale is None:
                self.op(eng, lambda e: e.tensor_copy(out=out_ap, in_=in_ap), reads=[in_t], writes=[out_t])
            else:
                self.op(eng, lambda e: e.tensor_scalar(out=out_ap, in0=in_ap, scalar1=scale, scalar2=None,
                                                       op0=ALU.mult), reads=[in_t], writes=[out_t])


def rsqrt(S, out_t, out_ap, in_t, in_ap, scale, bias_t, bias_ap):
    S.op("act", lambda e: e.activation(out=out_ap, in_=in_ap, func=AF.Sqrt, bias=bias_ap, scale=scale),
         reads=[in_t, bias_t], writes=[out_t])
    S.op("dve", lambda e: e.reciprocal(out=out_ap, in_=out_ap), reads=[out_t], writes=[out_t])


class Ring:
    def __init__(self, S, name, n, shape, dt, psum=False):
        self.tiles = [(S.psum if psum else S.sbuf)("%s%d" % (name, i), shape, dt) for i in range(n)]
        self.i = 0

    def next(self):
        t = self.tiles[self.i % len(self.tiles)]
        self.i += 1
        return t


def _emit_p4(S, nc, L):
    yT, oT, idf, idb, gfin, epst = L["yT"], L["oT"], L["idf"], L["idb"], L["gfin"], L["epst"]
    xs_d, p_d, out_d = L["xs_d"], L["p_d"], L["out_d"]
    Wg_s, Wor_s, Wod_s, Wout_s, W1_s, W2_s, Wpg_s, Wpp_s = (L[k] for k in ("Wg_s", "Wor_s", "Wod_s", "Wout_s", "W1_s", "W2_s", "Wpg_s", "Wpp_s"))
    S.push()
    onesb = S.sbuf("onesb", [128, 128], BF16)
    S.op("pool", lambda e: e.memset(onesb[:], 1.0), writes=[onesb])
    wp_ring = Ring(S, "wp", 4, [128, 32, 128], BF16)
    xT = S.sbuf("xT", [128, 8, 512], F32)
    hT4 = S.sbuf("hT4", [128, 8, 512], BF16)
    sqT = S.sbuf("sqT", [128, 8, 512], BF16)
    rst = S.sbuf("rst", [128, 512], F32)
    gat = S.sbuf("gat", [128, 16, 512], BF16)
    mer = S.sbuf("mer", [128, 8, 512], BF16)
    tmpA = Ring(S, "tmpA", 2, [128, 512], F32)
    hid = S.sbuf("hid", [128, 32, 512], BF16)
    sgp = S.sbuf("sgp", [128, 8, 512], BF16)
    pTt = S.sbuf("pTt", [128, 2, 512], BF16)
    xin_ring = Ring(S, "xin", 2, [128, 1024], F32)
    pin_ring = Ring(S, "pin", 2, [128, 256], F32)
    pinb_ring = Ring(S, "pinb", 2, [128, 256], BF16)
    outst_ring = Ring(S, "outst", 2, [128, 1024], F32)
    pM = Ring(S, "pM", 4, [128, 512], F32, psum=True)
    pTr = Ring(S, "pTr", 2, [128, 4, 128], F32, psum=True)
    pTb = S.psum("pTb", [128, 8, 128], BF16)
    pSS = S.psum("pSS", [128, 512])

    def load_w(scr, oc, KC):
        wp = wp_ring.next()
        S.dma("sp", wp[:, 0:KC, :], scr.t[oc], reads=[scr], writes=[wp])
        return wp

    def rms_h(gain_unused=None):
        for kc in range(8):
            S.op("act", lambda e: e.activation(out=sqT[:, kc, :], in_=xT[:, kc, :], func=AF.Square), reads=[xT], writes=[sqT])
        for kc in range(8):
            S.op("pe", lambda e: e.matmul(pSS[:], lhsT=onesb[:], rhs=sqT[:, kc, :], start=(kc == 0), stop=(kc == 7)),
                 reads=[onesb, sqT], writes=[pSS])
        rsqrt(S, rst, rst[:], pSS, pSS[:], 1.0 / 1024, epst, epst[:, 0:1])

    def apply_h():
        for kc in range(8):
            eng = ("dve", "pool")[kc % 2]
            S.op(eng, lambda e: e.tensor_tensor(out=hT4[:, kc, :], in0=xT[:, kc, :], in1=rst[:], op=ALU.mult),
                 reads=[xT, rst], writes=[hT4])

    def mm(scr, oc, KC, rhs_t, rhs_fn):
        wp = load_w(scr, oc, KC)
        pm_ = pM.next()
        for kc in range(KC):
            S.op("pe", lambda e: e.matmul(pm_[:], lhsT=wp[:, kc, :], rhs=rhs_fn(kc), start=(kc == 0), stop=(kc == KC - 1)),
                 reads=[wp, rhs_t], writes=[pm_])
        return pm_

    for blk in range(4):
        t0 = blk * 512
        for j in range(4):
            xin = xin_ring.next()
            S.dma("sp", xin[:], xs_d[(NCTX + blk * 4 + j) * 128:(NCTX + blk * 4 + j + 1) * 128, :], writes=[xin])
            for half in range(2):
                ptr = pTr.next()
                for k4 in range(4):
                    kc = half * 4 + k4
                    S.op("pe", lambda e: e.matmul(ptr[:, k4, :], lhsT=xin[:, kc * 128:(kc + 1) * 128], rhs=idf[:], start=True, stop=True),
                         reads=[xin, idf], writes=[ptr])
                S.copy("rr", xT, xT[:, half * 4:half * 4 + 4, j * 128:(j + 1) * 128], ptr, ptr[:])
            pin = pin_ring.next()
            pinb = pinb_ring.next()
            S.dma("sp", pin[:], p_d[t0 + j * 128:t0 + (j + 1) * 128, :], writes=[pin])
            S.copy("pool", pinb, pinb[:], pin, pin[:])
            for k2 in range(2):
                S.op("pe", lambda e: e.transpose(out=pTb[:, k2, :], in_=pinb[:, k2 * 128:(k2 + 1) * 128], identity=idb[:]),
                     reads=[pinb, idb], writes=[pTb])
            S.copy("dve", pTt, pTt[:, :, j * 128:(j + 1) * 128], pTb, pTb[:, 0:2, :])
        rms_h()
        apply_h()
        for oc in range(16):
            pm_ = mm(Wg_s, oc, 8, hT4, lambda kc: hT4[:, kc, :])
            S.op("act", lambda e: e.activation(out=gat[:, oc, :], in_=pm_[:], func=AF.Sigmoid), reads=[pm_], writes=[gat])
        for oc in range(8):
            pa_ = mm(Wor_s, oc, 4, yT, lambda kc: yT[:, kc, t0:t0 + 512])
            ta = tmpA.next()
            S.op("dve", lambda e: e.tensor_tensor(out=ta[:], in0=pa_[:], in1=gat[:, oc, :], op=ALU.mult), reads=[pa_, gat], writes=[ta])
            pb_ = mm(Wod_s, oc, 4, oT, lambda kc: oT[:, kc, t0:t0 + 512])
            tb = tmpA.next()
            S.op("dve", lambda e: e.tensor_tensor(out=tb[:], in0=pb_[:], in1=gat[:, 8 + oc, :], op=ALU.mult), reads=[pb_, gat], writes=[tb])
            S.op("pool", lambda e: e.tensor_tensor(out=mer[:, oc, :], in0=ta[:], in1=tb[:], op=ALU.add), reads=[ta, tb], writes=[mer])
        for oc in range(8):
            pm_ = mm(Wout_s, oc, 8, mer, lambda kc: mer[:, kc, :])
            S.op("dve", lambda e: e.tensor_tensor(out=xT[:, oc, :], in0=xT[:, oc, :], in1=pm_[:], op=ALU.add), reads=[xT, pm_], writes=[xT])
        rms_h()
        apply_h()
        for fc in range(32):
            pm_ = mm(W1_s, fc, 8, hT4, lambda kc: hT4[:, kc, :])
            tr = tmpA.next()
            S.op("act", lambda e: e.activation(out=tr[:], in_=pm_[:], func=AF.Relu), reads=[pm_], writes=[tr])
            eng = ("pool", "dve")[fc % 2]
            S.op(eng, lambda e: e.tensor_tensor(out=hid[:, fc, :], in0=tr[:], in1=tr[:], op=ALU.mult), reads=[tr], writes=[hid])
        for oc in range(8):
            pm_ = mm(W2_s, oc, 32, hid, lambda kc: hid[:, kc, :])
            S.op("dve", lambda e: e.tensor_tensor(out=xT[:, oc, :], in0=xT[:, oc, :], in1=pm_[:], op=ALU.add), reads=[xT, pm_], writes=[xT])
        rms_h()
        apply_h()
        for oc in range(8):
            pm_ = mm(Wpg_s, oc, 8, hT4, lambda kc: hT4[:, kc, :])
            S.op("act", lambda e: e.activation(out=sgp[:, oc, :], in_=pm_[:], func=AF.Sigmoid), reads=[pm_], writes=[sgp])
        for oc in range(8):
            pm_ = mm(Wpp_s, oc, 2, pTt, lambda kc: pTt[:, kc, :])
            ta = tmpA.next()
            S.op("dve", lambda e: e.tensor_tensor(out=ta[:], in0=pm_[:], in1=sgp[:, oc, :], op=ALU.mult), reads=[pm_, sgp], writes=[ta])
            S.op("pool", lambda e: e.tensor_tensor(out=xT[:, oc, :], in0=xT[:, oc, :], in1=ta[:], op=ALU.add), reads=[xT, ta], writes=[xT])
        rms_h()
        for kc in range(8):
            S.op("dve", lambda e: e.scalar_tensor_tensor(out=xT[:, kc, :], in0=xT[:, kc, :], scalar=gfin[:, kc:kc + 1], in1=rst[:],
                                                         op0=ALU.mult, op1=ALU.mult), reads=[xT, gfin, rst], writes=[xT])
        for j in range(4):
            ost = outst_ring.next()
            for half in range(2):
                ptr = pTr.next()
                for k4 in range(4):
                    kc = half * 4 + k4
                    S.op("pe", lambda e: e.matmul(ptr[:, k4, :], lhsT=xT[:, kc, j * 128:(j + 1) * 128], rhs=idf[:], start=True, stop=True),
                         reads=[xT, idf], writes=[ptr])
                S.copy("rr", ost, ost[:, half * 512:(half + 1) * 512], ptr, ptr[:].rearrange("p a b -> p (a b)"))
            S.dma("sp", out_d[t0 + j * 128:t0 + (j + 1) * 128, :], ost[:], reads=[ost])
    S.pop()


def build_program(dbg=None):
    nc = bass.Bass("TRN2", target_bir_lowering=False)

    def inp(name, shape, dt=F32):
        return nc.dram_tensor(name, list(shape), dt, kind="ExternalInput").ap()

    xs_d = inp("xs", [S_LEN, 1024])
    p_d = inp("p_own", [2048, 256])
    w_in_d = inp("w_in", [1024, 5504])
    wk_sw_d = inp("w_unused", [1, 1])
    gmix_d = inp("gmix", [128, 8])
    gffn_d = inp("gffn", [128, 8])
    gple_d = inp("gple", [128, 8])
    gfin_d = inp("gfin", [128, 8])
    mup_d = inp("mup", [128, 15])
    mun_d = inp("mun", [128, 15])
    w0_d = inp("w0t", [128, 8])
    a0_d = inp("a0t", [128, 8])
    w2_d = inp("w2t", [128, 512])
    a2_d = inp("a2t", [128, 512])
    g2_d = inp("g2", [128, 512])
    kk_d = inp("kkt", [128, 4])
    ka_d = inp("kat", [128, 4])
    rk_d = inp("rkt", [128, 4])
    lnw_d = inp("lnw_b", [128, 512])
    lnb_d = inp("lnb_b", [128, 512])
    wo_r_d = inp("rwkv_w_o", [512, 1024]) if dbg in (None, "p0") else None
    lam_d = inp("lam_b", [128, 4, 64])
    subln_d = inp("subln_b", [128, 128])
    wo_d_d = inp("da_w_o", [512, 1024]) if dbg in (None, "p0") else None
    wout_d = inp("w_out", [1024, 1024]) if dbg in (None, "p0") else None
    wff1_d = inp("w_ff1", [1024, 4096]) if dbg in (None, "p0") else None
    wff2_d = inp("w_ff2", [4096, 1024]) if dbg in (None, "p0") else None
    wpg_d = inp("w_ple_gate", [1024, 1024]) if dbg in (None, "p0") else None
    wpp_d = inp("w_ple_proj", [256, 1024]) if dbg in (None, "p0") else None
    cos_d = inp("cos_t", [S_LEN, 64])
    sin_d = inp("sin_t", [S_LEN, 64])
    eprev_d = inp("eprev", [128, NT])
    enext_d = inp("enext", [128, NT])
    keepf_d = inp("keepf", [128, NCTX])
    keepb_d = inp("keepb", [128, NCTX])
    ident_d = inp("ident", [128, 128])
    msl_d = inp("mask_sl", [128, 512])
    msu_d = inp("mask_su", [128, 512])
    mil_d = inp("mask_il", [128, 512])
    miu_d = inp("mask_iu", [128, 512])
    bones_d = inp("bones", [128, 128])
    hind_d = inp("hind", [128, 2])
    out_d = nc.dram_tensor("out", [2048, 1024], F32, kind="ExternalOutput").ap()
    dbg_outs = {}

    with contextlib.ExitStack() as stack:
        S = Sched(nc, stack)
        UT = S.dram("UT", [15, 128, S_LEN + 2], F32)
        KT = S.dram("KT", [4, 128, S_LEN], BF16)
        VD = S.dram("VD", [NT, 128, 512], BF16)

        def wscr(name, K, N):
            return S.dram(name, [N // 128, 128, K // 128, 128], BF16)
        Wg_s = wscr("Wg_s", 1024, 2048)
        Wor_s = wscr("Wor_s", 512, 1024)
        Wod_s = wscr("Wod_s", 512, 1024)
        Wout_s = wscr("Wout_s", 1024, 1024)
        W1_s = wscr("W1_s", 1024, 4096)
        W2_s = wscr("W2_s", 4096, 1024)
        Wpg_s = wscr("Wpg_s", 1024, 1024)
        Wpp_s = wscr("Wpp_s", 256, 1024)

        def const(name, src, shape, dt=F32, q="sp"):
            t = S.sbuf(name, shape, dt)
            S.dma(q, t[:], src, writes=[t])
            return t
        idf = const("idf", ident_d[:, :], [128, 128])
        idb = S.sbuf("idb", [128, 128], BF16)
        S.copy("dve", idb, idb[:], idf, idf[:])
        gmix = const("gmix", gmix_d[:, :], [128, 8])
        gffn = const("gffn", gffn_d[:, :], [128, 8])
        gple = const("gple", gple_d[:, :], [128, 8])
        gfin = const("gfin", gfin_d[:, :], [128, 8])
        QT = S.sbuf("QT", [128, 4, 2048], BF16)
        yT = S.sbuf("yT", [128, 4, 2048], BF16)
        epst = S.sbuf("epst", [128, 4], F32)
        for i_, v_ in enumerate((1e-6, 1e-5, 64e-5, 0.0)):
            S.op("pool", lambda e: e.memset(epst[:, i_:i_ + 1], v_), writes=[epst])
        qk2max = S.sbuf("qk2max", [128, 2], F32)
        S.op("pool", lambda e: e.memset(qk2max[:], 0.0), writes=[qk2max])

        S.push()
        wst_ring = Ring(S, "p0st", 2, [128, 4096], F32)
        wbf_ring = Ring(S, "p0bf", 2, [128, 4096], BF16)

        def convert(src_ap_fn, K, N, dst, gain):
            for kc in range(K // 128):
                st = wst_ring.next()
                bf = wbf_ring.next()
                S.dma("sp", st[:, 0:N], src_ap_fn(kc), writes=[st])
                if gain is None:
                    S.op("pool", lambda e: e.tensor_copy(out=bf[:, 0:N], in_=st[:, 0:N]), reads=[st], writes=[bf])
                else:
                    S.op("pool", lambda e: e.tensor_scalar(out=bf[:, 0:N], in0=st[:, 0:N],
                                                           scalar1=gain[:, kc:kc + 1], scalar2=None, op0=ALU.mult),
                         reads=[st, gain], writes=[bf])
                for o8 in range(0, N // 128, 8):
                    S.dma("sp", dst.t[o8:o8 + 8, :, kc, :].rearrange("o p c -> p o c"),
                          bf[:, o8 * 128:(o8 + 8) * 128].rearrange("p (o c) -> p o c", c=128), reads=[bf])

        if dbg in (None, "p0"):
          convert(lambda kc: w_in_d[kc * 128:(kc + 1) * 128, 3456:5504], 1024, 2048, Wg_s, gmix)
          convert(lambda kc: wo_r_d[kc * 128:(kc + 1) * 128, :], 512, 1024, Wor_s, None)
          convert(lambda kc: wo_d_d[kc * 128:(kc + 1) * 128, :], 512, 1024, Wod_s, None)
          convert(lambda kc: wout_d[kc * 128:(kc + 1) * 128, :], 1024, 1024, Wout_s, None)
          convert(lambda kc: wff1_d[kc * 128:(kc + 1) * 128, :], 1024, 4096, W1_s, gffn)
          convert(lambda kc: wff2_d[kc * 128:(kc + 1) * 128, :], 4096, 1024, W2_s, None)
          convert(lambda kc: wpg_d[kc * 128:(kc + 1) * 128, :], 1024, 1024, Wpg_s, gple)
          convert(lambda kc: wpp_d[kc * 128:(kc + 1) * 128, :], 256, 1024, Wpp_s, None)
        S.pop()

        if dbg == "p0":
            S.barrier()
            return nc
        if dbg == "p4":
            oT = S.sbuf("oT", [128, 4, 2048], BF16)
            S.op("pool", lambda e: e.memset(yT[:], 0.0), writes=[yT])
            S.op("pool", lambda e: e.memset(oT[:], 0.0), writes=[oT])
            _emit_p4(S, nc, locals())
            S.barrier()
            return nc
        S.push()
        NW = 3456
        Wb = S.sbuf("Wb", [128, 8, NW], BF16)
        wst1 = Ring(S, "p1wst", 2, [128, NW], F32)
        for kc in range(8):
            st = wst1.next()
            S.dma("sp", st[:], w_in_d[kc * 128:(kc + 1) * 128, 0:NW], writes=[st])
            S.op("pool", lambda e: e.tensor_scalar(out=Wb[:, kc, :], in0=st[:], scalar1=gmix[:, kc:kc + 1],
                                                   scalar2=None, op0=ALU.mult), reads=[st, gmix], writes=[Wb])
        x_ring = Ring(S, "xt", 3, [128, 1024], F32)
        junk_ring = Ring(S, "junk", 2, [128, 1024], BF16)
        xn_ring = Ring(S, "xn", 2, [128, 1024], BF16)
        ss_ring = Ring(S, "ss", 4, [128, 2], F32)
        hT_ring = Ring(S, "hT", 2, [128, 8, 512], BF16)
        tp_ring = Ring(S, "tp", 2, [128, 8, 128], BF16, psum=True)
        pu_ring = Ring(S, "pu", 3, [128, 512], F32, psum=True)
        pd_ring = Ring(S, "pd", 2, [128, 512], F32, psum=True)
        ust_ring = Ring(S, "ust", 3, [128, 512], F32)
        kf_ring = Ring(S, "kf", 2, [128, 8, 64], F32)
        kb_ring = Ring(S, "kb", 2, [128, 8, 64], BF16)
        cs_ring = Ring(S, "cs", 2, [128, 2, 8, 8], F32)
        rt_ring = Ring(S, "rt", 2, [128, 4, 8, 8], F32)
        sq_ring = Ring(S, "sq", 2, [128, 512], F32)
        red_ring = Ring(S, "red", 2, [128, 10], F32)
        kts_ring = Ring(S, "kts", 2, [128, 4, 128], BF16)
        vb_ring = Ring(S, "vb", 2, [128, 512], BF16)

        def rope_and_T(pd, cs, scale, is_q, s, stat_col):
            kf = kf_ring.next()
            kb = kb_ring.next()
            S.copy("act", kf, kf[:].rearrange("p a d -> p (a d)"), pd, pd[:], scale=scale)
            sq = sq_ring.next()
            red = red_ring.next()
            S.op("pool", lambda e: e.tensor_tensor(out=sq[:], in0=kf[:].rearrange("p a d -> p (a d)"),
                                                   in1=kf[:].rearrange("p a d -> p (a d)"), op=ALU.mult),
                 reads=[kf], writes=[sq])
            S.op("dve", lambda e: e.tensor_reduce(out=red[:, 0:8], in_=sq[:].rearrange("p (a d) -> p a d", d=64),
                                                  axis=AX.X, op=ALU.add), reads=[sq], writes=[red])
            S.op("dve", lambda e: e.tensor_reduce(out=red[:, 8:9], in_=red[:, 0:8], axis=AX.X, op=ALU.max),
                 reads=[red], writes=[red])
            S.op("dve", lambda e: e.tensor_tensor(out=qk2max[:, stat_col:stat_col + 1],
                                                  in0=qk2max[:, stat_col:stat_col + 1], in1=red[:, 8:9], op=ALU.max),
                 reads=[red, qk2max], writes=[qk2max])
            S.copy("act", kb, kb[:], kf, kf[:])
            rt = rt_ring.next()
            x1 = kf[:, :, 0:8]
            x2 = kf[:, :, 8:16]
            c_ = cs[:, 0, :, :]
            s_ = cs[:, 1, :, :]
            S.op("dve", lambda e: e.tensor_tensor(out=rt[:, 0], in0=x1, in1=c_, op=ALU.mult), reads=[kf, cs], writes=[rt])
            S.op("dve", lambda e: e.tensor_tensor(out=rt[:, 1], in0=x2, in1=s_, op=ALU.mult), reads=[kf, cs], writes=[rt])
            S.op("dve", lambda e: e.tensor_tensor(out=rt[:, 2], in0=x2, in1=c_, op=ALU.mult), reads=[kf, cs], writes=[rt])
            S.op("dve", lambda e: e.tensor_tensor(out=rt[:, 3], in0=x1, in1=s_, op=ALU.mult), reads=[kf, cs], writes=[rt])
            S.op("dve", lambda e: e.tensor_tensor(out=kb[:, :, 0:8], in0=rt[:, 0], in1=rt[:, 1], op=ALU.subtract),
                 reads=[rt], writes=[kb])
            S.op("dve", lambda e: e.tensor_tensor(out=kb[:, :, 8:16], in0=rt[:, 2], in1=rt[:, 3], op=ALU.add),
                 reads=[rt], writes=[kb])
            tp = tp_ring.next()
            kb2 = kb[:].rearrange("p a d -> p (a d)")
            for hd in range(4):
                S.op("pe", lambda e: e.transpose(out=tp[:, hd, :], in_=kb2[:, hd * 128:(hd + 1) * 128], identity=idb[:]),
                     reads=[kb, idb], writes=[tp])
            if is_q:
                o = s - NCTX
                S.copy("dve", QT, QT[:, :, o * 128:(o + 1) * 128], tp, tp[:, 0:4, :])
            else:
                kts = kts_ring.next()
                S.copy("dve", kts, kts[:], tp, tp[:, 0:4, :])
                S.dma("sp", KT.t[:, :, s * 128:(s + 1) * 128].rearrange("h p t -> p h t"), kts[:],
                      reads=[kts])

        for blk in range(16):
            own = blk >= 12
            hT = hT_ring.next()
            for j in range(4):
                s = blk * 4 + j
                xt = x_ring.next()
                S.dma("sp", xt[:], xs_d[s * 128:(s + 1) * 128, :], writes=[xt])
                junk = junk_ring.next()
                ss = ss_ring.next()
                S.op("pool", lambda e: e.memset(ss[:], 0.0), writes=[ss])
                S.op("act", lambda e: e.activation(out=junk[:], in_=xt[:], func=AF.Square, accum_out=ss[:, 0:1]),
                     reads=[xt], writes=[junk, ss])
                rsqrt(S, ss, ss[:, 1:2], ss, ss[:, 0:1], 1.0 / 1024, epst, epst[:, 0:1])
                xn = xn_ring.next()
                S.op("dve", lambda e: e.tensor_scalar(out=xn[:], in0=xt[:], scalar1=ss[:, 1:2], scalar2=None,
                                                      op0=ALU.mult), reads=[xt, ss], writes=[xn])
                tp = tp_ring.next()
                for kc in range(8):
                    S.op("pe", lambda e: e.transpose(out=tp[:, kc, :], in_=xn[:, kc * 128:(kc + 1) * 128],
                                                     identity=idb[:]), reads=[xn, idb], writes=[tp])
                S.copy("act", hT, hT[:, :, j * 128:(j + 1) * 128], tp, tp[:])
                cs = cs_ring.next()
                S.dma("sp", cs[:, 0].rearrange("p a d -> p (a d)"), cos_d[s * 128:(s + 1) * 128, :], writes=[cs])
                S.dma("sp", cs[:, 1].rearrange("p a d -> p (a d)"), sin_d[s * 128:(s + 1) * 128, :], writes=[cs])
                parts = [("k", 1920 + 512), ("v", 1920 + 1024)]
                if own:
                    parts.append(("q", 1920))
                for nm, c0 in parts:
                    pd = pd_ring.next()
                    for kc in range(8):
                        S.op("pe", lambda e: e.matmul(pd[:], lhsT=hT[:, kc, j * 128:(j + 1) * 128],
                                                      rhs=Wb[:, kc, c0:c0 + 512], start=(kc == 0), stop=(kc == 7)),
                             reads=[hT, Wb], writes=[pd])
                    if nm == "v":
                        vb = vb_ring.next()
                        S.copy("act", vb, vb[:], pd, pd[:])
                        S.dma("sp", VD.t[s], vb[:], reads=[vb])
                    elif nm == "k":
                        rope_and_T(pd, cs, None, False, s, 1)
                    else:
                        rope_and_T(pd, cs, 0.125, True, s, 0)
            ccs = range(15) if (own or blk == 0 or blk == 11) else range(4, 14)
            for cc in ccs:
                pu = pu_ring.next()
                for kc in range(8):
                    S.op("pe", lambda e: e.matmul(pu[:], lhsT=Wb[:, kc, cc * 128:(cc + 1) * 128], rhs=hT[:, kc, :],
                                                  start=(kc == 0), stop=(kc == 7)), reads=[hT, Wb], writes=[pu])
                ust = ust_ring.next()
                S.copy("rr", ust, ust[:], pu, pu[:])
                S.dma("sp", UT.t[cc, :, 1 + blk * 512:1 + (blk + 1) * 512], ust[:], reads=[ust])
        S.barrier()
        wr = S.sbuf("wrapc", [128, 15, 2], F32)
        S.dma("sp", wr[:, :, 0:1], UT.t[:, :, S_LEN:S_LEN + 1].rearrange("c p o -> p c o"), reads=[UT], writes=[wr],
              allow_slow_non_contiguous=True)
        S.dma("sp", wr[:, :, 1:2], UT.t[:, :, 1:2].rearrange("c p o -> p c o"), reads=[UT], writes=[wr],
              allow_slow_non_contiguous=True)
        S.dma("sp", UT.t[:, :, 0:1].rearrange("c p o -> p c o"), wr[:, :, 0:1], reads=[wr], writes=[UT],
              allow_slow_non_contiguous=True)
        S.dma("sp", UT.t[:, :, S_LEN + 1:S_LEN + 2].rearrange("c p o -> p c o"), wr[:, :, 1:2], reads=[wr], writes=[UT],
              allow_slow_non_contiguous=True)
        S.pop()

        if dbg == "p1":
            for nm, t, shape, dt in (("d_UT", UT, [15, 128, S_LEN + 2], F32), ("d_KT", KT, [4, 128, S_LEN], BF16),
                                     ("d_VD", VD, [NT, 128, 512], BF16)):
                o = nc.dram_tensor(nm, shape, dt, kind="ExternalOutput").ap()
                S.dma("sp", o, t.t, reads=[t])
            o = nc.dram_tensor("d_QT", [128, 4, 2048], BF16, kind="ExternalOutput").ap()
            S.dma("sp", o, QT[:], reads=[QT])
            o = nc.dram_tensor("d_qk", [128, 2], F32, kind="ExternalOutput").ap()
            S.dma("sp", o, qk2max[:], reads=[qk2max])
            S.barrier()
            return nc

        S.push()
        mup = const("mup", mup_d[:, :], [128, 15])
        mun = const("mun", mun_d[:, :], [128, 15])
        c0t = S.sbuf("c0t", [128, 15], F32)
        S.op("dve", lambda e: e.tensor_tensor(out=c0t[:], in0=mup[:], in1=mun[:], op=ALU.add), reads=[mup, mun], writes=[c0t])
        S.op("dve", lambda e: e.tensor_scalar(out=c0t[:], in0=c0t[:], scalar1=-1.0, scalar2=1.0, op0=ALU.mult, op1=ALU.add),
             reads=[c0t], writes=[c0t])
        w0t = const("w0t", w0_d[:, :], [128, 8])
        a0t = const("a0t", a0_d[:, :], [128, 8])
        kkt = const("kkt", kk_d[:, :], [128, 4])
        kat = const("kat", ka_d[:, :], [128, 4])
        rkt = const("rkt", rk_d[:, :], [128, 4])
        omka = S.sbuf("omka", [128, 4], F32)
        S.op("dve", lambda e: e.tensor_scalar(out=omka[:], in0=kat[:], scalar1=-1.0, scalar2=1.0, op0=ALU.mult, op1=ALU.add),
             reads=[kat], writes=[omka])
        eprev = const("eprev", eprev_d[:, :], [128, NT])
        enext = const("enext", enext_d[:, :], [128, NT])
        keepf = const("keepf", keepf_d[:, :], [128, NCTX])
        keepb = const("keepb", keepb_d[:, :], [128, NCTX])
        bones = const("bones", bones_d[:, :], [128, 128])
        hind = const("hind", hind_d[:, :], [128, 2])
        lnw = const("lnw", lnw_d[:, :], [128, 512])
        lnb = const("lnb", lnb_d[:, :], [128, 512])
        ones_t = S.sbuf("ones_t", [128, 128], F32)
        S.op("pool", lambda e: e.memset(ones_t[:], 1.0), writes=[ones_t])

        cbf_st = S.sbuf("cbf_st", [128, 512], F32)

        def cbf(name, src, shape):
            S.dma("sp", cbf_st[:], src, writes=[cbf_st])
            b = S.sbuf(name, shape, BF16)
            S.copy("pool", b, b[:], cbf_st, cbf_st[:])
            return b
        w2b = cbf("w2b", w2_d[:, :], [128, 512])
        a2b = cbf("a2b", a2_d[:, :], [128, 512])
        g2b = cbf("g2b", g2_d[:, :], [128, 512])
        msl = const("msl", msl_d[:, :], [128, 512])
        msu = const("msu", msu_d[:, :], [128, 512])
        mil = const("mil", mil_d[:, :], [128, 512])
        miu = const("miu", miu_d[:, :], [128, 512])
        idb4 = S.sbuf("idb4", [128, 4, 128], BF16)
        for g in range(4):
            S.copy("pool", idb4, idb4[:, g, :], idb, idb[:])

        Hs = [S.sbuf("Hs%d" % d, [128, 4, 64], F32) for d in range(2)]
        for d in range(2):
            S.op("pool", lambda e: e.memset(Hs[d][:], 0.0), writes=[Hs[d]])
        yacc = S.sbuf("yacc", [128, NOWN, 512], F32)
        bon = S.sbuf("bon", [128, NOWN, 8], F32)
        S.op("pool", lambda e: e.memset(bon[:], 0.0), writes=[bon])
        vtok_own = S.sbuf("vtok_own", [128, NOWN, 512], BF16)
        gate_own = S.sbuf("gate_own", [128, NOWN, 512], BF16)

        win_ring = Ring(S, "win", 1, [128, 15, 130], F32)
        sh_ring = Ring(S, "sh", 1, [128, 15, 128], F32)
        sht_ring = Ring(S, "sht", 1, [128, 15, 128], F32)
        f4 = lambda nm, n=1: Ring(S, nm, n, [128, 4, 128], F32)
        b4 = lambda nm, n=2: Ring(S, nm, n, [128, 4, 128], BF16)
        th_ring = Ring(S, "th", 2, [128, 2, 128], BF16)
        sg_ring, a_ring, kx_ring, sq4_ring, kkr_ring, t_ring = f4("sg"), f4("a"), f4("kx"), f4("sq4"), f4("kkr"), f4("tt")
        kd_ring, b_ring, gc_ring, gcx_ring, e_ring = f4("kd"), f4("b"), f4("gc"), f4("gcx"), f4("e", 3)
        sc_ring = Ring(S, "sc", 2, [128, 8, 4], F32)
        bT_ring, kT_ring, vT_ring = b4("bT"), b4("kT"), b4("vT")
        aTz_ring = [b4("aTz0"), b4("aTz1")]
        rTz_ring = [b4("rTz0"), b4("rTz1")]
        for rg in aTz_ring + rTz_ring:
            for t_ in rg.tiles:
                S.op("pool", lambda e: e.memset(t_[:], 0.0), writes=[t_])
        btok_ring, ktok_ring, vtok_ring = b4("btok", 1), b4("ktok", 1), b4("vtok", 1)
        P_ring, PT_ring, TT_ring, PI_ring = b4("Pm", 2), b4("PTm", 2), b4("TTm", 2), b4("PIm", 1)
        Aak_ring, Arb_ring, Ark_ring = b4("Aak", 1), b4("Arb", 1), b4("Ark", 1)
        H0_ring = Ring(S, "H0", 2, [128, 4, 64], BF16)
        X1_ring = Ring(S, "X1", 2, [128, 4, 64], BF16)
        U_ring = Ring(S, "U", 2, [128, 4, 64], BF16)
        prod_ring = f4("prod")
        psX = S.psum("psX", [128, 4, 128])
        psY = S.psum("psY", [128, 4, 128])
        psZ = S.psum("psZ", [128, 4, 128])
        psA = Ring(S, "psA", 2, [128, 4, 128], F32, psum=True)
        psC = Ring(S, "psC", 2, [128, 4, 128], F32, psum=True)
        psT = S.psum("psT", [128, 8, 128], BF16)

        def step(s, d, own):
            fwd = (d == 0)
            ccs = list(range(15)) if own else list(range(4, 14))
            c_lo, ncc = ccs[0], len(ccs)
            win = win_ring.next()
            S.dma("sp", win[:, c_lo:c_lo + ncc, :], UT.t[c_lo:c_lo + ncc, :, s * 128:s * 128 + 130].rearrange("c p t -> p c t"),
                  reads=[UT], writes=[win])
            S.op("pool", lambda e: e.tensor_scalar(out=win[:, c_lo:c_lo + ncc, 0:1], in0=win[:, c_lo:c_lo + ncc, 0:1],
                                                   scalar1=eprev[:, s:s + 1], scalar2=None, op0=ALU.mult),
                 reads=[win, eprev], writes=[win])
            S.op("pool", lambda e: e.tensor_scalar(out=win[:, c_lo:c_lo + ncc, 129:130], in0=win[:, c_lo:c_lo + ncc, 129:130],
                                                   scalar1=enext[:, s:s + 1], scalar2=None, op0=ALU.mult),
                 reads=[win, enext], writes=[win])
            sh = sh_ring.next()
            csl = slice(c_lo, c_lo + ncc)
            bc = lambda t_: t_[:, csl].unsqueeze(2).broadcast_to([128, ncc, 128])
            sht = sht_ring.next()
            S.op("pool", lambda e: e.tensor_tensor(out=sh[:, csl, :], in0=win[:, csl, 1:129], in1=bc(c0t), op=ALU.mult),
                 reads=[win, c0t], writes=[sh])
            for (lo, mu_) in ((0, mup), (2, mun)):
                S.op("pool", lambda e: e.tensor_tensor(out=sht[:, csl, :], in0=win[:, csl, lo:lo + 128], in1=bc(mu_), op=ALU.mult),
                     reads=[win, mu_], writes=[sht])
                S.op("pool", lambda e: e.tensor_tensor(out=sh[:, csl, :], in0=sh[:, csl, :], in1=sht[:, csl, :], op=ALU.add),
                     reads=[sh, sht], writes=[sh])
            STG = int(os.environ.get("MK_STAGE", "9"))
            if STG <= 1:
                return
            R_, K_, V_ = sh[:, 0:4, :], sh[:, 4:8, :], sh[:, 8:12, :]
            dp = slice(d * 64, d * 64 + 64)
            th = th_ring.next()
            S.op("act", lambda e: e.activation(out=th[dp, 0, :], in_=sh[dp, 12, :], func=AF.Tanh), reads=[sh], writes=[th])
            S.copy("dve", th, th[dp, 1, :], sh, sh[dp, 13, :])
            sg = sg_ring.next()
            a_ = a_ring.next()
            pz = psA.next()
            for q4 in range(4):
                S.op("pe", lambda e: e.matmul(pz[:, q4, :], lhsT=w2b[dp, q4 * 128:(q4 + 1) * 128], rhs=th[dp, 0, :],
                                              start=True, stop=True), reads=[w2b, th], writes=[pz])
            for q4 in range(4):
                S.op("act", lambda e: e.activation(out=sg[:, q4, :], in_=pz[:, q4, :], func=AF.Sigmoid,
                                                   bias=w0t[:, d * 4 + q4:d * 4 + q4 + 1]), reads=[pz, w0t], writes=[sg])
            pa = psA.next()
            for q4 in range(4):
                S.op("pe", lambda e: e.matmul(pa[:, q4, :], lhsT=a2b[dp, q4 * 128:(q4 + 1) * 128], rhs=th[dp, 1, :],
                                              start=True, stop=True), reads=[a2b, th], writes=[pa])
            for q4 in range(4):
                S.op("act", lambda e: e.activation(out=a_[:, q4, :], in_=pa[:, q4, :], func=AF.Sigmoid,
                                                   bias=a0t[:, d * 4 + q4:d * 4 + q4 + 1]), reads=[pa, a0t], writes=[a_])
            if STG <= 2:
                return
            kx = kx_ring.next()
            for q4 in range(4):
                S.op("pool", lambda e: e.tensor_scalar(out=kx[:, q4, :], in0=sh[:, 4 + q4, :], scalar1=kkt[:, q4:q4 + 1],
                                                       scalar2=None, op0=ALU.mult), reads=[sh, kkt], writes=[kx])
            sq4 = sq4_ring.next()
            S.op("pool", lambda e: e.tensor_tensor(out=sq4[:], in0=kx[:], in1=kx[:], op=ALU.mult), reads=[kx], writes=[sq4])
            pn = psA.next()
            for q4 in range(4):
                S.op("pe", lambda e: e.matmul(pn[:, q4, :], lhsT=bones[:], rhs=sq4[:, q4, :], start=True, stop=True),
                     reads=[bones, sq4], writes=[pn])
            kkr = kkr_ring.next()
            S.op("act", lambda e: e.activation(out=kkr[:], in_=pn[:], func=AF.Sqrt), reads=[pn], writes=[kkr])
            S.op("dve", lambda e: e.tensor_scalar(out=kkr[:], in0=kkr[:], scalar1=1e-12, scalar2=None, op0=ALU.max),
                 reads=[kkr], writes=[kkr])
            S.op("dve", lambda e: e.reciprocal(out=kkr[:], in_=kkr[:]), reads=[kkr], writes=[kkr])
            S.op("dve", lambda e: e.tensor_tensor(out=kkr[:], in0=kkr[:], in1=kx[:], op=ALU.mult), reads=[kkr, kx], writes=[kkr])
            tt = t_ring.next()
            for q4 in range(4):
                S.op("dve", lambda e: e.tensor_scalar(out=tt[:, q4, :], in0=a_[:, q4, :], scalar1=kat[:, q4:q4 + 1],
                                                      scalar2=omka[:, q4:q4 + 1], op0=ALU.mult, op1=ALU.add),
                     reads=[a_, kat, omka], writes=[tt])
            kd = kd_ring.next()
            S.op("dve", lambda e: e.tensor_tensor(out=kd[:], in0=tt[:], in1=K_, op=ALU.mult), reads=[tt, sh], writes=[kd])
            bb = b_ring.next()
            S.op("pool", lambda e: e.tensor_tensor(out=bb[:], in0=kkr[:], in1=a_[:], op=ALU.mult), reads=[kkr, a_], writes=[bb])
            if STG <= 3:
                return
            gc = gc_ring.next()
            gcx = gcx_ring.next()
            sc = sc_ring.next()
            for q4 in range(4):
                S.op("dve", lambda e: e.tensor_tensor_scan(out=gc[:, q4, :], data0=ones_t[:], data1=sg[:, q4, :], initial=0.0,
                                                           op0=ALU.mult, op1=ALU.add), reads=[ones_t, sg], writes=[gc])
            if not fwd:
                S.copy("dve", sc, sc[:, 7, :], gc, gc[:, :, 127])
                for q4 in range(4):
                    S.op("dve", lambda e: e.scalar_tensor_tensor(out=gc[:, q4, :], in0=gc[:, q4, :], scalar=-1.0,
                                                                 in1=sg[:, q4, :], op0=ALU.mult, op1=ALU.add),
                         reads=[gc, sg], writes=[gc])
                    S.op("dve", lambda e: e.tensor_scalar(out=gc[:, q4, :], in0=gc[:, q4, :], scalar1=sc[:, 7, q4:q4 + 1],
                                                          scalar2=None, op0=ALU.add), reads=[gc, sc], writes=[gc])
            S.op("pool", lambda e: e.tensor_tensor(out=gcx[:], in0=gc[:], in1=sg[:], op=ALU.subtract), reads=[gc, sg], writes=[gcx])
            mid = 63 if fwd else 64
            last = 127 if fwd else 0
            S.op("dve", lambda e: e.tensor_scalar(out=sc[:, 0, :], in0=gc[:, :, mid], scalar1=CDEC, scalar2=None, op0=ALU.mult),
                 reads=[gc], writes=[sc])
            S.op("dve", lambda e: e.tensor_scalar(out=sc[:, 1, :], in0=gc[:, :, mid], scalar1=-CDEC, scalar2=None, op0=ALU.mult),
                 reads=[gc], writes=[sc])
            S.op("dve", lambda e: e.tensor_tensor(out=sc[:, 5, :], in0=gc[:, :, last], in1=gc[:, :, mid], op=ALU.subtract),
                 reads=[gc], writes=[sc])
            S.op("act", lambda e: e.activation(out=sc[:, 2, :], in_=gc[:, :, mid], func=AF.Exp, scale=-CDEC), reads=[gc], writes=[sc])
            S.op("act", lambda e: e.activation(out=sc[:, 3, :], in_=gc[:, :, last], func=AF.Exp, scale=-CDEC), reads=[gc], writes=[sc])
            S.op("act", lambda e: e.activation(out=sc[:, 4, :], in_=sc[:, 5, :], func=AF.Exp, scale=-CDEC), reads=[sc], writes=[sc])
            if not own:
                keep = (keepf if fwd else keepb)
                for r in (3, 4):
                    S.op("dve", lambda e: e.tensor_scalar(out=sc[:, r, :], in0=sc[:, r, :], scalar1=keep[:, s:s + 1],
                                                          scalar2=None, op0=ALU.mult), reads=[sc, keep], writes=[sc])
            eP, eN, ePm = e_ring.next(), e_ring.next(), e_ring.next()
            for q4 in range(4):
                S.op("act", lambda e: e.activation(out=eN[:, q4, :], in_=gc[:, q4, :], func=AF.Exp, scale=CDEC,
                                                   bias=sc[:, 1, q4:q4 + 1]), reads=[gc, sc], writes=[eN])
                S.op("act", lambda e: e.activation(out=ePm[:, q4, :], in_=gcx[:, q4, :], func=AF.Exp, scale=-CDEC,
                                                   bias=sc[:, 0, q4:q4 + 1]), reads=[gcx, sc], writes=[ePm])
                if own:
                    S.op("act", lambda e: e.activation(out=eP[:, q4, :], in_=gc[:, q4, :], func=AF.Exp, scale=-CDEC,
                                                       bias=sc[:, 0, q4:q4 + 1]), reads=[gc, sc], writes=[eP])
            bT, kT, vT = bT_ring.next(), kT_ring.next(), vT_ring.next()
            aTz = [aTz_ring[0].next(), aTz_ring[1].next()]
            for par in range(2):
                pp_ = slice(par * 64, par * 64 + 64)
                S.op("dve", lambda e: e.scalar_tensor_tensor(out=aTz[par][pp_], in0=kkr[pp_], scalar=-1.0, in1=ePm[pp_],
                                                             op0=ALU.mult, op1=ALU.mult), reads=[kkr, ePm], writes=[aTz[par]])
            S.op("pool", lambda e: e.tensor_tensor(out=bT[:], in0=bb[:], in1=eN[:], op=ALU.mult), reads=[bb, eN], writes=[bT])
            S.op("dve", lambda e: e.tensor_tensor(out=kT[:], in0=kd[:], in1=eN[:], op=ALU.mult), reads=[kd, eN], writes=[kT])
            S.copy("pool", vT, vT[:], sh, V_)
            rTz = None
            if own:
                rTz = [rTz_ring[0].next(), rTz_ring[1].next()]
                for par in range(2):
                    pp_ = slice(par * 64, par * 64 + 64)
                    S.op("pool", lambda e: e.tensor_tensor(out=rTz[par][pp_], in0=sh[pp_, 0:4, :], in1=eP[pp_], op=ALU.mult),
                         reads=[sh, eP], writes=[rTz[par]])
                prod = prod_ring.next()
                S.op("pool", lambda e: e.tensor_tensor(out=prod[:], in0=R_, in1=kd[:], op=ALU.mult), reads=[sh, kd], writes=[prod])
                for q4 in range(4):
                    S.op("pool", lambda e: e.tensor_scalar(out=prod[:, q4, :], in0=prod[:, q4, :], scalar1=rkt[:, q4:q4 + 1],
                                                           scalar2=None, op0=ALU.mult), reads=[prod, rkt], writes=[prod])
                pb = psC.next()
                for q4 in range(4):
                    S.op("pe", lambda e: e.matmul(pb[:, 0, q4 * 2:q4 * 2 + 2], lhsT=prod[:, q4, :], rhs=hind[:], start=True, stop=True),
                         reads=[prod, hind], writes=[pb])
                o = s - NCTX
                S.op("dve", lambda e: e.tensor_tensor(out=bon[:, o, :], in0=bon[:, o, :], in1=pb[:, 0, 0:8], op=ALU.add),
                     reads=[bon, pb], writes=[bon])
            if STG <= 4:
                return
            btok, ktok, vtok = btok_ring.next(), ktok_ring.next(), vtok_ring.next()
            for src, dst in ((bT, btok), (kT, ktok), (vT, vtok)):
                for q4 in range(4):
                    S.op("pe", lambda e: e.transpose(out=psT[:, q4, :], in_=src[:, q4, :], identity=idb[:]),
                         reads=[src, idb], writes=[psT])
                S.copy("rr", dst, dst[:], psT, psT[:, 0:4, :])
            if own and fwd:
                o = s - NCTX
                S.copy("pool", vtok_own, vtok_own[:, o, :], vtok, vtok[:].rearrange("p a b -> p (a b)"))
                gsb = th_ring.next()
                S.op("act", lambda e: e.activation(out=gsb[:, 0, :], in_=sh[:, 14, :], func=AF.Sigmoid), reads=[sh], writes=[gsb])
                pg = psC.next()
                S.op("pe", lambda e: e.matmul(pg[:].rearrange("p a b -> p (a b)"), lhsT=gsb[:, 0, :], rhs=g2b[:], start=True, stop=True),
                     reads=[gsb, g2b], writes=[pg])
                S.copy("act", gate_own, gate_own[:, o, :], pg, pg[:].rearrange("p a b -> p (a b)"))
            if STG <= 5:
                return
            mL, mLT, mI = (msl, msu, miu) if fwd else (msu, msl, mil)
            H = Hs[d]
            for hg in range(2):
                def hsl(hh):
                    h = hg * 4 + hh
                    return h // 2, h % 2
                pAak = psA.next()
                for hh in range(4):
                    q4, par = hsl(hh)
                    ps_ = slice(par * 64, par * 64 + 64)
                    S.op("pe", lambda e: e.matmul(psX[:, hh, :], lhsT=aTz[par][:, q4, :], rhs=bT[:, q4, :], start=True, stop=True),
                         reads=[aTz[par], bT], writes=[psX])
                    S.op("pe", lambda e: e.matmul(psY[:, hh, :], lhsT=bT[:, q4, :], rhs=aTz[par][:, q4, :], start=True, stop=True),
                         reads=[aTz[par], bT], writes=[psY])
                    S.op("pe", lambda e: e.matmul(pAak[:, hh, :], lhsT=kT[:, q4, :], rhs=aTz[par][:, q4, :], start=True, stop=True),
                         reads=[aTz[par], kT], writes=[pAak])
                Pm, PTm, TTm, Aak = P_ring.next(), PT_ring.next(), TT_ring.next(), Aak_ring.next()
                m2 = lambda m: m[:].rearrange("p (a b) -> p a b", b=128)
                S.op("dve", lambda e: e.tensor_tensor(out=Pm[:], in0=psX[:], in1=m2(mL), op=ALU.mult), reads=[psX, mL], writes=[Pm])
                S.op("dve", lambda e: e.tensor_tensor(out=PTm[:], in0=psY[:], in1=m2(mLT), op=ALU.mult), reads=[psY, mLT], writes=[PTm])
                S.op("dve", lambda e: e.tensor_tensor(out=Aak[:], in0=pAak[:], in1=m2(mLT), op=ALU.mult), reads=[pAak, mLT], writes=[Aak])
                Arb = Ark = None
                if own:
                    pArb, pArk = psA.next(), psC.next()
                    for hh in range(4):
                        q4, par = hsl(hh)
                        ps_ = slice(par * 64, par * 64 + 64)
                        S.op("pe", lambda e: e.matmul(pArb[:, hh, :], lhsT=bT[:, q4, :], rhs=rTz[par][:, q4, :], start=True, stop=True),
                             reads=[rTz[par], bT], writes=[pArb])
                        S.op("pe", lambda e: e.matmul(pArk[:, hh, :], lhsT=kT[:, q4, :], rhs=rTz[par][:, q4, :], start=True, stop=True),
                             reads=[rTz[par], kT], writes=[pArk])
                    Arb, Ark = Arb_ring.next(), Ark_ring.next()
                    S.op("dve", lambda e: e.tensor_tensor(out=Arb[:], in0=pArb[:], in1=m2(mI), op=ALU.mult), reads=[pArb, mI], writes=[Arb])
                    S.op("dve", lambda e: e.tensor_tensor(out=Ark[:], in0=pArk[:], in1=m2(mI), op=ALU.mult), reads=[pArk, mI], writes=[Ark])
                S.op("pool", lambda e: e.tensor_tensor(out=TTm[:], in0=PTm[:], in1=idb4[:], op=ALU.add), reads=[PTm, idb4], writes=[TTm])
                for lev in range(1, 7):
                    for hh in range(4):
                        S.op("pe", lambda e: e.matmul(psX[:, hh, :], lhsT=PTm[:, hh, :], rhs=Pm[:, hh, :], start=True, stop=True),
                             reads=[Pm, PTm], writes=[psX])
                    if lev < 6:
                        for hh in range(4):
                            S.op("pe", lambda e: e.matmul(psY[:, hh, :], lhsT=Pm[:, hh, :], rhs=PTm[:, hh, :], start=True, stop=True),
                                 reads=[Pm, PTm], writes=[psY])
                    Pn = P_ring.next()
                    S.copy("act", Pn, Pn[:], psX, psX[:])
                    if lev < 6:
                        PTn = PT_ring.next()
                        S.copy("dve", PTn, PTn[:], psY, psY[:])
                    for hh in range(4):
                        S.op("pe", lambda e: e.matmul(psZ[:, hh, :], lhsT=Pn[:, hh, :], rhs=TTm[:, hh, :], start=True, stop=True),
                             reads=[Pn, TTm], writes=[psZ])
                    dT = PI_ring.next()
                    S.copy("rr", dT, dT[:], psZ, psZ[:])
                    TTn = TT_ring.next()
                    S.op("pool", lambda e: e.tensor_tensor(out=TTn[:], in0=dT[:], in1=TTm[:], op=ALU.add), reads=[dT, TTm], writes=[TTn])
                    Pm, TTm = Pn, TTn
                    if lev < 6:
                        PTm = PTn
                if STG <= 6:
                    continue
                H0 = H0_ring.next()
                for qq in range(2):
                    q4 = hg * 2 + qq
                    S.op("dve", lambda e: e.tensor_scalar(out=H0[:, q4, :], in0=H[:, q4, :], scalar1=sc[:, 2, q4:q4 + 1],
                                                          scalar2=None, op0=ALU.mult), reads=[H, sc], writes=[H0])
                pX1 = psC.next()
                for hh in range(4):
                    q4, par = hsl(hh)
                    ps_ = slice(par * 64, par * 64 + 64)
                    S.op("pe", lambda e: e.matmul(pX1[:, hh, 0:64], lhsT=aTz[par][:, q4, :], rhs=H0[:, q4, :], start=True, stop=False),
                         reads=[aTz[par], H0], writes=[pX1])
                    S.op("pe", lambda e: e.matmul(pX1[:, hh, 0:64], lhsT=Aak[:, hh, :], rhs=vtok[:, q4, ps_], start=False, stop=True),
                         reads=[Aak, vtok], writes=[pX1])
                X1 = X1_ring.next()
                S.copy("act", X1, X1[:], pX1, pX1[:, :, 0:64])
                pU = psC.next()
                for hh in range(4):
                    S.op("pe", lambda e: e.matmul(pU[:, hh, 0:64], lhsT=TTm[:, hh, :], rhs=X1[:, hh, :], start=True, stop=True),
                         reads=[TTm, X1], writes=[pU])
                U = U_ring.next()
                S.copy("dve", U, U[:], pU, pU[:, :, 0:64])
                if own:
                    pY = psA.next()
                    for hh in range(4):
                        q4, par = hsl(hh)
                        ps_ = slice(par * 64, par * 64 + 64)
                        S.op("pe", lambda e: e.matmul(pY[:, hh, 0:64], lhsT=rTz[par][:, q4, :], rhs=H0[:, q4, :], start=True, stop=False),
                             reads=[rTz[par], H0], writes=[pY])
                        S.op("pe", lambda e: e.matmul(pY[:, hh, 0:64], lhsT=Arb[:, hh, :], rhs=U[:, hh, :], start=False, stop=False),
                             reads=[Arb, U], writes=[pY])
                        S.op("pe", lambda e: e.matmul(pY[:, hh, 0:64], lhsT=Ark[:, hh, :], rhs=vtok[:, q4, ps_], start=False, stop=True),
                             reads=[Ark, vtok], writes=[pY])
                    o = s - NCTX
                    ysl = yacc[:, o, hg * 256:(hg + 1) * 256].rearrange("p (a b) -> p a b", b=64)
                    if fwd:
                        S.copy("act", yacc, ysl, pY, pY[:, :, 0:64])
                    else:
                        S.op("dve", lambda e: e.tensor_tensor(out=ysl, in0=ysl, in1=pY[:, :, 0:64], op=ALU.add),
                             reads=[yacc, pY], writes=[yacc])
                pH = psC.next()
                for qq in range(2):
                    q4 = hg * 2 + qq
                    S.op("pe", lambda e: e.matmul(pH[:, qq, :], lhsT=btok[:, q4, :], rhs=U[:, qq * 2:qq * 2 + 2, :].rearrange("p a b -> p (a b)"),
                                                  start=True, stop=False), reads=[btok, U], writes=[pH])
                    S.op("pe", lambda e: e.matmul(pH[:, qq, :], lhsT=ktok[:, q4, :], rhs=vtok[:, q4, :], start=False, stop=True),
                         reads=[ktok, vtok], writes=[pH])
                for qq in range(2):
                    q4 = hg * 2 + qq
                    for par in range(2):
                        pp = slice(par * 64, par * 64 + 64)
                        S.op("dve", lambda e: e.tensor_scalar(out=H[pp, q4, :], in0=H[pp, q4, :], scalar1=sc[pp, 3, q4:q4 + 1],
                                                              scalar2=None, op0=ALU.mult), reads=[H, sc], writes=[H])
                        S.op("dve", lambda e: e.scalar_tensor_tensor(out=H[pp, q4, :], in0=pH[pp, qq, par * 64:par * 64 + 64],
                                                                     scalar=sc[pp, 4, q4:q4 + 1], in1=H[pp, q4, :],
                                                                     op0=ALU.mult, op1=ALU.add), reads=[pH, sc, H], writes=[H])

        lim = int(os.environ.get("MK_P2LIM", "0"))
        ctx_f = list(range(NCTX))
        ctx_b = list(range(NCTX - 1, -1, -1))
        own_f = list(range(NCTX, NT))
        own_b = list(range(NT - 1, NCTX - 1, -1))
        if lim:
            ctx_f, ctx_b = ctx_f[-lim:], ctx_b[:0]
        olim = int(os.environ.get("MK_OWNLIM", "0"))
        if olim:
            own_f, own_b = own_f[:olim], own_b[:olim]
        if dbg == "p13":
            ctx_f, ctx_b, own_f, own_b = [], [], [], []
        for s in ctx_f:
            step(s, 0, False)
        for s in own_f:
            step(s, 0, True)
        for s in ctx_b:
            step(s, 1, False)
        for s in own_b:
            step(s, 1, True)

        st_ring = Ring(S, "gnst", 2, [128, 8, 4], F32)
        yn_ring = Ring(S, "yn", 1, [128, 8, 64], F32)
        yb_ring = Ring(S, "yb", 2, [128, 512], BF16)
        for o in range(NOWN):
            y3 = yacc[:, o, :].rearrange("p (h j) -> p h j", j=64)
            st = st_ring.next()
            yn = yn_ring.next()
            S.op("dve", lambda e: e.tensor_reduce(out=st[:, :, 0], in_=y3, axis=AX.X, op=ALU.add), reads=[yacc], writes=[st])
            S.op("dve", lambda e: e.tensor_scalar(out=st[:, :, 0], in0=st[:, :, 0], scalar1=1.0 / 64, scalar2=None, op0=ALU.mult),
                 reads=[st], writes=[st])
            for h in range(8):
                S.op("dve", lambda e: e.tensor_scalar(out=yn[:, h, :], in0=y3[:, h, :], scalar1=st[:, h, 0:1], scalar2=None,
                                                      op0=ALU.subtract), reads=[yacc, st], writes=[yn])
            sqt = sq4_ring.next()
            sq3 = sqt[:].rearrange("p a b -> p (a b)").rearrange("p (h j) -> p h j", j=64)
            S.op("pool", lambda e: e.tensor_tensor(out=sq3, in0=yn[:], in1=yn[:], op=ALU.mult), reads=[yn], writes=[sqt])
            S.op("dve", lambda e: e.tensor_reduce(out=st[:, :, 1], in_=sq3, axis=AX.X, op=ALU.add), reads=[sqt], writes=[st])
            rsqrt(S, st, st[:, :, 1], st, st[:, :, 1], 1.0 / 64, epst, epst[:, 2:3])
            for h in range(8):
                S.op("dve", lambda e: e.tensor_scalar(out=yn[:, h, :], in0=yn[:, h, :], scalar1=st[:, h, 1:2], scalar2=None,
                                                      op0=ALU.mult), reads=[yn, st], writes=[yn])
            ynf = yn[:].rearrange("p h j -> p (h j)")
            S.op("pool", lambda e: e.tensor_tensor(out=ynf, in0=ynf, in1=lnw[:], op=ALU.mult), reads=[yn, lnw], writes=[yn])
            S.op("pool", lambda e: e.tensor_tensor(out=ynf, in0=ynf, in1=lnb[:], op=ALU.add), reads=[yn, lnb], writes=[yn])
            for h in range(8):
                S.op("dve", lambda e: e.scalar_tensor_tensor(out=yn[:, h, :], in0=vtok_own[:, o, h * 64:(h + 1) * 64],
                                                             scalar=bon[:, o, h:h + 1], in1=yn[:, h, :], op0=ALU.mult, op1=ALU.add),
                     reads=[vtok_own, bon, yn], writes=[yn])
            yb = yb_ring.next()
            S.op("dve", lambda e: e.tensor_tensor(out=yb[:], in0=ynf, in1=gate_own[:, o, :], op=ALU.mult),
                 reads=[yn, gate_own], writes=[yb])
            for q4 in range(4):
                S.op("pe", lambda e: e.transpose(out=psT[:, q4, :], in_=yb[:, q4 * 128:(q4 + 1) * 128], identity=idb[:]),
                     reads=[yb, idb], writes=[psT])
            S.copy("act", yT, yT[:, :, o * 128:(o + 1) * 128], psT, psT[:, 0:4, :])
        if dbg in ("p2", "p23"):
            o_ = nc.dram_tensor("d_yacc", [128, NOWN, 512], F32, kind="ExternalOutput").ap()
            S.dma("sp", o_, yacc[:], reads=[yacc])
            o_ = nc.dram_tensor("d_yT", [128, 4, 2048], BF16, kind="ExternalOutput").ap()
            S.dma("sp", o_, yT[:], reads=[yT])
            o_ = nc.dram_tensor("d_H", [2, 128, 4, 64], F32, kind="ExternalOutput").ap()
            for d in range(2):
                S.dma("sp", o_[d], Hs[d][:], reads=[Hs[d]])
        if dbg == "p2":
            S.barrier()
            S.pop()
            return nc
        S.pop()

        oT = S.sbuf("oT", [128, 4, 2048], BF16)
        S.push()
        pm = S.psum("pm", [128, 512])
        mrow = S.sbuf("mrow", [1, 4], F32)
        negM = S.sbuf("negM", [128, 1], F32)
        ones1 = S.sbuf("ones1", [1, 128], F32)
        S.op("pool", lambda e: e.memset(ones1[:], 1.0), writes=[ones1])
        for c in range(2):
            S.op("pe", lambda e: e.matmul(pm[0:1, 0:128], lhsT=qk2max[:, c:c + 1], rhs=idf[:], start=True, stop=True),
                 reads=[qk2max, idf], writes=[pm])
            S.op("dve", lambda e: e.tensor_reduce(out=mrow[:, c:c + 1], in_=pm[0:1, 0:128], axis=AX.X, op=ALU.max), reads=[pm], writes=[mrow])
        S.op("dve", lambda e: e.tensor_scalar(out=mrow[:, 2:3], in0=mrow[:, 0:1], scalar1=-4.0, scalar2=None, op0=ALU.mult),
             reads=[mrow], writes=[mrow])
        S.op("dve", lambda e: e.scalar_tensor_tensor(out=mrow[:, 3:4], in0=mrow[:, 1:2], scalar=-1.0 / 16, in1=mrow[:, 2:3],
                                                     op0=ALU.mult, op1=ALU.add), reads=[mrow], writes=[mrow])
        S.op("pe", lambda e: e.matmul(pm[:, 0:1], lhsT=ones1[:], rhs=mrow[:, 3:4], start=True, stop=True), reads=[ones1, mrow], writes=[pm])
        S.copy("dve", negM, negM[:], pm, pm[:, 0:1])
        lamt = const("lamt", lam_d[:, :, :], [128, 4, 64])
        lam = S.sbuf("lam", [128, 4], F32)
        S.op("dve", lambda e: e.tensor_tensor(out=lamt[:, 0, :], in0=lamt[:, 0, :], in1=lamt[:, 1, :], op=ALU.mult), reads=[lamt], writes=[lamt])
        S.op("dve", lambda e: e.tensor_tensor(out=lamt[:, 2, :], in0=lamt[:, 2, :], in1=lamt[:, 3, :], op=ALU.mult), reads=[lamt], writes=[lamt])
        S.op("dve", lambda e: e.tensor_reduce(out=lam[:, 0:1], in_=lamt[:, 0, :], axis=AX.X, op=ALU.add), reads=[lamt], writes=[lam])
        S.op("dve", lambda e: e.tensor_reduce(out=lam[:, 1:2], in_=lamt[:, 2, :], axis=AX.X, op=ALU.add), reads=[lamt], writes=[lam])
        S.op("act", lambda e: e.activation(out=lam[:, 0:2], in_=lam[:, 0:2], func=AF.Exp), reads=[lam], writes=[lam])
        S.op("dve", lambda e: e.tensor_tensor(out=lam[:, 2:3], in0=lam[:, 1:2], in1=lam[:, 0:1], op=ALU.subtract), reads=[lam], writes=[lam])
        S.op("dve", lambda e: e.tensor_scalar(out=lam[:, 2:3], in0=lam[:, 2:3], scalar1=-LAMBDA_INIT, scalar2=None, op0=ALU.add),
             reads=[lam], writes=[lam])
        subln = const("subln", subln_d[:, :], [128, 128])
        S.op("dve", lambda e: e.tensor_scalar(out=subln[:], in0=subln[:], scalar1=1.0 - LAMBDA_INIT, scalar2=None, op0=ALU.mult),
             reads=[subln], writes=[subln])
        KTh_ring = Ring(S, "KTh", 2, [128, S_LEN], BF16)
        Vh_ring = Ring(S, "Vh", 2, [128, NT, 130], BF16)
        PT_ring2 = Ring(S, "PTa", 3, [128, 2, 512], BF16)
        psS = Ring(S, "psS", 2, [128, 2, 512], F32, psum=True)
        accA = S.psum("accA", [128, 512])
        accB = S.psum("accB", [128, 512])
        accC = S.psum("accC", [128, 512])
        acc_slots = [(accA, 0), (accA, 1), (accA, 2), (accB, 0), (accB, 1), (accB, 2), (accC, 0), (accC, 1)]
        o1_ring = Ring(S, "o1", 2, [128, 130], F32)
        o2_ring = Ring(S, "o2", 2, [128, 130], F32)
        ob_ring = Ring(S, "ob", 2, [128, 128], BF16)
        for hd in range(4):
            KTh = KTh_ring.next()
            Vh = Vh_ring.next()
            S.dma("sp", KTh[:], KT.t[hd], reads=[KT], writes=[KTh])
            for g8 in range(8):
                S.dma("sp", Vh[:, g8 * 8:(g8 + 1) * 8, 0:128],
                      VD.t[g8 * 8:(g8 + 1) * 8, :, hd * 128:(hd + 1) * 128].rearrange("s p c -> p s c"), reads=[VD], writes=[Vh])
            S.op("pool", lambda e: e.memset(Vh[:, :, 128:130], 1.0), writes=[Vh])
            for qs in range(4):
                for kc in range(NT):
                    ps = psS.next()
                    for br in range(2):
                        bp = slice(br * 64, br * 64 + 64)
                        S.op("pe", lambda e: e.matmul(ps[:, br, :], lhsT=KTh[bp, kc * 128:(kc + 1) * 128],
                                                      rhs=QT[bp, hd, qs * 512:(qs + 1) * 512], start=True, stop=True),
                             reads=[KTh, QT], writes=[ps])
                    PT = PT_ring2.next()
                    S.op("act", lambda e: e.activation(out=PT[:], in_=ps[:], func=AF.Exp, bias=negM[:, 0:1]),
                         reads=[ps, negM], writes=[PT])
                    for br in range(2):
                        for qb in range(4):
                            at, ai = acc_slots[br * 4 + qb]
                            S.op("pe", lambda e: e.matmul(at[:, ai * 130:ai * 130 + 129], lhsT=PT[:, br, qb * 128:(qb + 1) * 128],
                                                          rhs=Vh[:, kc, 0:129], start=(kc == 0 and ai == 0), stop=(kc == NT - 1)),
                                 reads=[PT, Vh], writes=[at])
                for qb in range(4):
                    o1, o2 = o1_ring.next(), o2_ring.next()
                    a1, i1 = acc_slots[qb]
                    a2_, i2 = acc_slots[4 + qb]
                    S.copy("act", o1, o1[:, 0:129], a1, a1[:, i1 * 130:i1 * 130 + 129])
                    S.copy("dve", o2, o2[:, 0:129], a2_, a2_[:, i2 * 130:i2 * 130 + 129])
                    S.op("dve", lambda e: e.reciprocal(out=o1[:, 129:130], in_=o1[:, 128:129]), reads=[o1], writes=[o1])
                    S.op("dve", lambda e: e.reciprocal(out=o2[:, 129:130], in_=o2[:, 128:129]), reads=[o2], writes=[o2])
                    S.op("dve", lambda e: e.tensor_tensor(out=o2[:, 129:130], in0=o2[:, 129:130], in1=lam[:, 2:3], op=ALU.mult),
                         reads=[o2, lam], writes=[o2])
                    S.op("dve", lambda e: e.tensor_scalar(out=o1[:, 0:128], in0=o1[:, 0:128], scalar1=o1[:, 129:130], scalar2=None,
                                                          op0=ALU.mult), reads=[o1], writes=[o1])
                    S.op("dve", lambda e: e.scalar_tensor_tensor(out=o1[:, 0:128], in0=o2[:, 0:128], scalar=o2[:, 129:130],
                                                                 in1=o1[:, 0:128], op0=ALU.mult, op1=ALU.add),
                         reads=[o1, o2], writes=[o1])
                    S.op("pool", lambda e: e.memset(o2[:, 128:130], 0.0), writes=[o2])
                    S.op("act", lambda e: e.activation(out=o2[:, 0:128], in_=o1[:, 0:128], func=AF.Square, accum_out=o2[:, 128:129]),
                         reads=[o1], writes=[o2])
                    rsqrt(S, o2, o2[:, 128:129], o2, o2[:, 128:129], 1.0 / 128, epst, epst[:, 1:2])
                    ob = ob_ring.next()
                    S.op("dve", lambda e: e.scalar_tensor_tensor(out=ob[:], in0=o1[:, 0:128], scalar=o2[:, 128:129], in1=subln[:],
                                                                 op0=ALU.mult, op1=ALU.mult), reads=[o1, o2, subln], writes=[ob])
                    ptp = psS.next()
                    ptb = ptp[:].rearrange("p a b -> p (a b)")
                    S.op("pe", lambda e: e.matmul(ptb[:, 0:128], lhsT=ob[:], rhs=idb[:], start=True, stop=True),
                         reads=[ob, idb], writes=[ptp])
                    tok0 = qs * 512 + qb * 128
                    S.copy("act", oT, oT[:, hd, tok0:tok0 + 128], ptp, ptb[:, 0:128])
        if dbg in ("p3", "p23", "p13"):
            o_ = nc.dram_tensor("d_oT", [128, 4, 2048], BF16, kind="ExternalOutput").ap()
            S.dma("sp", o_, oT[:], reads=[oT])
            S.barrier()
            S.pop()
            return nc
        S.pop()

        _emit_p4(S, nc, locals())
        S.barrier()
    return nc


def _tab(v, ncol):
    return np.ascontiguousarray(np.asarray(v, np.float32).reshape(ncol, 128).T)


def prep_core_inputs(inputs, c):
    b, qi = c // 4, c % 4
    x = np.asarray(inputs["x"], np.float32)
    own0 = 2048 * qi
    idx = np.concatenate([np.arange(own0 + 2048, S_LEN), np.arange(0, own0), np.arange(own0, own0 + 2048)])
    d = {}
    d["xs"] = np.ascontiguousarray(x[b, idx])
    d["p_own"] = np.ascontiguousarray(np.asarray(inputs["p"], np.float32)[0, b, own0:own0 + 2048])
    d["w_in"] = np.asarray(inputs["w_in"], np.float32)[0]
    d["w_unused"] = np.zeros((1, 1), np.float32)
    d["gmix"] = _tab(inputs["norm_mix"][0], 8)
    d["gffn"] = _tab(inputs["norm_ffn"][0], 8)
    d["gple"] = _tab(inputs["norm_ple"][0], 8)
    d["gfin"] = _tab(inputs["norm_final"], 8)
    d["mup"] = _tab(inputs["shift_mu_prev"][0], 15)
    d["mun"] = _tab(inputs["shift_mu_next"][0], 15)
    d["w0t"] = _tab(np.asarray(inputs["rwkv_w0"], np.float32)[0].reshape(-1), 8)
    d["a0t"] = _tab(np.asarray(inputs["rwkv_a0"], np.float32)[0].reshape(-1), 8)
    d["w2t"] = np.ascontiguousarray(np.asarray(inputs["rwkv_w2"], np.float32)[0].reshape(128, 512))
    d["a2t"] = np.ascontiguousarray(np.asarray(inputs["rwkv_a2"], np.float32)[0].reshape(128, 512))
    d["g2"] = np.asarray(inputs["rwkv_g2"], np.float32)[0]
    d["kkt"] = _tab(inputs["rwkv_k_k"][0], 4)
    d["kat"] = _tab(inputs["rwkv_k_a"][0], 4)
    d["rkt"] = _tab(np.asarray(inputs["rwkv_r_k"], np.float32)[0].reshape(-1), 4)
    d["lnw_b"] = np.ascontiguousarray(np.broadcast_to(np.asarray(inputs["rwkv_ln_w"], np.float32)[0], (128, 512)))
    d["lnb_b"] = np.ascontiguousarray(np.broadcast_to(np.asarray(inputs["rwkv_ln_b"], np.float32)[0], (128, 512)))
    d["rwkv_w_o"] = np.asarray(inputs["rwkv_w_o"], np.float32)[0]
    lam = np.stack([np.asarray(inputs[k], np.float32)[0] for k in ("da_lq1", "da_lk1", "da_lq2", "da_lk2")])
    d["lam_b"] = np.ascontiguousarray(np.broadcast_to(lam, (128, 4, 64)))
    d["subln_b"] = np.ascontiguousarray(np.broadcast_to(np.asarray(inputs["da_subln_w"], np.float32)[0], (128, 128)))
    d["da_w_o"] = np.asarray(inputs["da_w_o"], np.float32)[0]
    d["w_out"] = np.asarray(inputs["w_out"], np.float32)[0]
    d["w_ff1"] = np.asarray(inputs["w_ff1"], np.float32)[0]
    d["w_ff2"] = np.asarray(inputs["w_ff2"], np.float32)[0]
    d["w_ple_gate"] = np.asarray(inputs["w_ple_gate"], np.float32)[0]
    d["w_ple_proj"] = np.asarray(inputs["w_ple_proj"], np.float32)[0]
    inv_freq = (np.float32(500000.0) ** (-np.arange(0, 16, 2, dtype=np.float32) / np.float32(16))).astype(np.float32)
    ang = idx.astype(np.float32)[:, None] * inv_freq[None, :]
    d["cos_t"] = np.ascontiguousarray(np.tile(np.cos(ang).astype(np.float32), (1, 8)))
    d["sin_t"] = np.ascontiguousarray(np.tile(np.sin(ang).astype(np.float32), (1, 8)))
    first = idx[0::128]
    last = idx[127::128]
    d["eprev"] = np.ascontiguousarray(np.broadcast_to((first != 0).astype(np.float32), (128, NT)))
    d["enext"] = np.ascontiguousarray(np.broadcast_to((last != S_LEN - 1).astype(np.float32), (128, NT)))
    nA = 16 * (3 - qi)
    j = np.arange(NCTX)
    d["keepf"] = np.ascontiguousarray(np.broadcast_to((j >= nA).astype(np.float32), (128, NCTX)))
    d["keepb"] = np.ascontiguousarray(np.broadcast_to((j < nA).astype(np.float32), (128, NCTX)))
    d["ident"] = np.eye(128, dtype=np.float32)
    r = np.arange(128)
    sl = (r[:, None] > r[None, :]).astype(np.float32)
    il = (r[:, None] >= r[None, :]).astype(np.float32)
    d["mask_sl"] = np.ascontiguousarray(np.tile(sl, (1, 4)))
    d["mask_su"] = np.ascontiguousarray(np.tile(sl.T, (1, 4)))
    d["mask_il"] = np.ascontiguousarray(np.tile(il, (1, 4)))
    d["mask_iu"] = np.ascontiguousarray(np.tile(il.T, (1, 4)))
    bo = np.zeros((128, 128), np.float32)
    bo[:64, :64] = 1
    bo[64:, 64:] = 1
    d["bones"] = bo
    hi = np.zeros((128, 2), np.float32)
    hi[:64, 0] = 1
    hi[64:, 1] = 1
    d["hind"] = hi
    return d


_NC_CACHE = {}


def kernel(**inputs):
    if "nc" not in _NC_CACHE:
        _NC_CACHE["nc"] = build_program()
    nc = _NC_CACHE["nc"]
    in_maps = [prep_core_inputs(inputs, c) for c in range(8)]
    res = run_bass_kernel_spmd(nc, in_maps, core_ids=list(range(8)))
    out = np.zeros((2, S_LEN, 1024), np.float32)
    for c in range(8):
        b, qi = c // 4, c % 4
        out[b, 2048 * qi:2048 * (qi + 1)] = res.results[c]["out"]
    return out
```

```python
import contextlib
import math
import os
import numpy as np
import concourse.bass as bass
import concourse.mybir as mybir
from concourse.bass_utils import run_bass_kernel_spmd

F32 = mybir.dt.float32
BF16 = mybir.dt.bfloat16
ALU = mybir.AluOpType
AF = mybir.ActivationFunctionType
AX = mybir.AxisListType

S_LEN = 8192
NT = 64
NCTX = 48
NOWN = 16
CDEC = 0.6065306597126334
LAMBDA_INIT = 0.8 - 0.6 * math.exp(0.0)


class _Eng:
    def __init__(self, name, eng, sem):
        self.name, self.eng, self.sem = name, eng, sem
        self.count = 0
        self.seen = {}


class T:
    def __init__(self, t, name=""):
        self.t, self.name = t, name
        self.w = None
        self.r = {}

    def __getitem__(self, idx):
        return self.t[idx]


class Sched:
    N_DMA_SEMS = 48

    def __init__(self, nc, stack):
        self.nc, self.stack = nc, stack
        self.E = {}
        for name, e in (("pe", nc.tensor), ("act", nc.scalar), ("dve", nc.vector),
                        ("pool", nc.gpsimd), ("sp", nc.sync)):
            self.E[name] = _Eng(name, e, stack.enter_context(nc.semaphore("sem_" + name)))
        self.dsems = [[stack.enter_context(nc.semaphore("dsem%d" % i)), 0] for i in range(self.N_DMA_SEMS)]
        self.dnext = 0
        self.scopes = []
        self.rr = 0

    def sbuf(self, name, shape, dt):
        st = self.scopes[-1] if self.scopes else self.stack
        return T(st.enter_context(self.nc.sbuf_tensor("s_" + name, list(shape), dt)), name)

    def psum(self, name, shape, dt=F32):
        st = self.scopes[-1] if self.scopes else self.stack
        return T(st.enter_context(self.nc.psum_tensor("ps_" + name, list(shape), dt)), name)

    def dram(self, name, shape, dt):
        return T(self.nc.dram_tensor("dr_" + name, list(shape), dt, kind="Internal").ap(), name)

    def _need(self, E, ticket, raw):
        if ticket is None:
            return
        sem, val, src = ticket
        if src is E and (E.name == "pe" or not raw):
            return
        key = id(sem)
        if E.seen.get(key, 0) >= val:
            return
        E.eng.wait_ge(sem, val)
        E.seen[key] = val

    def _deps(self, E, reads, writes):
        for t in reads:
            self._need(E, t.w, True)
        for t in writes:
            self._need(E, t.w, False)
            for tk in t.r.values():
                self._need(E, tk, False)

    def _record(self, ticket, key, reads, writes):
        for t in reads:
            t.r[key] = ticket
        for t in writes:
            t.w = ticket
            t.r = {}

    def op(self, eng, fn, reads=(), writes=()):
        E = self.E[eng]
        self._deps(E, reads, writes)
        ins = fn(E.eng)
        E.count += 1
        ins.then_inc(E.sem, 1)
        self._record((E.sem, E.count, E), eng, reads, writes)

    def dma(self, q, out, in_, reads=(), writes=(), **kw):
        E = self.E[q]
        self._deps(E, reads, writes)
        slot = self.dsems[self.dnext]
        self.dnext = (self.dnext + 1) % len(self.dsems)
        sem, tot = slot
        if tot > 0:
            self._need(E, (sem, tot, None), False)
        E.eng.dma_start(out=out, in_=in_, **kw).then_inc(sem, 16)
        slot[1] = tot + 16
        self._record((sem, tot + 16, None), ("dma", id(sem)), reads, writes)

    def barrier(self):
        for E in self.E.values():
            for sem, tot in self.dsems:
                if tot > 0:
                    self._need(E, (sem, tot, None), False)
            for o in self.E.values():
                if o is not E and o.count > 0:
                    self._need(E, (o.sem, o.count, o), False)

    def push(self):
        st = contextlib.ExitStack()
        st.__enter__()
        self.scopes.append(st)

    def pop(self):
        self.barrier()
        self.scopes.pop().__exit__(None, None, None)

    def copy(self, eng, out_t, out_ap, in_t, in_ap, scale=None):
        if eng == "rr":
            eng = ("act", "dve")[self.rr % 2]
            self.rr += 1
        if eng == "act":
            if scale is None:
                self.op("act", lambda e: e.activation(out=out_ap, in_=in_ap, func=AF.Copy),
                        reads=[in_t], writes=[out_t])
            else:
                self.op("act", lambda e: e.activation(out=out_ap, in_=in_ap, func=AF.Copy, scale=scale),
                        reads=[in_t], writes=[out_t])
        else:
            if scale is None:
                self.op(eng, lambda e: e.tensor_copy(out=out_ap, in_=in_ap), reads=[in_t], writes=[out_t])
            else:
                self.op(eng, lambda e: e.tensor_scalar(out=out_ap, in0=in_ap, scalar1=scale, scalar2=None,
                                                       op0=ALU.mult), reads=[in_t], writes=[out_t])


def rsqrt(S, out_t, out_ap, in_t, in_ap, scale, bias_t, bias_ap):
    S.op("act", lambda e: e.activation(out=out_ap, in_=in_ap, func=AF.Sqrt, bias=bias_ap, scale=scale),
         reads=[in_t, bias_t], writes=[out_t])
    S.op("dve", lambda e: e.reciprocal(out=out_ap, in_=out_ap), reads=[out_t], writes=[out_t])


class Ring:
    def __init__(self, S, name, n, shape, dt, psum=False):
        self.tiles = [(S.psum if psum else S.sbuf)("%s%d" % (name, i), shape, dt) for i in range(n)]
        self.i = 0

    def next(self):
        t = self.tiles[self.i % len(self.tiles)]
        self.i += 1
        return t


def _emit_p4(S, nc, L):
    yT, oT, idf, idb, gfin, epst = L["yT"], L["oT"], L["idf"], L["idb"], L["gfin"], L["epst"]
    xs_d, p_d, out_d = L["xs_d"], L["p_d"], L["out_d"]
    Wg_s, Wor_s, Wod_s, Wout_s, W1_s, W2_s, Wpg_s, Wpp_s = (L[k] for k in ("Wg_s", "Wor_s", "Wod_s", "Wout_s", "W1_s", "W2_s", "Wpg_s", "Wpp_s"))
    S.push()
    onesb = S.sbuf("onesb", [128, 128], BF16)
    S.op("pool", lambda e: e.memset(onesb[:], 1.0), writes=[onesb])
    wp_ring = Ring(S, "wp", 4, [128, 32, 128], BF16)
    xT = S.sbuf("xT", [128, 8, 512], F32)
    hT4 = S.sbuf("hT4", [128, 8, 512], BF16)
    sqT = S.sbuf("sqT", [128, 8, 512], BF16)
    rst = S.sbuf("rst", [128, 512], F32)
    gat = S.sbuf("gat", [128, 16, 512], BF16)
    mer = S.sbuf("mer", [128, 8, 512], BF16)
    tmpA = Ring(S, "tmpA", 2, [128, 512], F32)
    hid = S.sbuf("hid", [128, 32, 512], BF16)
    sgp = S.sbuf("sgp", [128, 8, 512], BF16)
    pTt = S.sbuf("pTt", [128, 2, 512], BF16)
    xin_ring = Ring(S, "xin", 2, [128, 1024], F32)
    pin_ring = Ring(S, "pin", 2, [128, 256], F32)
    pinb_ring = Ring(S, "pinb", 2, [128, 256], BF16)
    outst_ring = Ring(S, "outst", 2, [128, 1024], F32)
    pM = Ring(S, "pM", 4, [128, 512], F32, psum=True)
    pTr = Ring(S, "pTr", 2, [128, 4, 128], F32, psum=True)
    pTb = S.psum("pTb", [128, 8, 128], BF16)
    pSS = S.psum("pSS", [128, 512])

    def load_w(scr, oc, KC):
        wp = wp_ring.next()
        S.dma("sp", wp[:, 0:KC, :], scr.t[oc], reads=[scr], writes=[wp])
        return wp

    def rms_h(gain_unused=None):
        for kc in range(8):
            S.op("act", lambda e: e.activation(out=sqT[:, kc, :], in_=xT[:, kc, :], func=AF.Square), reads=[xT], writes=[sqT])
        for kc in range(8):
            S.op("pe", lambda e: e.matmul(pSS[:], lhsT=onesb[:], rhs=sqT[:, kc, :], start=(kc == 0), stop=(kc == 7)),
                 reads=[onesb, sqT], writes=[pSS])
        rsqrt(S, rst, rst[:], pSS, pSS[:], 1.0 / 1024, epst, epst[:, 0:1])

    def apply_h():
        for kc in range(8):
            eng = ("dve", "pool")[kc % 2]
            S.op(eng, lambda e: e.tensor_tensor(out=hT4[:, kc, :], in0=xT[:, kc, :], in1=rst[:], op=ALU.mult),
                 reads=[xT, rst], writes=[hT4])

    def mm(scr, oc, KC, rhs_t, rhs_fn):
        wp = load_w(scr, oc, KC)
        pm_ = pM.next()
        for kc in range(KC):
            S.op("pe", lambda e: e.matmul(pm_[:], lhsT=wp[:, kc, :], rhs=rhs_fn(kc), start=(kc == 0), stop=(kc == KC - 1)),
                 reads=[wp, rhs_t], writes=[pm_])
        return pm_

    for blk in range(4):
        t0 = blk * 512
        for j in range(4):
            xin = xin_ring.next()
            S.dma("sp", xin[:], xs_d[(NCTX + blk * 4 + j) * 128:(NCTX + blk * 4 + j + 1) * 128, :], writes=[xin])
            for half in range(2):
                ptr = pTr.next()
                for k4 in range(4):
                    kc = half * 4 + k4
                    S.op("pe", lambda e: e.matmul(ptr[:, k4, :], lhsT=xin[:, kc * 128:(kc + 1) * 128], rhs=idf[:], start=True, stop=True),
                         reads=[xin, idf], writes=[ptr])
                S.copy("rr", xT, xT[:, half * 4:half * 4 + 4, j * 128:(j + 1) * 128], ptr, ptr[:])
            pin = pin_ring.next()
            pinb = pinb_ring.next()
            S.dma("sp", pin[:], p_d[t0 + j * 128:t0 + (j + 1) * 128, :], writes=[pin])
            S.copy("pool", pinb, pinb[:], pin, pin[:])
            for k2 in range(2):
                S.op("pe", lambda e: e.transpose(out=pTb[:, k2, :], in_=pinb[:, k2 * 128:(k2 + 1) * 128], identity=idb[:]),
                     reads=[pinb, idb], writes=[pTb])
            S.copy("dve", pTt, pTt[:, :, j * 128:(j + 1) * 128], pTb, pTb[:, 0:2, :])
        rms_h()
        apply_h()
        for oc in range(16):
            pm_ = mm(Wg_s, oc, 8, hT4, lambda kc: hT4[:, kc, :])
            S.op("act", lambda e: e.activation(out=gat[:, oc, :], in_=pm_[:], func=AF.Sigmoid), reads=[pm_], writes=[gat])
        for oc in range(8):
            pa_ = mm(Wor_s, oc, 4, yT, lambda kc: yT[:, kc, t0:t0 + 512])
            ta = tmpA.next()
            S.op("dve", lambda e: e.tensor_tensor(out=ta[:], in0=pa_[:], in1=gat[:, oc, :], op=ALU.mult), reads=[pa_, gat], writes=[ta])
            pb_ = mm(Wod_s, oc, 4, oT, lambda kc: oT[:, kc, t0:t0 + 512])
            tb = tmpA.next()
            S.op("dve", lambda e: e.tensor_tensor(out=tb[:], in0=pb_[:], in1=gat[:, 8 + oc, :], op=ALU.mult), reads=[pb_, gat], writes=[tb])
            S.op("pool", lambda e: e.tensor_tensor(out=mer[:, oc, :], in0=ta[:], in1=tb[:], op=ALU.add), reads=[ta, tb], writes=[mer])
        for oc in range(8):
            pm_ = mm(Wout_s, oc, 8, mer, lambda kc: mer[:, kc, :])
            S.op("dve", lambda e: e.tensor_tensor(out=xT[:, oc, :], in0=xT[:, oc, :], in1=pm_[:], op=ALU.add), reads=[xT, pm_], writes=[xT])
        rms_h()
        apply_h()
        for fc in range(32):
            pm_ = mm(W1_s, fc, 8, hT4, lambda kc: hT4[:, kc, :])
            tr = tmpA.next()
            S.op("act", lambda e: e.activation(out=tr[:], in_=pm_[:], func=AF.Relu), reads=[pm_], writes=[tr])
            eng = ("pool", "dve")[fc % 2]
            S.op(eng, lambda e: e.tensor_tensor(out=hid[:, fc, :], in0=tr[:], in1=tr[:], op=ALU.mult), reads=[tr], writes=[hid])
        for oc in range(8):
            pm_ = mm(W2_s, oc, 32, hid, lambda kc: hid[:, kc, :])
            S.op("dve", lambda e: e.tensor_tensor(out=xT[:, oc, :], in0=xT[:, oc, :], in1=pm_[:], op=ALU.add), reads=[xT, pm_], writes=[xT])
        rms_h()
        apply_h()
        for oc in range(8):
            pm_ = mm(Wpg_s, oc, 8, hT4, lambda kc: hT4[:, kc, :])
            S.op("act", lambda e: e.activation(out=sgp[:, oc, :], in_=pm_[:], func=AF.Sigmoid), reads=[pm_], writes=[sgp])
        for oc in range(8):
            pm_ = mm(Wpp_s, oc, 2, pTt, lambda kc: pTt[:, kc, :])
            ta = tmpA.next()
            S.op("dve", lambda e: e.tensor_tensor(out=ta[:], in0=pm_[:], in1=sgp[:, oc, :], op=ALU.mult), reads=[pm_, sgp], writes=[ta])
            S.op("pool", lambda e: e.tensor_tensor(out=xT[:, oc, :], in0=xT[:, oc, :], in1=ta[:], op=ALU.add), reads=[xT, ta], writes=[xT])
        rms_h()
        for kc in range(8):
            S.op("dve", lambda e: e.scalar_tensor_tensor(out=xT[:, kc, :], in0=xT[:, kc, :], scalar=gfin[:, kc:kc + 1], in1=rst[:],
                                                         op0=ALU.mult, op1=ALU.mult), reads=[xT, gfin, rst], writes=[xT])
        for j in range(4):
            ost = outst_ring.next()
            for half in range(2):
                ptr = pTr.next()
                for k4 in range(4):
                    kc = half * 4 + k4
                    S.op("pe", lambda e: e.matmul(ptr[:, k4, :], lhsT=xT[:, kc, j * 128:(j + 1) * 128], rhs=idf[:], start=True, stop=True),
                         reads=[xT, idf], writes=[ptr])
                S.copy("rr", ost, ost[:, half * 512:(half + 1) * 512], ptr, ptr[:].rearrange("p a b -> p (a b)"))
            S.dma("sp", out_d[t0 + j * 128:t0 + (j + 1) * 128, :], ost[:], reads=[ost])
    S.pop()


def build_program(dbg=None):
    nc = bass.Bass("TRN2", target_bir_lowering=False)

    def inp(name, shape, dt=F32):
        return nc.dram_tensor(name, list(shape), dt, kind="ExternalInput").ap()

    xs_d = inp("xs", [S_LEN, 1024])
    p_d = inp("p_own", [2048, 256])
    w_in_d = inp("w_in", [1024, 5504])
    wk_sw_d = inp("w_unused", [1, 1])
    gmix_d = inp("gmix", [128, 8])
    gffn_d = inp("gffn", [128, 8])
    gple_d = inp("gple", [128, 8])
    gfin_d = inp("gfin", [128, 8])
    mup_d = inp("mup", [128, 15])
    mun_d = inp("mun", [128, 15])
    w0_d = inp("w0t", [128, 8])
    a0_d = inp("a0t", [128, 8])
    w2_d = inp("w2t", [128, 512])
    a2_d = inp("a2t", [128, 512])
    g2_d = inp("g2", [128, 512])
    kk_d = inp("kkt", [128, 4])
    ka_d = inp("kat", [128, 4])
    rk_d = inp("rkt", [128, 4])
    lnw_d = inp("lnw_b", [128, 512])
    lnb_d = inp("lnb_b", [128, 512])
    wo_r_d = inp("rwkv_w_o", [512, 1024]) if dbg in (None, "p0") else None
    lam_d = inp("lam_b", [128, 4, 64])
    subln_d = inp("subln_b", [128, 128])
    wo_d_d = inp("da_w_o", [512, 1024]) if dbg in (None, "p0") else None
    wout_d = inp("w_out", [1024, 1024]) if dbg in (None, "p0") else None
    wff1_d = inp("w_ff1", [1024, 4096]) if dbg in (None, "p0") else None
    wff2_d = inp("w_ff2", [4096, 1024]) if dbg in (None, "p0") else None
    wpg_d = inp("w_ple_gate", [1024, 1024]) if dbg in (None, "p0") else None
    wpp_d = inp("w_ple_proj", [256, 1024]) if dbg in (None, "p0") else None
    cos_d = inp("cos_t", [S_LEN, 64])
    sin_d = inp("sin_t", [S_LEN, 64])
    eprev_d = inp("eprev", [128, NT])
    enext_d = inp("enext", [128, NT])
    keepf_d = inp("keepf", [128, NCTX])
    keepb_d = inp("keepb", [128, NCTX])
    ident_d = inp("ident", [128, 128])
    msl_d = inp("mask_sl", [128, 512])
    msu_d = inp("mask_su", [128, 512])
    mil_d = inp("mask_il", [128, 512])
    miu_d = inp("mask_iu", [128, 512])
    bones_d = inp("bones", [128, 128])
    hind_d = inp("hind", [128, 2])
    out_d = nc.dram_tensor("out", [2048, 1024], F32, kind="ExternalOutput").ap()
    dbg_outs = {}

    with contextlib.ExitStack() as stack:
        S = Sched(nc, stack)
        UT = S.dram("UT", [15, 128, S_LEN + 2], F32)
        KT = S.dram("KT", [4, 128, S_LEN], BF16)
        VD = S.dram("VD", [NT, 128, 512], BF16)

        def wscr(name, K, N):
            return S.dram(name, [N // 128, 128, K // 128, 128], BF16)
        Wg_s = wscr("Wg_s", 1024, 2048)
        Wor_s = wscr("Wor_s", 512, 1024)
        Wod_s = wscr("Wod_s", 512, 1024)
        Wout_s = wscr("Wout_s", 1024, 1024)
        W1_s = wscr("W1_s", 1024, 4096)
        W2_s = wscr("W2_s", 4096, 1024)
        Wpg_s = wscr("Wpg_s", 1024, 1024)
        Wpp_s = wscr("Wpp_s", 256, 1024)

        def const(name, src, shape, dt=F32, q="sp"):
            t = S.sbuf(name, shape, dt)
            S.dma(q, t[:], src, writes=[t])
            return t
        idf = const("idf", ident_d[:, :], [128, 128])
        idb = S.sbuf("idb", [128, 128], BF16)
        S.copy("dve", idb, idb[:], idf, idf[:])
        gmix = const("gmix", gmix_d[:, :], [128, 8])
        gffn = const("gffn", gffn_d[:, :], [128, 8])
        gple = const("gple", gple_d[:, :], [128, 8])
        gfin = const("gfin", gfin_d[:, :], [128, 8])
        QT = S.sbuf("QT", [128, 4, 2048], BF16)
        yT = S.sbuf("yT", [128, 4, 2048], BF16)
        epst = S.sbuf("epst", [128, 4], F32)
        for i_, v_ in enumerate((1e-6, 1e-5, 64e-5, 0.0)):
            S.op("pool", lambda e: e.memset(epst[:, i_:i_ + 1], v_), writes=[epst])
        qk2max = S.sbuf("qk2max", [128, 2], F32)
        S.op("pool", lambda e: e.memset(qk2max[:], 0.0), writes=[qk2max])

        S.push()
        wst_ring = Ring(S, "p0st", 2, [128, 4096], F32)
        wbf_ring = Ring(S, "p0bf", 2, [128, 4096], BF16)

        def convert(src_ap_fn, K, N, dst, gain):
            for kc in range(K // 128):
                st = wst_ring.next()
                bf = wbf_ring.next()
                S.dma("sp", st[:, 0:N], src_ap_fn(kc), writes=[st])
                if gain is None:
                    S.op("pool", lambda e: e.tensor_copy(out=bf[:, 0:N], in_=st[:, 0:N]), reads=[st], writes=[bf])
                else:
                    S.op("pool", lambda e: e.tensor_scalar(out=bf[:, 0:N], in0=st[:, 0:N],
                                                           scalar1=gain[:, kc:kc + 1], scalar2=None, op0=ALU.mult),
                         reads=[st, gain], writes=[bf])
                for o8 in range(0, N // 128, 8):
                    S.dma("sp", dst.t[o8:o8 + 8, :, kc, :].rearrange("o p c -> p o c"),
                          bf[:, o8 * 128:(o8 + 8) * 128].rearrange("p (o c) -> p o c", c=128), reads=[bf])

        if dbg in (None, "p0"):
          convert(lambda kc: w_in_d[kc * 128:(kc + 1) * 128, 3456:5504], 1024, 2048, Wg_s, gmix)
          convert(lambda kc: wo_r_d[kc * 128:(kc + 1) * 128, :], 512, 1024, Wor_s, None)
          convert(lambda kc: wo_d_d[kc * 128:(kc + 1) * 128, :], 512, 1024, Wod_s, None)
          convert(lambda kc: wout_d[kc * 128:(kc + 1) * 128, :], 1024, 1024, Wout_s, None)
          convert(lambda kc: wff1_d[kc * 128:(kc + 1) * 128, :], 1024, 4096, W1_s, gffn)
          convert(lambda kc: wff2_d[kc * 128:(kc + 1) * 128, :], 4096, 1024, W2_s, None)
          convert(lambda kc: wpg_d[kc * 128:(kc + 1) * 128, :], 1024, 1024, Wpg_s, gple)
          convert(lambda kc: wpp_d[kc * 128:(kc + 1) * 128, :], 256, 1024, Wpp_s, None)
        S.pop()

        if dbg == "p0":
            S.barrier()
            return nc
        if dbg == "p4":
            oT = S.sbuf("oT", [128, 4, 2048], BF16)
            S.op("pool", lambda e: e.memset(yT[:], 0.0), writes=[yT])
            S.op("pool", lambda e: e.memset(oT[:], 0.0), writes=[oT])
            _emit_p4(S, nc, locals())
            S.barrier()
            return nc
        S.push()
        NW = 3456
        Wb = S.sbuf("Wb", [128, 8, NW], BF16)
        wst1 = Ring(S, "p1wst", 2, [128, NW], F32)
        for kc in range(8):
            st = wst1.next()
            S.dma("sp", st[:], w_in_d[kc * 128:(kc + 1) * 128, 0:NW], writes=[st])
            S.op("pool", lambda e: e.tensor_scalar(out=Wb[:, kc, :], in0=st[:], scalar1=gmix[:, kc:kc + 1],
                                                   scalar2=None, op0=ALU.mult), reads=[st, gmix], writes=[Wb])
        x_ring = Ring(S, "xt", 3, [128, 1024], F32)
        junk_ring = Ring(S, "junk", 2, [128, 1024], BF16)
        xn_ring = Ring(S, "xn", 2, [128, 1024], BF16)
        ss_ring = Ring(S, "ss", 4, [128, 2], F32)
        hT_ring = Ring(S, "hT", 2, [128, 8, 512], BF16)
        tp_ring = Ring(S, "tp", 2, [128, 8, 128], BF16, psum=True)
        pu_ring = Ring(S, "pu", 3, [128, 512], F32, psum=True)
        pd_ring = Ring(S, "pd", 2, [128, 512], F32, psum=True)
        ust_ring = Ring(S, "ust", 3, [128, 512], F32)
        kf_ring = Ring(S, "kf", 2, [128, 8, 64], F32)
        kb_ring = Ring(S, "kb", 2, [128, 8, 64], BF16)
        cs_ring = Ring(S, "cs", 2, [128, 2, 8, 8], F32)
        rt_ring = Ring(S, "rt", 2, [128, 4, 8, 8], F32)
        sq_ring = Ring(S, "sq", 2, [128, 512], F32)
        red_ring = Ring(S, "red", 2, [128, 10], F32)
        kts_ring = Ring(S, "kts", 2, [128, 4, 128], BF16)
        vb_ring = Ring(S, "vb", 2, [128, 512], BF16)

        def rope_and_T(pd, cs, scale, is_q, s, stat_col):
            kf = kf_ring.next()
            kb = kb_ring.next()
            S.copy("act", kf, kf[:].rearrange("p a d -> p (a d)"), pd, pd[:], scale=scale)
            sq = sq_ring.next()
            red = red_ring.next()
            S.op("pool", lambda e: e.tensor_tensor(out=sq[:], in0=kf[:].rearrange("p a d -> p (a d)"),
                                                   in1=kf[:].rearrange("p a d -> p (a d)"), op=ALU.mult),
                 reads=[kf], writes=[sq])
            S.op("dve", lambda e: e.tensor_reduce(out=red[:, 0:8], in_=sq[:].rearrange("p (a d) -> p a d", d=64),
                                                  axis=AX.X, op=ALU.add), reads=[sq], writes=[red])
            S.op("dve", lambda e: e.tensor_reduce(out=red[:, 8:9], in_=red[:, 0:8], axis=AX.X, op=ALU.max),
                 reads=[red], writes=[red])
            S.op("dve", lambda e: e.tensor_tensor(out=qk2max[:, stat_col:stat_col + 1],
                                                  in0=qk2max[:, stat_col:stat_col + 1], in1=red[:, 8:9], op=ALU.max),
                 reads=[red, qk2max], writes=[qk2max])
            S.copy("act", kb, kb[:], kf, kf[:])
            rt = rt_ring.next()
            x1 = kf[:, :, 0:8]
            x2 = kf[:, :, 8:16]
            c_ = cs[:, 0, :, :]
            s_ = cs[:, 1, :, :]
            S.op("dve", lambda e: e.tensor_tensor(out=rt[:, 0], in0=x1, in1=c_, op=ALU.mult), reads=[kf, cs], writes=[rt])
            S.op("dve", lambda e: e.tensor_tensor(out=rt[:, 1], in0=x2, in1=s_, op=ALU.mult), reads=[kf, cs], writes=[rt])
            S.op("dve", lambda e: e.tensor_tensor(out=rt[:, 2], in0=x2, in1=c_, op=ALU.mult), reads=[kf, cs], writes=[rt])
            S.op("dve", lambda e: e.tensor_tensor(out=rt[:, 3], in0=x1, in1=s_, op=ALU.mult), reads=[kf, cs], writes=[rt])
            S.op("dve", lambda e: e.tensor_tensor(out=kb[:, :, 0:8], in0=rt[:, 0], in1=rt[:, 1], op=ALU.subtract),
                 reads=[rt], writes=[kb])
            S.op("dve", lambda e: e.tensor_tensor(out=kb[:, :, 8:16], in0=rt[:, 2], in1=rt[:, 3], op=ALU.add),
                 reads=[rt], writes=[kb])
            tp = tp_ring.next()
            kb2 = kb[:].rearrange("p a d -> p (a d)")
            for hd in range(4):
                S.op("pe", lambda e: e.transpose(out=tp[:, hd, :], in_=kb2[:, hd * 128:(hd + 1) * 128], identity=idb[:]),
                     reads=[kb, idb], writes=[tp])
            if is_q:
                o = s - NCTX
                S.copy("dve", QT, QT[:, :, o * 128:(o + 1) * 128], tp, tp[:, 0:4, :])
            else:
                kts = kts_ring.next()
                S.copy("dve", kts, kts[:], tp, tp[:, 0:4, :])
                S.dma("sp", KT.t[:, :, s * 128:(s + 1) * 128].rearrange("h p t -> p h t"), kts[:],
                      reads=[kts])

        for blk in range(16):
            own = blk >= 12
            hT = hT_ring.next()
            for j in range(4):
                s = blk * 4 + j
                xt = x_ring.next()
                S.dma("sp", xt[:], xs_d[s * 128:(s + 1) * 128, :], writes=[xt])
                junk = junk_ring.next()
                ss = ss_ring.next()
                S.op("pool", lambda e: e.memset(ss[:], 0.0), writes=[ss])
                S.op("act", lambda e: e.activation(out=junk[:], in_=xt[:], func=AF.Square, accum_out=ss[:, 0:1]),
                     reads=[xt], writes=[junk, ss])
                rsqrt(S, ss, ss[:, 1:2], ss, ss[:, 0:1], 1.0 / 1024, epst, epst[:, 0:1])
                xn = xn_ring.next()
                S.op("dve", lambda e: e.tensor_scalar(out=xn[:], in0=xt[:], scalar1=ss[:, 1:2], scalar2=None,
                                                      op0=ALU.mult), reads=[xt, ss], writes=[xn])
                tp = tp_ring.next()
                for kc in range(8):
                    S.op("pe", lambda e: e.transpose(out=tp[:, kc, :], in_=xn[:, kc * 128:(kc + 1) * 128],
                                                     identity=idb[:]), reads=[xn, idb], writes=[tp])
                S.copy("act", hT, hT[:, :, j * 128:(j + 1) * 128], tp, tp[:])
                cs = cs_ring.next()
                S.dma("sp", cs[:, 0].rearrange("p a d -> p (a d)"), cos_d[s * 128:(s + 1) * 128, :], writes=[cs])
                S.dma("sp", cs[:, 1].rearrange("p a d -> p (a d)"), sin_d[s * 128:(s + 1) * 128, :], writes=[cs])
                parts = [("k", 1920 + 512), ("v", 1920 + 1024)]
                if own:
                    parts.append(("q", 1920))
                for nm, c0 in parts:
                    pd = pd_ring.next()
                    for kc in range(8):
                        S.op("pe", lambda e: e.matmul(pd[:], lhsT=hT[:, kc, j * 128:(j + 1) * 128],
                                                      rhs=Wb[:, kc, c0:c0 + 512], start=(kc == 0), stop=(kc == 7)),
                             reads=[hT, Wb], writes=[pd])
                    if nm == "v":
                        vb = vb_ring.next()
                        S.copy("act", vb, vb[:], pd, pd[:])
                        S.dma("sp", VD.t[s], vb[:], reads=[vb])
                    elif nm == "k":
                        rope_and_T(pd, cs, None, False, s, 1)
                    else:
                        rope_and_T(pd, cs, 0.125, True, s, 0)
            ccs = range(15) if (own or blk == 0 or blk == 11) else range(4, 14)
            for cc in ccs:
                pu = pu_ring.next()
                for kc in range(8):
                    S.op("pe", lambda e: e.matmul(pu[:], lhsT=Wb[:, kc, cc * 128:(cc + 1) * 128], rhs=hT[:, kc, :],
                                                  start=(kc == 0), stop=(kc == 7)), reads=[hT, Wb], writes=[pu])
                ust = ust_ring.next()
                S.copy("rr", ust, ust[:], pu, pu[:])
                S.dma("sp", UT.t[cc, :, 1 + blk * 512:1 + (blk + 1) * 512], ust[:], reads=[ust])
        S.barrier()
        wr = S.sbuf("wrapc", [128, 15, 2], F32)
        S.dma("sp", wr[:, :, 0:1], UT.t[:, :, S_LEN:S_LEN + 1].rearrange("c p o -> p c o"), reads=[UT], writes=[wr],
              allow_slow_non_contiguous=True)
        S.dma("sp", wr[:, :, 1:2], UT.t[:, :, 1:2].rearrange("c p o -> p c o"), reads=[UT], writes=[wr],
              allow_slow_non_contiguous=True)
        S.dma("sp", UT.t[:, :, 0:1].rearrange("c p o -> p c o"), wr[:, :, 0:1], reads=[wr], writes=[UT],
              allow_slow_non_contiguous=True)
        S.dma("sp", UT.t[:, :, S_LEN + 1:S_LEN + 2].rearrange("c p o -> p c o"), wr[:, :, 1:2], reads=[wr], writes=[UT],
              allow_slow_non_contiguous=True)
        S.pop()

        if dbg == "p1":
            for nm, t, shape, dt in (("d_UT", UT, [15, 128, S_LEN + 2], F32), ("d_KT", KT, [4, 128, S_LEN], BF16),
                                     ("d_VD", VD, [NT, 128, 512], BF16)):
                o = nc.dram_tensor(nm, shape, dt, kind="ExternalOutput").ap()
                S.dma("sp", o, t.t, reads=[t])
            o = nc.dram_tensor("d_QT", [128, 4, 2048], BF16, kind="ExternalOutput").ap()
            S.dma("sp", o, QT[:], reads=[QT])
            o = nc.dram_tensor("d_qk", [128, 2], F32, kind="ExternalOutput").ap()
            S.dma("sp", o, qk2max[:], reads=[qk2max])
            S.barrier()
            return nc

        S.push()
        mup = const("mup", mup_d[:, :], [128, 15])
        mun = const("mun", mun_d[:, :], [128, 15])
        c0t = S.sbuf("c0t", [128, 15], F32)
        S.op("dve", lambda e: e.tensor_tensor(out=c0t[:], in0=mup[:], in1=mun[:], op=ALU.add), reads=[mup, mun], writes=[c0t])
        S.op("dve", lambda e: e.tensor_scalar(out=c0t[:], in0=c0t[:], scalar1=-1.0, scalar2=1.0, op0=ALU.mult, op1=ALU.add),
             reads=[c0t], writes=[c0t])
        w0t = const("w0t", w0_d[:, :], [128, 8])
        a0t = const("a0t", a0_d[:, :], [128, 8])
        kkt = const("kkt", kk_d[:, :], [128, 4])
        kat = const("kat", ka_d[:, :], [128, 4])
        rkt = const("rkt", rk_d[:, :], [128, 4])
        omka = S.sbuf("omka", [128, 4], F32)
        S.op("dve", lambda e: e.tensor_scalar(out=omka[:], in0=kat[:], scalar1=-1.0, scalar2=1.0, op0=ALU.mult, op1=ALU.add),
             reads=[kat], writes=[omka])
        eprev = const("eprev", eprev_d[:, :], [128, NT])
        enext = const("enext", enext_d[:, :], [128, NT])
        keepf = const("keepf", keepf_d[:, :], [128, NCTX])
        keepb = const("keepb", keepb_d[:, :], [128, NCTX])
        bones = const("bones", bones_d[:, :], [128, 128])
        hind = const("hind", hind_d[:, :], [128, 2])
        lnw = const("lnw", lnw_d[:, :], [128, 512])
        lnb = const("lnb", lnb_d[:, :], [128, 512])
        ones_t = S.sbuf("ones_t", [128, 128], F32)
        S.op("pool", lambda e: e.memset(ones_t[:], 1.0), writes=[ones_t])

        cbf_st = S.sbuf("cbf_st", [128, 512], F32)

        def cbf(name, src, shape):
            S.dma("sp", cbf_st[:], src, writes=[cbf_st])
            b = S.sbuf(name, shape, BF16)
            S.copy("pool", b, b[:], cbf_st, cbf_st[:])
            return b
        w2b = cbf("w2b", w2_d[:, :], [128, 512])
        a2b = cbf("a2b", a2_d[:, :], [128, 512])
        g2b = cbf("g2b", g2_d[:, :], [128, 512])
        msl = const("msl", msl_d[:, :], [128, 512])
        msu = const("msu", msu_d[:, :], [128, 512])
        mil = const("mil", mil_d[:, :], [128, 512])
        miu = const("miu", miu_d[:, :], [128, 512])
        idb4 = S.sbuf("idb4", [128, 4, 128], BF16)
        for g in range(4):
            S.copy("pool", idb4, idb4[:, g, :], idb, idb[:])

        Hs = [S.sbuf("Hs%d" % d, [128, 4, 64], F32) for d in range(2)]
        for d in range(2):
            S.op("pool", lambda e: e.memset(Hs[d][:], 0.0), writes=[Hs[d]])
        yacc = S.sbuf("yacc", [128, NOWN, 512], F32)
        bon = S.sbuf("bon", [128, NOWN, 8], F32)
        S.op("pool", lambda e: e.memset(bon[:], 0.0), writes=[bon])
        vtok_own = S.sbuf("vtok_own", [128, NOWN, 512], BF16)
        gate_own = S.sbuf("gate_own", [128, NOWN, 512], BF16)

        win_ring = Ring(S, "win", 1, [128, 15, 130], F32)
        sh_ring = Ring(S, "sh", 1, [128, 15, 128], F32)
        sht_ring = Ring(S, "sht", 1, [128, 8, 128], F32)
        f4 = lambda nm, n=1: Ring(S, nm, n, [128, 4, 128], F32)
        b4 = lambda nm, n=2: Ring(S, nm, n, [128, 4, 128], BF16)
        th_ring = Ring(S, "th", 2, [128, 2, 128], BF16)
        sg_ring, a_ring, kx_ring, sq4_ring, kkr_ring, t_ring = f4("sg"), f4("a"), f4("kx"), f4("sq4"), f4("kkr"), f4("tt")
        kd_ring, b_ring, gc_ring, gcx_ring, e_ring = f4("kd"), f4("b"), f4("gc"), f4("gcx"), f4("e", 3)
        sc_ring = Ring(S, "sc", 2, [128, 8, 4], F32)
        bT_ring, kT_ring, vT_ring = b4("bT"), b4("kT"), b4("vT")
        aTz_ring = [b4("aTz0"), b4("aTz1")]
        rTz_ring = [b4("rTz0"), b4("rTz1")]
        for rg in aTz_ring + rTz_ring:
            for t_ in rg.tiles:
                S.op("pool", lambda e: e.memset(t_[:], 0.0), writes=[t_])
        btok_ring, ktok_ring, vtok_ring = b4("btok", 1), b4("ktok", 1), b4("vtok", 1)
        P_ring, PT_ring, TT_ring, PI_ring = b4("Pm", 3), b4("PTm", 3), b4("TTm", 3), b4("PIm", 2)
        Aak_ring, Arb_ring, Ark_ring = b4("Aak", 2), b4("Arb", 2), b4("Ark", 2)
        H0_ring = Ring(S, "H0", 2, [128, 4, 64], BF16)
        X1_ring = Ring(S, "X1", 2, [128, 4, 64], BF16)
        U_ring = Ring(S, "U", 2, [128, 4, 64], BF16)
        prod_ring = sq4_ring
        psX = S.psum("psX", [128, 4, 128])
        psY = S.psum("psY", [128, 4, 128])
        psZ = S.psum("psZ", [128, 4, 128])
        psA = Ring(S, "psA", 2, [128, 4, 128], F32, psum=True)
        psC = Ring(S, "psC", 2, [128, 4, 128], F32, psum=True)
        psT = S.psum("psT", [128, 8, 128], BF16)

        def step(s, d, own):
            fwd = (d == 0)
            ccs = list(range(15)) if own else list(range(4, 14))
            c_lo, ncc = ccs[0], len(ccs)
            win = win_ring.next()
            S.dma("sp", win[:, c_lo:c_lo + ncc, :], UT.t[c_lo:c_lo + ncc, :, s * 128:s * 128 + 130].rearrange("c p t -> p c t"),
                  reads=[UT], writes=[win])
            S.op("pool", lambda e: e.tensor_scalar(out=win[:, c_lo:c_lo + ncc, 0:1], in0=win[:, c_lo:c_lo + ncc, 0:1],
                                                   scalar1=eprev[:, s:s + 1], scalar2=None, op0=ALU.mult),
                 reads=[win, eprev], writes=[win])
            S.op("pool", lambda e: e.tensor_scalar(out=win[:, c_lo:c_lo + ncc, 129:130], in0=win[:, c_lo:c_lo + ncc, 129:130],
                                                   scalar1=enext[:, s:s + 1], scalar2=None, op0=ALU.mult),
                 reads=[win, enext], writes=[win])
            sh = sh_ring.next()
            csl = slice(c_lo, c_lo + ncc)
            bc = lambda t_: t_[:, csl].unsqueeze(2).broadcast_to([128, ncc, 128])
            sht = sht_ring.next()
            S.op("pool", lambda e: e.tensor_tensor(out=sh[:, csl, :], in0=win[:, csl, 1:129], in1=bc(c0t), op=ALU.mult),
                 reads=[win, c0t], writes=[sh])
            for (lo, mu_) in ((0, mup), (2, mun)):
                for h0 in range(c_lo, c_lo + ncc, 8):
                    n_ = min(8, c_lo + ncc - h0)
                    hs_ = slice(h0, h0 + n_)
                    bch = mu_[:, hs_].unsqueeze(2).broadcast_to([128, n_, 128])
                    S.op("pool", lambda e: e.tensor_tensor(out=sht[:, 0:n_, :], in0=win[:, hs_, lo:lo + 128], in1=bch, op=ALU.mult),
                         reads=[win, mu_], writes=[sht])
                    S.op("pool", lambda e: e.tensor_tensor(out=sh[:, hs_, :], in0=sh[:, hs_, :], in1=sht[:, 0:n_, :], op=ALU.add),
                         reads=[sh, sht], writes=[sh])
            STG = int(os.environ.get("MK_STAGE", "9"))
            if STG <= 1:
                return
            R_, K_, V_ = sh[:, 0:4, :], sh[:, 4:8, :], sh[:, 8:12, :]
            dp = slice(d * 64, d * 64 + 64)
            th = th_ring.next()
            S.op("act", lambda e: e.activation(out=th[dp, 0, :], in_=sh[dp, 12, :], func=AF.Tanh), reads=[sh], writes=[th])
            S.copy("dve", th, th[dp, 1, :], sh, sh[dp, 13, :])
            sg = sg_ring.next()
            a_ = a_ring.next()
            pz = psA.next()
            for q4 in range(4):
                S.op("pe", lambda e: e.matmul(pz[:, q4, :], lhsT=w2b[dp, q4 * 128:(q4 + 1) * 128], rhs=th[dp, 0, :],
                                              start=True, stop=True), reads=[w2b, th], writes=[pz])
            for q4 in range(4):
                S.op("act", lambda e: e.activation(out=sg[:, q4, :], in_=pz[:, q4, :], func=AF.Sigmoid,
                                                   bias=w0t[:, d * 4 + q4:d * 4 + q4 + 1]), reads=[pz, w0t], writes=[sg])
            pa = psA.next()
            for q4 in range(4):
                S.op("pe", lambda e: e.matmul(pa[:, q4, :], lhsT=a2b[dp, q4 * 128:(q4 + 1) * 128], rhs=th[dp, 1, :],
                                              start=True, stop=True), reads=[a2b, th], writes=[pa])
            for q4 in range(4):
                S.op("act", lambda e: e.activation(out=a_[:, q4, :], in_=pa[:, q4, :], func=AF.Sigmoid,
                                                   bias=a0t[:, d * 4 + q4:d * 4 + q4 + 1]), reads=[pa, a0t], writes=[a_])
            if STG <= 2:
                return
            kx = kx_ring.next()
            for q4 in range(4):
                S.op("pool", lambda e: e.tensor_scalar(out=kx[:, q4, :], in0=sh[:, 4 + q4, :], scalar1=kkt[:, q4:q4 + 1],
                                                       scalar2=None, op0=ALU.mult), reads=[sh, kkt], writes=[kx])
            sq4 = sq4_ring.next()
            S.op("pool", lambda e: e.tensor_tensor(out=sq4[:], in0=kx[:], in1=kx[:], op=ALU.mult), reads=[kx], writes=[sq4])
            pn = psA.next()
            for q4 in range(4):
                S.op("pe", lambda e: e.matmul(pn[:, q4, :], lhsT=bones[:], rhs=sq4[:, q4, :], start=True, stop=True),
                     reads=[bones, sq4], writes=[pn])
            kkr = kkr_ring.next()
            S.op("act", lambda e: e.activation(out=kkr[:], in_=pn[:], func=AF.Sqrt), reads=[pn], writes=[kkr])
            S.op("dve", lambda e: e.tensor_scalar(out=kkr[:], in0=kkr[:], scalar1=1e-12, scalar2=None, op0=ALU.max),
                 reads=[kkr], writes=[kkr])
            S.op("dve", lambda e: e.reciprocal(out=kkr[:], in_=kkr[:]), reads=[kkr], writes=[kkr])
            S.op("dve", lambda e: e.tensor_tensor(out=kkr[:], in0=kkr[:], in1=kx[:], op=ALU.mult), reads=[kkr, kx], writes=[kkr])
            tt = t_ring.next()
            for q4 in range(4):
                S.op("dve", lambda e: e.tensor_scalar(out=tt[:, q4, :], in0=a_[:, q4, :], scalar1=kat[:, q4:q4 + 1],
                                                      scalar2=omka[:, q4:q4 + 1], op0=ALU.mult, op1=ALU.add),
                     reads=[a_, kat, omka], writes=[tt])
            kd = kd_ring.next()
            S.op("dve", lambda e: e.tensor_tensor(out=kd[:], in0=tt[:], in1=K_, op=ALU.mult), reads=[tt, sh], writes=[kd])
            bb = b_ring.next()
            S.op("pool", lambda e: e.tensor_tensor(out=bb[:], in0=kkr[:], in1=a_[:], op=ALU.mult), reads=[kkr, a_], writes=[bb])
            if STG <= 3:
                return
            gc = gc_ring.next()
            gcx = gcx_ring.next()
            sc = sc_ring.next()
            for q4 in range(4):
                S.op("dve", lambda e: e.tensor_tensor_scan(out=gc[:, q4, :], data0=ones_t[:], data1=sg[:, q4, :], initial=0.0,
                                                           op0=ALU.mult, op1=ALU.add), reads=[ones_t, sg], writes=[gc])
            if not fwd:
                S.copy("dve", sc, sc[:, 7, :], gc, gc[:, :, 127])
                for q4 in range(4):
                    S.op("dve", lambda e: e.scalar_tensor_tensor(out=gc[:, q4, :], in0=gc[:, q4, :], scalar=-1.0,
                                                                 in1=sg[:, q4, :], op0=ALU.mult, op1=ALU.add),
                         reads=[gc, sg], writes=[gc])
                    S.op("dve", lambda e: e.tensor_scalar(out=gc[:, q4, :], in0=gc[:, q4, :], scalar1=sc[:, 7, q4:q4 + 1],
                                                          scalar2=None, op0=ALU.add), reads=[gc, sc], writes=[gc])
            S.op("pool", lambda e: e.tensor_tensor(out=gcx[:], in0=gc[:], in1=sg[:], op=ALU.subtract), reads=[gc, sg], writes=[gcx])
            mid = 63 if fwd else 64
            last = 127 if fwd else 0
            S.op("dve", lambda e: e.tensor_scalar(out=sc[:, 0, :], in0=gc[:, :, mid], scalar1=CDEC, scalar2=None, op0=ALU.mult),
                 reads=[gc], writes=[sc])
            S.op("dve", lambda e: e.tensor_scalar(out=sc[:, 1, :], in0=gc[:, :, mid], scalar1=-CDEC, scalar2=None, op0=ALU.mult),
                 reads=[gc], writes=[sc])
            S.op("dve", lambda e: e.tensor_tensor(out=sc[:, 5, :], in0=gc[:, :, last], in1=gc[:, :, mid], op=ALU.subtract),
                 reads=[gc], writes=[sc])
            S.op("act", lambda e: e.activation(out=sc[:, 2, :], in_=gc[:, :, mid], func=AF.Exp, scale=-CDEC), reads=[gc], writes=[sc])
            S.op("act", lambda e: e.activation(out=sc[:, 3, :], in_=gc[:, :, last], func=AF.Exp, scale=-CDEC), reads=[gc], writes=[sc])
            S.op("act", lambda e: e.activation(out=sc[:, 4, :], in_=sc[:, 5, :], func=AF.Exp, scale=-CDEC), reads=[sc], writes=[sc])
            if not own:
                keep = (keepf if fwd else keepb)
                for r in (3, 4):
                    S.op("dve", lambda e: e.tensor_scalar(out=sc[:, r, :], in0=sc[:, r, :], scalar1=keep[:, s:s + 1],
                                                          scalar2=None, op0=ALU.mult), reads=[sc, keep], writes=[sc])
            eP, eN, ePm = e_ring.next(), e_ring.next(), e_ring.next()
            for q4 in range(4):
                S.op("act", lambda e: e.activation(out=eN[:, q4, :], in_=gc[:, q4, :], func=AF.Exp, scale=CDEC,
                                                   bias=sc[:, 1, q4:q4 + 1]), reads=[gc, sc], writes=[eN])
                S.op("act", lambda e: e.activation(out=ePm[:, q4, :], in_=gcx[:, q4, :], func=AF.Exp, scale=-CDEC,
                                                   bias=sc[:, 0, q4:q4 + 1]), reads=[gcx, sc], writes=[ePm])
                if own:
                    S.op("act", lambda e: e.activation(out=eP[:, q4, :], in_=gc[:, q4, :], func=AF.Exp, scale=-CDEC,
                                                       bias=sc[:, 0, q4:q4 + 1]), reads=[gc, sc], writes=[eP])
            bT, kT, vT = bT_ring.next(), kT_ring.next(), vT_ring.next()
            aTz = [aTz_ring[0].next(), aTz_ring[1].next()]
            for par in range(2):
                pp_ = slice(par * 64, par * 64 + 64)
                S.op("dve", lambda e: e.scalar_tensor_tensor(out=aTz[par][pp_], in0=kkr[pp_], scalar=-1.0, in1=ePm[pp_],
                                                             op0=ALU.mult, op1=ALU.mult), reads=[kkr, ePm], writes=[aTz[par]])
            S.op("pool", lambda e: e.tensor_tensor(out=bT[:], in0=bb[:], in1=eN[:], op=ALU.mult), reads=[bb, eN], writes=[bT])
            S.op("dve", lambda e: e.tensor_tensor(out=kT[:], in0=kd[:], in1=eN[:], op=ALU.mult), reads=[kd, eN], writes=[kT])
            S.copy("pool", vT, vT[:], sh, V_)
            rTz = None
            if own:
                rTz = [rTz_ring[0].next(), rTz_ring[1].next()]
                for par in range(2):
                    pp_ = slice(par * 64, par * 64 + 64)
                    S.op("pool", lambda e: e.tensor_tensor(out=rTz[par][pp_], in0=sh[pp_, 0:4, :], in1=eP[pp_], op=ALU.mult),
                         reads=[sh, eP], writes=[rTz[par]])
                prod = prod_ring.next()
                S.op("pool", lambda e: e.tensor_tensor(out=prod[:], in0=R_, in1=kd[:], op=ALU.mult), reads=[sh, kd], writes=[prod])
                for q4 in range(4):
                    S.op("pool", lambda e: e.tensor_scalar(out=prod[:, q4, :], in0=prod[:, q4, :], scalar1=rkt[:, q4:q4 + 1],
                                                           scalar2=None, op0=ALU.mult), reads=[prod, rkt], writes=[prod])
                pb = psC.next()
                for q4 in range(4):
                    S.op("pe", lambda e: e.matmul(pb[:, 0, q4 * 2:q4 * 2 + 2], lhsT=prod[:, q4, :], rhs=hind[:], start=True, stop=True),
                         reads=[prod, hind], writes=[pb])
                o = s - NCTX
                S.op("dve", lambda e: e.tensor_tensor(out=bon[:, o, :], in0=bon[:, o, :], in1=pb[:, 0, 0:8], op=ALU.add),
                     reads=[bon, pb], writes=[bon])
            if STG <= 4:
                return
            btok, ktok, vtok = btok_ring.next(), ktok_ring.next(), vtok_ring.next()
            for src, dst in ((bT, btok), (kT, ktok), (vT, vtok)):
                for q4 in range(4):
                    S.op("pe", lambda e: e.transpose(out=psT[:, q4, :], in_=src[:, q4, :], identity=idb[:]),
                         reads=[src, idb], writes=[psT])
                S.copy("rr", dst, dst[:], psT, psT[:, 0:4, :])
            if own and fwd:
                o = s - NCTX
                S.copy("pool", vtok_own, vtok_own[:, o, :], vtok, vtok[:].rearrange("p a b -> p (a b)"))
                gsb = th_ring.next()
                S.op("act", lambda e: e.activation(out=gsb[:, 0, :], in_=sh[:, 14, :], func=AF.Sigmoid), reads=[sh], writes=[gsb])
                pg = psC.next()
                S.op("pe", lambda e: e.matmul(pg[:].rearrange("p a b -> p (a b)"), lhsT=gsb[:, 0, :], rhs=g2b[:], start=True, stop=True),
                     reads=[gsb, g2b], writes=[pg])
                S.copy("act", gate_own, gate_own[:, o, :], pg, pg[:].rearrange("p a b -> p (a b)"))
            if STG <= 5:
                return
            mL, mLT, mI = (msl, msu, miu) if fwd else (msu, msl, mil)
            H = Hs[d]
            stt = {}
            for hg in range(2):
                def hsl(hh):
                    h = hg * 4 + hh
                    return h // 2, h % 2
                pAak = psA.next()
                for hh in range(4):
                    q4, par = hsl(hh)
                    ps_ = slice(par * 64, par * 64 + 64)
                    S.op("pe", lambda e: e.matmul(psX[:, hh, :], lhsT=aTz[par][:, q4, :], rhs=bT[:, q4, :], start=True, stop=True),
                         reads=[aTz[par], bT], writes=[psX])
                    S.op("pe", lambda e: e.matmul(psY[:, hh, :], lhsT=bT[:, q4, :], rhs=aTz[par][:, q4, :], start=True, stop=True),
                         reads=[aTz[par], bT], writes=[psY])
                    S.op("pe", lambda e: e.matmul(pAak[:, hh, :], lhsT=kT[:, q4, :], rhs=aTz[par][:, q4, :], start=True, stop=True),
                         reads=[aTz[par], kT], writes=[pAak])
                Pm, PTm, TTm, Aak = P_ring.next(), PT_ring.next(), TT_ring.next(), Aak_ring.next()
                m2 = lambda m: m[:].rearrange("p (a b) -> p a b", b=128)
                S.op("dve", lambda e: e.tensor_tensor(out=Pm[:], in0=psX[:], in1=m2(mL), op=ALU.mult), reads=[psX, mL], writes=[Pm])
                S.op("dve", lambda e: e.tensor_tensor(out=PTm[:], in0=psY[:], in1=m2(mLT), op=ALU.mult), reads=[psY, mLT], writes=[PTm])
                S.op("dve", lambda e: e.tensor_tensor(out=Aak[:], in0=pAak[:], in1=m2(mLT), op=ALU.mult), reads=[pAak, mLT], writes=[Aak])
                Arb = Ark = None
                if own:
                    pArb, pArk = psA.next(), psC.next()
                    for hh in range(4):
                        q4, par = hsl(hh)
                        ps_ = slice(par * 64, par * 64 + 64)
                        S.op("pe", lambda e: e.matmul(pArb[:, hh, :], lhsT=bT[:, q4, :], rhs=rTz[par][:, q4, :], start=True, stop=True),
                             reads=[rTz[par], bT], writes=[pArb])
                        S.op("pe", lambda e: e.matmul(pArk[:, hh, :], lhsT=kT[:, q4, :], rhs=rTz[par][:, q4, :], start=True, stop=True),
                             reads=[rTz[par], kT], writes=[pArk])
                    Arb, Ark = Arb_ring.next(), Ark_ring.next()
                    S.op("dve", lambda e: e.tensor_tensor(out=Arb[:], in0=pArb[:], in1=m2(mI), op=ALU.mult), reads=[pArb, mI], writes=[Arb])
                    S.op("dve", lambda e: e.tensor_tensor(out=Ark[:], in0=pArk[:], in1=m2(mI), op=ALU.mult), reads=[pArk, mI], writes=[Ark])
                S.op("pool", lambda e: e.tensor_tensor(out=TTm[:], in0=PTm[:], in1=idb4[:], op=ALU.add), reads=[PTm, idb4], writes=[TTm])
                stt[hg] = dict(Pm=Pm, PTm=PTm, TTm=TTm, Aak=Aak, Arb=Arb, Ark=Ark)
            XYZ = [(psX, psY, psZ), (psA.tiles[0], psA.tiles[1], psC.tiles[0])]
            for lev in range(1, 7):
                for hg in range(2):
                    X_, Y_, Z_ = XYZ[hg]
                    st_ = stt[hg]
                    Pm, PTm = st_["Pm"], st_["PTm"]
                    for hh in range(4):
                        S.op("pe", lambda e: e.matmul(X_[:, hh, :], lhsT=PTm[:, hh, :], rhs=Pm[:, hh, :], start=True, stop=True),
                             reads=[Pm, PTm], writes=[X_])
                    if lev < 6:
                        for hh in range(4):
                            S.op("pe", lambda e: e.matmul(Y_[:, hh, :], lhsT=Pm[:, hh, :], rhs=PTm[:, hh, :], start=True, stop=True),
                                 reads=[Pm, PTm], writes=[Y_])
                for hg in range(2):
                    X_, Y_, Z_ = XYZ[hg]
                    st_ = stt[hg]
                    Pn = P_ring.next()
                    S.copy("act", Pn, Pn[:], X_, X_[:])
                    st_["Pn"] = Pn
                    if lev < 6:
                        PTn = PT_ring.next()
                        S.copy("dve", PTn, PTn[:], Y_, Y_[:])
                        st_["PTn"] = PTn
                for hg in range(2):
                    X_, Y_, Z_ = XYZ[hg]
                    st_ = stt[hg]
                    Pn, TTm = st_["Pn"], st_["TTm"]
                    for hh in range(4):
                        S.op("pe", lambda e: e.matmul(Z_[:, hh, :], lhsT=Pn[:, hh, :], rhs=TTm[:, hh, :], start=True, stop=True),
                             reads=[Pn, TTm], writes=[Z_])
                for hg in range(2):
                    X_, Y_, Z_ = XYZ[hg]
                    st_ = stt[hg]
                    dT = PI_ring.next()
                    S.copy("rr", dT, dT[:], Z_, Z_[:])
                    TTn = TT_ring.next()
                    TTm = st_["TTm"]
                    S.op("pool", lambda e: e.tensor_tensor(out=TTn[:], in0=dT[:], in1=TTm[:], op=ALU.add), reads=[dT, TTm], writes=[TTn])
                    st_["Pm"], st_["TTm"] = st_["Pn"], TTn
                    if lev < 6:
                        st_["PTm"] = st_["PTn"]
            for hg in range(2):
                def hsl(hh):
                    h = hg * 4 + hh
                    return h // 2, h % 2
                TTm, Aak, Arb, Ark = stt[hg]["TTm"], stt[hg]["Aak"], stt[hg]["Arb"], stt[hg]["Ark"]
                if STG <= 6:
                    continue
                H0 = H0_ring.next()
                for qq in range(2):
                    q4 = hg * 2 + qq
                    S.op("dve", lambda e: e.tensor_scalar(out=H0[:, q4, :], in0=H[:, q4, :], scalar1=sc[:, 2, q4:q4 + 1],
                                                          scalar2=None, op0=ALU.mult), reads=[H, sc], writes=[H0])
                pX1 = psC.next()
                for hh in range(4):
                    q4, par = hsl(hh)
                    ps_ = slice(par * 64, par * 64 + 64)
                    S.op("pe", lambda e: e.matmul(pX1[:, hh, 0:64], lhsT=aTz[par][:, q4, :], rhs=H0[:, q4, :], start=True, stop=False),
                         reads=[aTz[par], H0], writes=[pX1])
                    S.op("pe", lambda e: e.matmul(pX1[:, hh, 0:64], lhsT=Aak[:, hh, :], rhs=vtok[:, q4, ps_], start=False, stop=True),
                         reads=[Aak, vtok], writes=[pX1])
                X1 = X1_ring.next()
                S.copy("act", X1, X1[:], pX1, pX1[:, :, 0:64])
                pU = psC.next()
                for hh in range(4):
                    S.op("pe", lambda e: e.matmul(pU[:, hh, 0:64], lhsT=TTm[:, hh, :], rhs=X1[:, hh, :], start=True, stop=True),
                         reads=[TTm, X1], writes=[pU])
                U = U_ring.next()
                S.copy("dve", U, U[:], pU, pU[:, :, 0:64])
                if own:
                    pY = psA.next()
                    for hh in range(4):
                        q4, par = hsl(hh)
                        ps_ = slice(par * 64, par * 64 + 64)
                        S.op("pe", lambda e: e.matmul(pY[:, hh, 0:64], lhsT=rTz[par][:, q4, :], rhs=H0[:, q4, :], start=True, stop=False),
                             reads=[rTz[par], H0], writes=[pY])
                        S.op("pe", lambda e: e.matmul(pY[:, hh, 0:64], lhsT=Arb[:, hh, :], rhs=U[:, hh, :], start=False, stop=False),
                             reads=[Arb, U], writes=[pY])
                        S.op("pe", lambda e: e.matmul(pY[:, hh, 0:64], lhsT=Ark[:, hh, :], rhs=vtok[:, q4, ps_], start=False, stop=True),
                             reads=[Ark, vtok], writes=[pY])
                    o = s - NCTX
                    ysl = yacc[:, o, hg * 256:(hg + 1) * 256].rearrange("p (a b) -> p a b", b=64)
                    if fwd:
                        S.copy("act", yacc, ysl, pY, pY[:, :, 0:64])
                    else:
                        S.op("dve", lambda e: e.tensor_tensor(out=ysl, in0=ysl, in1=pY[:, :, 0:64], op=ALU.add),
                             reads=[yacc, pY], writes=[yacc])
                pH = psC.next()
                for qq in range(2):
                    q4 = hg * 2 + qq
                    S.op("pe", lambda e: e.matmul(pH[:, qq, :], lhsT=btok[:, q4, :], rhs=U[:, qq * 2:qq * 2 + 2, :].rearrange("p a b -> p (a b)"),
                                                  start=True, stop=False), reads=[btok, U], writes=[pH])
                    S.op("pe", lambda e: e.matmul(pH[:, qq, :], lhsT=ktok[:, q4, :], rhs=vtok[:, q4, :], start=False, stop=True),
                         reads=[ktok, vtok], writes=[pH])
                for qq in range(2):
                    q4 = hg * 2 + qq
                    for par in range(2):
                        pp = slice(par * 64, par * 64 + 64)
                        S.op("dve", lambda e: e.tensor_scalar(out=H[pp, q4, :], in0=H[pp, q4, :], scalar1=sc[pp, 3, q4:q4 + 1],
                                                              scalar2=None, op0=ALU.mult), reads=[H, sc], writes=[H])
                        S.op("dve", lambda e: e.scalar_tensor_tensor(out=H[pp, q4, :], in0=pH[pp, qq, par * 64:par * 64 + 64],
                                                                     scalar=sc[pp, 4, q4:q4 + 1], in1=H[pp, q4, :],
                                                                     op0=ALU.mult, op1=ALU.add), reads=[pH, sc, H], writes=[H])

        lim = int(os.environ.get("MK_P2LIM", "0"))
        ctx_f = list(range(NCTX))
        ctx_b = list(range(NCTX - 1, -1, -1))
        own_f = list(range(NCTX, NT))
        own_b = list(range(NT - 1, NCTX - 1, -1))
        if lim:
            ctx_f, ctx_b = ctx_f[-lim:], ctx_b[:0]
        olim = int(os.environ.get("MK_OWNLIM", "0"))
        if olim:
            own_f, own_b = own_f[:olim], own_b[:olim]
        if dbg == "p13":
            ctx_f, ctx_b, own_f, own_b = [], [], [], []
        for s in ctx_f:
            step(s, 0, False)
        for s in own_f:
            step(s, 0, True)
        for s in ctx_b:
            step(s, 1, False)
        for s in own_b:
            step(s, 1, True)

        st_ring = Ring(S, "gnst", 2, [128, 8, 4], F32)
        yn_ring = Ring(S, "yn", 1, [128, 8, 64], F32)
        yb_ring = Ring(S, "yb", 2, [128, 512], BF16)
        for o in range(NOWN):
            y3 = yacc[:, o, :].rearrange("p (h j) -> p h j", j=64)
            st = st_ring.next()
            yn = yn_ring.next()
            S.op("dve", lambda e: e.tensor_reduce(out=st[:, :, 0], in_=y3, axis=AX.X, op=ALU.add), reads=[yacc], writes=[st])
            S.op("dve", lambda e: e.tensor_scalar(out=st[:, :, 0], in0=st[:, :, 0], scalar1=1.0 / 64, scalar2=None, op0=ALU.mult),
                 reads=[st], writes=[st])
            for h in range(8):
                S.op("dve", lambda e: e.tensor_scalar(out=yn[:, h, :], in0=y3[:, h, :], scalar1=st[:, h, 0:1], scalar2=None,
                                                      op0=ALU.subtract), reads=[yacc, st], writes=[yn])
            sqt = sq4_ring.next()
            sq3 = sqt[:].rearrange("p a b -> p (a b)").rearrange("p (h j) -> p h j", j=64)
            S.op("pool", lambda e: e.tensor_tensor(out=sq3, in0=yn[:], in1=yn[:], op=ALU.mult), reads=[yn], writes=[sqt])
            S.op("dve", lambda e: e.tensor_reduce(out=st[:, :, 1], in_=sq3, axis=AX.X, op=ALU.add), reads=[sqt], writes=[st])
            rsqrt(S, st, st[:, :, 1], st, st[:, :, 1], 1.0 / 64, epst, epst[:, 2:3])
            for h in range(8):
                S.op("dve", lambda e: e.tensor_scalar(out=yn[:, h, :], in0=yn[:, h, :], scalar1=st[:, h, 1:2], scalar2=None,
                                                      op0=ALU.mult), reads=[yn, st], writes=[yn])
            ynf = yn[:].rearrange("p h j -> p (h j)")
            S.op("pool", lambda e: e.tensor_tensor(out=ynf, in0=ynf, in1=lnw[:], op=ALU.mult), reads=[yn, lnw], writes=[yn])
            S.op("pool", lambda e: e.tensor_tensor(out=ynf, in0=ynf, in1=lnb[:], op=ALU.add), reads=[yn, lnb], writes=[yn])
            for h in range(8):
                S.op("dve", lambda e: e.scalar_tensor_tensor(out=yn[:, h, :], in0=vtok_own[:, o, h * 64:(h + 1) * 64],
                                                             scalar=bon[:, o, h:h + 1], in1=yn[:, h, :], op0=ALU.mult, op1=ALU.add),
                     reads=[vtok_own, bon, yn], writes=[yn])
            yb = yb_ring.next()
            S.op("dve", lambda e: e.tensor_tensor(out=yb[:], in0=ynf, in1=gate_own[:, o, :], op=ALU.mult),
                 reads=[yn, gate_own], writes=[yb])
            for q4 in range(4):
                S.op("pe", lambda e: e.transpose(out=psT[:, q4, :], in_=yb[:, q4 * 128:(q4 + 1) * 128], identity=idb[:]),
                     reads=[yb, idb], writes=[psT])
            S.copy("act", yT, yT[:, :, o * 128:(o + 1) * 128], psT, psT[:, 0:4, :])
        if dbg in ("p2", "p23"):
            o_ = nc.dram_tensor("d_yacc", [128, NOWN, 512], F32, kind="ExternalOutput").ap()
            S.dma("sp", o_, yacc[:], reads=[yacc])
            o_ = nc.dram_tensor("d_yT", [128, 4, 2048], BF16, kind="ExternalOutput").ap()
            S.dma("sp", o_, yT[:], reads=[yT])
            o_ = nc.dram_tensor("d_H", [2, 128, 4, 64], F32, kind="ExternalOutput").ap()
            for d in range(2):
                S.dma("sp", o_[d], Hs[d][:], reads=[Hs[d]])
        if dbg == "p2":
            S.barrier()
            S.pop()
            return nc
        S.pop()

        oT = S.sbuf("oT", [128, 4, 2048], BF16)
        S.push()
        pm = S.psum("pm", [128, 512])
        mrow = S.sbuf("mrow", [1, 4], F32)
        negM = S.sbuf("negM", [128, 1], F32)
        ones1 = S.sbuf("ones1", [1, 128], F32)
        S.op("pool", lambda e: e.memset(ones1[:], 1.0), writes=[ones1])
        for c in range(2):
            S.op("pe", lambda e: e.matmul(pm[0:1, 0:128], lhsT=qk2max[:, c:c + 1], rhs=idf[:], start=True, stop=True),
                 reads=[qk2max, idf], writes=[pm])
            S.op("dve", lambda e: e.tensor_reduce(out=mrow[:, c:c + 1], in_=pm[0:1, 0:128], axis=AX.X, op=ALU.max), reads=[pm], writes=[mrow])
        S.op("dve", lambda e: e.tensor_scalar(out=mrow[:, 2:3], in0=mrow[:, 0:1], scalar1=-4.0, scalar2=None, op0=ALU.mult),
             reads=[mrow], writes=[mrow])
        S.op("dve", lambda e: e.scalar_tensor_tensor(out=mrow[:, 3:4], in0=mrow[:, 1:2], scalar=-1.0 / 16, in1=mrow[:, 2:3],
                                                     op0=ALU.mult, op1=ALU.add), reads=[mrow], writes=[mrow])
        S.op("pe", lambda e: e.matmul(pm[:, 0:1], lhsT=ones1[:], rhs=mrow[:, 3:4], start=True, stop=True), reads=[ones1, mrow], writes=[pm])
        S.copy("dve", negM, negM[:], pm, pm[:, 0:1])
        lamt = const("lamt", lam_d[:, :, :], [128, 4, 64])
        lam = S.sbuf("lam", [128, 4], F32)
        S.op("dve", lambda e: e.tensor_tensor(out=lamt[:, 0, :], in0=lamt[:, 0, :], in1=lamt[:, 1, :], op=ALU.mult), reads=[lamt], writes=[lamt])
        S.op("dve", lambda e: e.tensor_tensor(out=lamt[:, 2, :], in0=lamt[:, 2, :], in1=lamt[:, 3, :], op=ALU.mult), reads=[lamt], writes=[lamt])
        S.op("dve", lambda e: e.tensor_reduce(out=lam[:, 0:1], in_=lamt[:, 0, :], axis=AX.X, op=ALU.add), reads=[lamt], writes=[lam])
        S.op("dve", lambda e: e.tensor_reduce(out=lam[:, 1:2], in_=lamt[:, 2, :], axis=AX.X, op=ALU.add), reads=[lamt], writes=[lam])
        S.op("act", lambda e: e.activation(out=lam[:, 0:2], in_=lam[:, 0:2], func=AF.Exp), reads=[lam], writes=[lam])
        S.op("dve", lambda e: e.tensor_tensor(out=lam[:, 2:3], in0=lam[:, 1:2], in1=lam[:, 0:1], op=ALU.subtract), reads=[lam], writes=[lam])
        S.op("dve", lambda e: e.tensor_scalar(out=lam[:, 2:3], in0=lam[:, 2:3], scalar1=-LAMBDA_INIT, scalar2=None, op0=ALU.add),
             reads=[lam], writes=[lam])
        subln = const("subln", subln_d[:, :], [128, 128])
        S.op("dve", lambda e: e.tensor_scalar(out=subln[:], in0=subln[:], scalar1=1.0 - LAMBDA_INIT, scalar2=None, op0=ALU.mult),
             reads=[subln], writes=[subln])
        KTh_ring = Ring(S, "KTh", 2, [128, S_LEN], BF16)
        Vh_ring = Ring(S, "Vh", 2, [128, NT, 130], BF16)
        PT_ring2 = Ring(S, "PTa", 3, [128, 2, 512], BF16)
        psS = Ring(S, "psS", 2, [128, 2, 512], F32, psum=True)
        accA = S.psum("accA", [128, 512])
        accB = S.psum("accB", [128, 512])
        accC = S.psum("accC", [128, 512])
        acc_slots = [(accA, 0), (accA, 1), (accA, 2), (accB, 0), (accB, 1), (accB, 2), (accC, 0), (accC, 1)]
        o1_ring = Ring(S, "o1", 2, [128, 130], F32)
        o2_ring = Ring(S, "o2", 2, [128, 130], F32)
        ob_ring = Ring(S, "ob", 2, [128, 128], BF16)
        for hd in range(4):
            KTh = KTh_ring.next()
            Vh = Vh_ring.next()
            S.dma("sp", KTh[:], KT.t[hd], reads=[KT], writes=[KTh])
            for g8 in range(8):
                S.dma("sp", Vh[:, g8 * 8:(g8 + 1) * 8, 0:128],
                      VD.t[g8 * 8:(g8 + 1) * 8, :, hd * 128:(hd + 1) * 128].rearrange("s p c -> p s c"), reads=[VD], writes=[Vh])
            S.op("pool", lambda e: e.memset(Vh[:, :, 128:130], 1.0), writes=[Vh])
            for qs in range(4):
                for kc in range(NT):
                    ps = psS.next()
                    for br in range(2):
                        bp = slice(br * 64, br * 64 + 64)
                        S.op("pe", lambda e: e.matmul(ps[:, br, :], lhsT=KTh[bp, kc * 128:(kc + 1) * 128],
                                                      rhs=QT[bp, hd, qs * 512:(qs + 1) * 512], start=True, stop=True),
                             reads=[KTh, QT], writes=[ps])
                    PT = PT_ring2.next()
                    S.op("act", lambda e: e.activation(out=PT[:], in_=ps[:], func=AF.Exp, bias=negM[:, 0:1]),
                         reads=[ps, negM], writes=[PT])
                    for br in range(2):
                        for qb in range(4):
                            at, ai = acc_slots[br * 4 + qb]
                            S.op("pe", lambda e: e.matmul(at[:, ai * 130:ai * 130 + 129], lhsT=PT[:, br, qb * 128:(qb + 1) * 128],
                                                          rhs=Vh[:, kc, 0:129], start=(kc == 0 and ai == 0), stop=(kc == NT - 1)),
                                 reads=[PT, Vh], writes=[at])
                for qb in range(4):
                    o1, o2 = o1_ring.next(), o2_ring.next()
                    a1, i1 = acc_slots[qb]
                    a2_, i2 = acc_slots[4 + qb]
                    S.copy("act", o1, o1[:, 0:129], a1, a1[:, i1 * 130:i1 * 130 + 129])
                    S.copy("dve", o2, o2[:, 0:129], a2_, a2_[:, i2 * 130:i2 * 130 + 129])
                    S.op("dve", lambda e: e.reciprocal(out=o1[:, 129:130], in_=o1[:, 128:129]), reads=[o1], writes=[o1])
                    S.op("dve", lambda e: e.reciprocal(out=o2[:, 129:130], in_=o2[:, 128:129]), reads=[o2], writes=[o2])
                    S.op("dve", lambda e: e.tensor_tensor(out=o2[:, 129:130], in0=o2[:, 129:130], in1=lam[:, 2:3], op=ALU.mult),
                         reads=[o2, lam], writes=[o2])
                    S.op("dve", lambda e: e.tensor_scalar(out=o1[:, 0:128], in0=o1[:, 0:128], scalar1=o1[:, 129:130], scalar2=None,
                                                          op0=ALU.mult), reads=[o1], writes=[o1])
                    S.op("dve", lambda e: e.scalar_tensor_tensor(out=o1[:, 0:128], in0=o2[:, 0:128], scalar=o2[:, 129:130],
                                                                 in1=o1[:, 0:128], op0=ALU.mult, op1=ALU.add),
                         reads=[o1, o2], writes=[o1])
                    S.op("pool", lambda e: e.memset(o2[:, 128:130], 0.0), writes=[o2])
                    S.op("act", lambda e: e.activation(out=o2[:, 0:128], in_=o1[:, 0:128], func=AF.Square, accum_out=o2[:, 128:129]),
                         reads=[o1], writes=[o2])
                    rsqrt(S, o2, o2[:, 128:129], o2, o2[:, 128:129], 1.0 / 128, epst, epst[:, 1:2])
                    ob = ob_ring.next()
                    S.op("dve", lambda e: e.scalar_tensor_tensor(out=ob[:], in0=o1[:, 0:128], scalar=o2[:, 128:129], in1=subln[:],
                                                                 op0=ALU.mult, op1=ALU.mult), reads=[o1, o2, subln], writes=[ob])
                    ptp = psS.next()
                    ptb = ptp[:].rearrange("p a b -> p (a b)")
                    S.op("pe", lambda e: e.matmul(ptb[:, 0:128], lhsT=ob[:], rhs=idb[:], start=True, stop=True),
                         reads=[ob, idb], writes=[ptp])
                    tok0 = qs * 512 + qb * 128
                    S.copy("act", oT, oT[:, hd, tok0:tok0 + 128], ptp, ptb[:, 0:128])
        if dbg in ("p3", "p23", "p13"):
            o_ = nc.dram_tensor("d_oT", [128, 4, 2048], BF16, kind="ExternalOutput").ap()
            S.dma("sp", o_, oT[:], reads=[oT])
            S.barrier()
            S.pop()
            return nc
        S.pop()

        _emit_p4(S, nc, locals())
        S.barrier()
    return nc


def _tab(v, ncol):
    return np.ascontiguousarray(np.asarray(v, np.float32).reshape(ncol, 128).T)


def prep_core_inputs(inputs, c):
    b, qi = c // 4, c % 4
    x = np.asarray(inputs["x"], np.float32)
    own0 = 2048 * qi
    idx = np.concatenate([np.arange(own0 + 2048, S_LEN), np.arange(0, own0), np.arange(own0, own0 + 2048)])
    d = {}
    d["xs"] = np.ascontiguousarray(x[b, idx])
    d["p_own"] = np.ascontiguousarray(np.asarray(inputs["p"], np.float32)[0, b, own0:own0 + 2048])
    d["w_in"] = np.asarray(inputs["w_in"], np.float32)[0]
    d["w_unused"] = np.zeros((1, 1), np.float32)
    d["gmix"] = _tab(inputs["norm_mix"][0], 8)
    d["gffn"] = _tab(inputs["norm_ffn"][0], 8)
    d["gple"] = _tab(inputs["norm_ple"][0], 8)
    d["gfin"] = _tab(inputs["norm_final"], 8)
    d["mup"] = _tab(inputs["shift_mu_prev"][0], 15)
    d["mun"] = _tab(inputs["shift_mu_next"][0], 15)
    d["w0t"] = _tab(np.asarray(inputs["rwkv_w0"], np.float32)[0].reshape(-1), 8)
    d["a0t"] = _tab(np.asarray(inputs["rwkv_a0"], np.float32)[0].reshape(-1), 8)
    d["w2t"] = np.ascontiguousarray(np.asarray(inputs["rwkv_w2"], np.float32)[0].reshape(128, 512))
    d["a2t"] = np.ascontiguousarray(np.asarray(inputs["rwkv_a2"], np.float32)[0].reshape(128, 512))
    d["g2"] = np.asarray(inputs["rwkv_g2"], np.float32)[0]
    d["kkt"] = _tab(inputs["rwkv_k_k"][0], 4)
    d["kat"] = _tab(inputs["rwkv_k_a"][0], 4)
    d["rkt"] = _tab(np.asarray(inputs["rwkv_r_k"], np.float32)[0].reshape(-1), 4)
    d["lnw_b"] = np.ascontiguousarray(np.broadcast_to(np.asarray(inputs["rwkv_ln_w"], np.float32)[0], (128, 512)))
    d["lnb_b"] = np.ascontiguousarray(np.broadcast_to(np.asarray(inputs["rwkv_ln_b"], np.float32)[0], (128, 512)))
    d["rwkv_w_o"] = np.asarray(inputs["rwkv_w_o"], np.float32)[0]
    lam = np.stack([np.asarray(inputs[k], np.float32)[0] for k in ("da_lq1", "da_lk1", "da_lq2", "da_lk2")])
    d["lam_b"] = np.ascontiguousarray(np.broadcast_to(lam, (128, 4, 64)))
    d["subln_b"] = np.ascontiguousarray(np.broadcast_to(np.asarray(inputs["da_subln_w"], np.float32)[0], (128, 128)))
    d["da_w_o"] = np.asarray(inputs["da_w_o"], np.float32)[0]
    d["w_out"] = np.asarray(inputs["w_out"], np.float32)[0]
    d["w_ff1"] = np.asarray(inputs["w_ff1"], np.float32)[0]
    d["w_ff2"] = np.asarray(inputs["w_ff2"], np.float32)[0]
    d["w_ple_gate"] = np.asarray(inputs["w_ple_gate"], np.float32)[0]
    d["w_ple_proj"] = np.asarray(inputs["w_ple_proj"], np.float32)[0]
    inv_freq = (np.float32(500000.0) ** (-np.arange(0, 16, 2, dtype=np.float32) / np.float32(16))).astype(np.float32)
    ang = idx.astype(np.float32)[:, None] * inv_freq[None, :]
    d["cos_t"] = np.ascontiguousarray(np.tile(np.cos(ang).astype(np.float32), (1, 8)))
    d["sin_t"] = np.ascontiguousarray(np.tile(np.sin(ang).astype(np.float32), (1, 8)))
    first = idx[0::128]
    last = idx[127::128]
    d["eprev"] = np.ascontiguousarray(np.broadcast_to((first != 0).astype(np.float32), (128, NT)))
    d["enext"] = np.ascontiguousarray(np.broadcast_to((last != S_LEN - 1).astype(np.float32), (128, NT)))
    nA = 16 * (3 - qi)
    j = np.arange(NCTX)
    d["keepf"] = np.ascontiguousarray(np.broadcast_to((j >= nA).astype(np.float32), (128, NCTX)))
    d["keepb"] = np.ascontiguousarray(np.broadcast_to((j < nA).astype(np.float32), (128, NCTX)))
    d["ident"] = np.eye(128, dtype=np.float32)
    r = np.arange(128)
    sl = (r[:, None] > r[None, :]).astype(np.float32)
    il = (r[:, None] >= r[None, :]).astype(np.float32)
    d["mask_sl"] = np.ascontiguousarray(np.tile(sl, (1, 4)))
    d["mask_su"] = np.ascontiguousarray(np.tile(sl.T, (1, 4)))
    d["mask_il"] = np.ascontiguousarray(np.tile(il, (1, 4)))
    d["mask_iu"] = np.ascontiguousarray(np.tile(il.T, (1, 4)))
    bo = np.zeros((128, 128), np.float32)
    bo[:64, :64] = 1
    bo[64:, 64:] = 1
    d["bones"] = bo
    hi = np.zeros((128, 2), np.float32)
    hi[:64, 0] = 1
    hi[64:, 1] = 1
    d["hind"] = hi
    return d


_NC_CACHE = {}


def kernel(**inputs):
    if "nc" not in _NC_CACHE:
        _NC_CACHE["nc"] = build_program()
    nc = _NC_CACHE["nc"]
    in_maps = [prep_core_inputs(inputs, c) for c in range(8)]
    res = run_bass_kernel_spmd(nc, in_maps, core_ids=list(range(8)))
    out = np.zeros((2, S_LEN, 1024), np.float32)
    for c in range(8):
        b, qi = c // 4, c % 4
        out[b, 2048 * qi:2048 * (qi + 1)] = res.results[c]["out"]
    return out
```
